# Optimizing a Trainium2 kernel written in Bass

```python
import jax, jax.numpy as jnp
from jax import lax
import numpy as np

D_MODEL = 1024
BATCH = 16
SEQ = 256
DEPTH = 4
DEC_BATCH = 4
DEC_SEQ = 4096
PAST_LEN = 256

GRID_W = 64
N_EVEN = (DEPTH + 1) // 2
N_ODD = DEPTH // 2
EPS = 1e-6

HEAD_DIM = 64
A_Q_HEADS = 8
A_KV_HEADS = 2
A_GROUPS = A_Q_HEADS // A_KV_HEADS
WINDOW = 128
BLOCK = 128
ROPE_BASE = 10000.0
NEG_INF = -1e30

B_HEADS = 4
B_DK = 64
B_DV = 128
GLA_RANK = 16
GLA_NORMALIZER = 16.0
GLA_CHUNK = 64

A_WIDTH = A_Q_HEADS * HEAD_DIM
B_WIDTH = B_HEADS * B_DV
MIX_WIDTH = A_WIDTH + B_WIDTH
SIZES_EVEN = (A_Q_HEADS * HEAD_DIM, A_KV_HEADS * HEAD_DIM, A_KV_HEADS * HEAD_DIM,
              B_HEADS * B_DK, B_HEADS * B_DK, B_HEADS * B_DV, B_HEADS * B_DV, 2 * GLA_RANK)
P_EVEN = sum(SIZES_EVEN)

D_RNN = D_MODEL
LRU_BLOCK_W = 256
LRU_BLOCKS = D_RNN // LRU_BLOCK_W
CONV_W = 4
LRU_C = 8.0

D_FF = ((8 * D_MODEL + 3 * 256 - 1) // (3 * 256)) * 256

kernel_name = 'hybrid_diffusion_prefix_step'


def rmsnorm(x, g):
    xf = x.astype(jnp.float32)
    y = xf * lax.rsqrt(jnp.mean(xf * xf, axis=-1, keepdims=True) + EPS)
    return (y * g.astype(jnp.float32)).astype(x.dtype)


def modulate(x, g, shift, scale):
    return rmsnorm(x, g) * (1 + scale) + shift


def ada(cvec, w, b):
    return jnp.split(jax.nn.silu(cvec) @ w + b, 6, axis=-1)


def split_cols(z, sizes):
    return jnp.split(z, list(np.cumsum(sizes)[:-1]), axis=-1)


def rope_2d(x):
    T = x.shape[1]
    n_rows = T // GRID_W
    rows = jnp.repeat(jnp.arange(n_rows, dtype=jnp.float32), GRID_W)
    cols = jnp.tile(jnp.arange(GRID_W, dtype=jnp.float32), n_rows)
    nf = HEAD_DIM // 4
    inv = ROPE_BASE ** (-jnp.arange(nf, dtype=jnp.float32) / nf)
    ar = rows[:, None] * inv
    ac = cols[:, None] * inv
    ang = jnp.concatenate([ar, ar, ac, ac], axis=-1)
    shp = (T,) + (1,) * (x.ndim - 3) + (HEAD_DIM,)
    cos = jnp.cos(ang).reshape(shp)
    sin = jnp.sin(ang).reshape(shp)

    def rot_half(z):
        z1, z2 = jnp.split(z, 2, axis=-1)
        return jnp.concatenate([-z2, z1], axis=-1)

    xr, xc = jnp.split(x, 2, axis=-1)
    xrot = jnp.concatenate([rot_half(xr), rot_half(xc)], axis=-1)
    return (x * cos + xrot * sin).astype(x.dtype)


def sink_softmax(scores, sink):
    s = sink.astype(jnp.float32).reshape(A_KV_HEADS, A_GROUPS, 1, 1)
    s = jnp.broadcast_to(s, scores.shape[:-1] + (1,))
    p = jax.nn.softmax(jnp.concatenate([s, scores], axis=-1), axis=-1)
    return p[..., 1:]


def ctx_attention(q, k, v, sink):
    B, Tc = q.shape[:2]
    nb = Tc // BLOCK
    scale = HEAD_DIM ** -0.5
    qb = q.reshape(B, nb, BLOCK, A_KV_HEADS, A_GROUPS, HEAD_DIM).transpose(1, 0, 2, 3, 4, 5)

    def one_block(qblk):
        s = jnp.einsum('bqkgd,bskd->bkgqs', qblk, k).astype(jnp.float32) * scale
        p = sink_softmax(s, sink)
        return jnp.einsum('bkgqs,bskd->bqkgd', p.astype(v.dtype), v)

    o = lax.map(one_block, qb)
    return o.transpose(1, 0, 2, 3, 4, 5).reshape(B, Tc, A_WIDTH)


def latent_attention(q, k, v, kc, vc, sink):
    B, T = q.shape[:2]
    Tc = kc.shape[1]
    nb = T // BLOCK
    scale = HEAD_DIM ** -0.5
    qb = q.reshape(B, nb, BLOCK, A_KV_HEADS, A_GROUPS, HEAD_DIM)
    pad = ((0, 0), (BLOCK, BLOCK), (0, 0), (0, 0))
    kp = jnp.pad(k, pad).reshape(B, nb + 2, BLOCK, A_KV_HEADS, HEAD_DIM)
    vp = jnp.pad(v, pad).reshape(B, nb + 2, BLOCK, A_KV_HEADS, HEAD_DIM)
    kw = jnp.concatenate([kp[:, :-2], kp[:, 1:-1], kp[:, 2:]], axis=2)
    vw = jnp.concatenate([vp[:, :-2], vp[:, 1:-1], vp[:, 2:]], axis=2)
    qi = jnp.arange(nb)[:, None] * BLOCK + jnp.arange(BLOCK)[None, :]
    kj = (jnp.arange(nb)[:, None] - 1) * BLOCK + jnp.arange(3 * BLOCK)[None, :]
    valid = ((kj[:, None, :] >= 0) & (kj[:, None, :] < T)
             & (jnp.abs(qi[:, :, None] - kj[:, None, :]) <= WINDOW))
    s_w = jnp.einsum('bnqkgd,bnskd->bnkgqs', qb, kw).astype(jnp.float32) * scale
    s_w = jnp.where(valid[None, :, None, None], s_w, NEG_INF)
    s_c = jnp.einsum('bnqkgd,bskd->bnkgqs', qb, kc).astype(jnp.float32) * scale
    p = sink_softmax(jnp.concatenate([s_c, s_w], axis=-1), sink)
    p_c = p[..., :Tc].astype(v.dtype)
    p_w = p[..., Tc:].astype(v.dtype)
    o = (jnp.einsum('bnkgqs,bskd->bnqkgd', p_c, vc)
         + jnp.einsum('bnkgqs,bnskd->bnqkgd', p_w, vw))
    return o.reshape(B, T, A_WIDTH)


def gla_chunked(q, k, v, log_a, s0):
    B, T, H, dk = q.shape
    n = T // GLA_CHUNK

    def chunks(z):
        return z.reshape(B, n, GLA_CHUNK, H, z.shape[-1]).transpose(1, 0, 3, 2, 4)

    qc, kc, vc, ac = chunks(q), chunks(k), chunks(v), chunks(log_a)
    bcum = jnp.cumsum(ac, axis=3)
    b_last = bcum[:, :, :, -1:, :]
    q_e = qc * jnp.exp(bcum)
    k_e = kc * jnp.exp(-bcum)
    k_s = kc * jnp.exp(b_last - bcum)
    mask = jnp.tril(jnp.ones((GLA_CHUNK, GLA_CHUNK), dtype=bool))
    attn = jnp.where(mask, jnp.einsum('nbhid,nbhjd->nbhij', q_e, k_e), 0.0)
    o_intra = jnp.einsum('nbhij,nbhjv->nbhiv', attn, vc)
    decay_last = jnp.exp(b_last[:, :, :, 0, :])
    kv = jnp.einsum('nbhjd,nbhjv->nbhdv', k_s, vc)

    def step(s, xs):
        q_ec, dl, kv_c = xs
        o = jnp.einsum('bhid,bhdv->bhiv', q_ec, s)
        return dl[..., None] * s + kv_c, o

    s_fin, o_inter = lax.scan(step, s0, (q_e, decay_last, kv))
    o = (o_intra + o_inter).transpose(1, 0, 3, 2, 4).reshape(B, T, H, v.shape[-1])
    return o, s_fin


def gla_bidir(q, k, v, lr, w_alpha, b_alpha, s0_f, s0_b):
    B, T = q.shape[:2]
    la = jax.nn.log_sigmoid(jnp.einsum('btzr,zrk->btzk', lr.astype(jnp.float32), w_alpha.astype(jnp.float32))
                            + b_alpha.astype(jnp.float32)) / GLA_NORMALIZER
    la = la.reshape(B, T, 2, B_HEADS, B_DK)
    q, k, v = q.astype(jnp.float32), k.astype(jnp.float32), v.astype(jnp.float32)
    o_f, s_f = gla_chunked(q, k, v, la[:, :, 0], s0_f.astype(jnp.float32))

    def flip(z):
        return jnp.flip(z, axis=1)

    o_b, s_b = gla_chunked(flip(q), flip(k), flip(v), flip(la[:, :, 1]), s0_b.astype(jnp.float32))
    return o_f + flip(o_b), s_f, s_b


def gla_output(o, r, gain):
    B, T = o.shape[:2]
    o = o * lax.rsqrt(jnp.mean(o * o, axis=-1, keepdims=True) + EPS)
    o = o.reshape(B, T, B_WIDTH) * gain.astype(jnp.float32)
    return (o * jax.nn.silu(r.astype(jnp.float32))).astype(r.dtype)


def even_project(h, w_in):
    B, T, _ = h.shape
    qa, ka, va, qb, kb, vb, rb, lr = split_cols(h @ w_in, SIZES_EVEN)
    qa = qa.reshape(B, T, A_KV_HEADS, A_GROUPS, HEAD_DIM)
    ka = ka.reshape(B, T, A_KV_HEADS, HEAD_DIM)
    va = va.reshape(B, T, A_KV_HEADS, HEAD_DIM)
    qb = qb.reshape(B, T, B_HEADS, B_DK) * (B_DK ** -0.5)
    kb = kb.reshape(B, T, B_HEADS, B_DK)
    vb = vb.reshape(B, T, B_HEADS, B_DV)
    lr = lr.reshape(B, T, 2, GLA_RANK)
    return qa, ka, va, qb, kb, vb, rb, lr


def even_ctx(h, w_in, sink, w_alpha, b_alpha, gla_gain, w_out):
    B = h.shape[0]
    qa, ka, va, qb, kb, vb, rb, lr = even_project(h, w_in)
    oa = ctx_attention(qa, ka, va, sink)
    zeros = jnp.zeros((B, B_HEADS, B_DK, B_DV), jnp.float32)
    ob, s_f, s_b = gla_bidir(qb, kb, vb, lr, w_alpha, b_alpha, zeros, zeros)
    out = jnp.concatenate([oa, gla_output(ob, rb, gla_gain)], axis=-1) @ w_out
    return out, ka, va, jnp.stack([s_f, s_b], axis=1)


def even_lat(h, kc, vc, s0, w_in, sink, w_alpha, b_alpha, gla_gain, w_out):
    qa, ka, va, qb, kb, vb, rb, lr = even_project(h, w_in)
    oa = latent_attention(rope_2d(qa), rope_2d(ka), va, kc, vc, sink)
    ob, _, _ = gla_bidir(qb, kb, vb, lr, w_alpha, b_alpha, s0[:, 0], s0[:, 1])
    return jnp.concatenate([oa, gla_output(ob, rb, gla_gain)], axis=-1) @ w_out


def conv_centred(x, w, b):
    T = x.shape[1]
    left = CONV_W // 2
    xp = jnp.pad(x, ((0, 0), (left, CONV_W - 1 - left), (0, 0)))
    y = xp[:, 0:T] * w[0]
    for i in range(1, CONV_W):
        y = y + xp[:, i:i + T] * w[i]
    return y + b


def blockdiag(x, w, b):
    B, T, _ = x.shape
    y = jnp.einsum('btnc,ncd->btnd', x.reshape(B, T, LRU_BLOCKS, LRU_BLOCK_W), w.astype(jnp.float32))
    return y.reshape(B, T, D_RNN) + b.astype(jnp.float32)


def _lin_combine(e1, e2):
    a1, b1 = e1
    a2, b2 = e2
    return a1 * a2, a2 * b1 + b2


def rglru_dir(x, w_a, b_a, w_x, b_x, lam, h0):
    xf = x.astype(jnp.float32)
    r = jax.nn.sigmoid(blockdiag(xf, w_a, b_a))
    i = jax.nn.sigmoid(blockdiag(xf, w_x, b_x))
    log_a = -LRU_C * r * jax.nn.softplus(-lam.astype(jnp.float32))
    a = jnp.exp(log_a)
    u = jnp.sqrt(-jnp.expm1(2.0 * log_a)) * (i * xf)
    u = u.at[:, 0].add(a[:, 0] * h0.astype(jnp.float32))
    _, hs = lax.associative_scan(_lin_combine, (a, u), axis=1)
    return hs, hs[:, -1]


def odd_mixer(h, h0, w_in, conv_w, conv_b, w_ga, b_ga, w_gx, b_gx, lam, w_out):
    g, u = jnp.split(h @ w_in, 2, axis=-1)
    u = conv_centred(u, conv_w, conv_b)
    y_f, hf = rglru_dir(u, w_ga[0], b_ga[0], w_gx[0], b_gx[0], lam[0], h0[:, 0])
    y_b, hb = rglru_dir(jnp.flip(u, axis=1), w_ga[1], b_ga[1], w_gx[1], b_gx[1], lam[1], h0[:, 1])
    y = y_f + jnp.flip(y_b, axis=1)
    out = (jax.nn.gelu(g).astype(jnp.float32) * y).astype(h.dtype) @ w_out
    return out, jnp.stack([hf, hb], axis=1)


def swiglu(h, w_in, w_out):
    g, u = jnp.split(h @ w_in, 2, axis=-1)
    return (jax.nn.silu(g) * u) @ w_out


def setup_inputs(seed: int = 0) -> dict:
    key = jax.random.key(seed)
    ks = iter(jax.random.split(key, 32))

    def nrm(shape, s):
        return jax.random.normal(next(ks), shape, jnp.float32) * s

    D = D_MODEL
    x_prompt = nrm((BATCH, SEQ, D), 1.0)
    x_sample = nrm((DEC_BATCH, DEC_SEQ, D), 1.0)
    c = nrm((DEC_BATCH, D), 1.0)
    cache_k = nrm((DEC_BATCH, N_EVEN, PAST_LEN, A_KV_HEADS, HEAD_DIM), 1.0)
    cache_v = nrm((DEC_BATCH, N_EVEN, PAST_LEN, A_KV_HEADS, HEAD_DIM), 1.0)
    state_gla = nrm((DEC_BATCH, N_EVEN, 2, B_HEADS, B_DK, B_DV), 0.5)
    state_lru = nrm((DEC_BATCH, N_ODD, 2, D_RNN), 0.5)
    c_ctx = nrm((D,), 1.0)
    w_ada = nrm((DEPTH, D, 6 * D), 0.5 * D ** -0.5)
    b_ada = nrm((DEPTH, 6 * D), 0.01)
    norm_mix = 1.0 + nrm((DEPTH, D), 0.05)
    norm_ffn = 1.0 + nrm((DEPTH, D), 0.05)
    w_in_even = nrm((N_EVEN, D, P_EVEN), D ** -0.5)
    attn_sink = nrm((N_EVEN, A_Q_HEADS), 1.0)
    w_alpha = nrm((N_EVEN, 2, GLA_RANK, B_HEADS * B_DK), GLA_RANK ** -0.5)
    b_alpha = nrm((N_EVEN, 2, B_HEADS * B_DK), 0.1)
    gla_gain = 1.0 + nrm((N_EVEN, B_WIDTH), 0.05)
    w_out_even = nrm((N_EVEN, MIX_WIDTH, D), MIX_WIDTH ** -0.5)
    w_in_odd = nrm((N_ODD, D, 2 * D_RNN), D ** -0.5)
    conv_w = nrm((N_ODD, CONV_W, D_RNN), CONV_W ** -0.5)
    conv_b = nrm((N_ODD, D_RNN), 0.01)
    w_gate_a = nrm((N_ODD, 2, LRU_BLOCKS, LRU_BLOCK_W, LRU_BLOCK_W), LRU_BLOCK_W ** -0.5)
    b_gate_a = nrm((N_ODD, 2, D_RNN), 0.01)
    w_gate_x = nrm((N_ODD, 2, LRU_BLOCKS, LRU_BLOCK_W, LRU_BLOCK_W), LRU_BLOCK_W ** -0.5)
    b_gate_x = nrm((N_ODD, 2, D_RNN), 0.01)
    a0 = jax.random.uniform(next(ks), (N_ODD, 2, D_RNN), jnp.float32, minval=0.9, maxval=0.999)
    lru_lambda = jnp.log(a0) - jnp.log1p(-a0)
    w_out_odd = nrm((N_ODD, D_RNN, D), D_RNN ** -0.5)
    w_ffn_in = nrm((DEPTH, D, 2 * D_FF), D ** -0.5)
    w_ffn_out = nrm((DEPTH, D_FF, D), D_FF ** -0.5)
    norm_final = 1.0 + nrm((D,), 0.05)
    return {'x_prompt': x_prompt, 'x_sample': x_sample, 'c': c,
            'cache_k': cache_k, 'cache_v': cache_v, 'state_gla': state_gla, 'state_lru': state_lru,
            'c_ctx': c_ctx, 'w_ada': w_ada, 'b_ada': b_ada, 'norm_mix': norm_mix, 'norm_ffn': norm_ffn,
            'w_in_even': w_in_even, 'attn_sink': attn_sink, 'w_alpha': w_alpha, 'b_alpha': b_alpha,
            'gla_gain': gla_gain, 'w_out_even': w_out_even, 'w_in_odd': w_in_odd, 'conv_w': conv_w,
            'conv_b': conv_b, 'w_gate_a': w_gate_a, 'b_gate_a': b_gate_a, 'w_gate_x': w_gate_x,
            'b_gate_x': b_gate_x, 'lru_lambda': lru_lambda, 'w_out_odd': w_out_odd,
            'w_ffn_in': w_ffn_in, 'w_ffn_out': w_ffn_out, 'norm_final': norm_final}


def reference(x_prompt, x_sample, c, cache_k, cache_v, state_gla, state_lru, c_ctx,
              w_ada, b_ada, norm_mix, norm_ffn, w_in_even, attn_sink, w_alpha, b_alpha, gla_gain,
              w_out_even, w_in_odd, conv_w, conv_b, w_gate_a, b_gate_a, w_gate_x, b_gate_x,
              lru_lambda, w_out_odd, w_ffn_in, w_ffn_out, norm_final):
    xp, xs = x_prompt, x_sample
    new_k, new_v, new_gla, new_lru = [], [], [], []
    for l in range(DEPTH):
        sh1c, sc1c, g1c, sh2c, sc2c, g2c = ada(c_ctx, w_ada[l], b_ada[l])
        sh1s, sc1s, g1s, sh2s, sc2s, g2s = (m[:, None] for m in ada(c, w_ada[l], b_ada[l]))
        hp = modulate(xp, norm_mix[l], sh1c, sc1c)
        hs = modulate(xs, norm_mix[l], sh1s, sc1s)
        if l % 2 == 0:
            e = l // 2
            ew = (w_in_even[e], attn_sink[e], w_alpha[e], b_alpha[e], gla_gain[e], w_out_even[e])
            op, k_c, v_c, s_c = even_ctx(hp, *ew)
            os_ = even_lat(hs, cache_k[:, e], cache_v[:, e], state_gla[:, e], *ew)
            new_k.append(k_c)
            new_v.append(v_c)
            new_gla.append(s_c)
        else:
            o = l // 2
            ow = (w_in_odd[o], conv_w[o], conv_b[o], w_gate_a[o], b_gate_a[o], w_gate_x[o],
                  b_gate_x[o], lru_lambda[o], w_out_odd[o])
            h0 = jnp.zeros((xp.shape[0], 2, D_RNN), jnp.float32)
            op, h_c = odd_mixer(hp, h0, *ow)
            os_, _ = odd_mixer(hs, state_lru[:, o], *ow)
            new_lru.append(h_c)
        xp = xp + g1c * op
        xs = xs + g1s * os_
        xp = xp + g2c * swiglu(modulate(xp, norm_ffn[l], sh2c, sc2c), w_ffn_in[l], w_ffn_out[l])
        xs = xs + g2s * swiglu(modulate(xs, norm_ffn[l], sh2s, sc2s), w_ffn_in[l], w_ffn_out[l])
    y_prompt = rmsnorm(xp, norm_final)
    y_sample = rmsnorm(xs, norm_final)
    new_cache_k = jnp.stack(new_k, axis=1)
    new_cache_v = jnp.stack(new_v, axis=1)
    new_state_gla = jnp.stack(new_gla, axis=1)
    new_state_lru = jnp.stack(new_lru, axis=1)
    return (y_prompt, y_sample, new_cache_k, new_cache_v, new_state_gla, new_state_lru)
```

```python
import os
from contextlib import ExitStack
import numpy as np
import concourse.bass as bass
import concourse.mybir as mybir
from concourse.bass_utils import run_bass_kernel_spmd

F32 = mybir.dt.float32
BF16 = mybir.dt.bfloat16
ALU = mybir.AluOpType
AF = mybir.ActivationFunctionType

D = 1024
KC = 8
DEPTH = 4
TS = 2048
NPS = 2
PL = 256
NT = TS + NPS * PL
TILE = 512
NTILE = NT // TILE
NBLK = NT // 128
DFF = 2816
FC = 22
EPS = 1e-6
P_EVEN = 2336
NBS = TS // 128
C_Q, C_PQ, C_K, C_PK, C_V = 0, 512, 1024, 1152, 1280
NCA = 1408
G_QB, G_KB, G_LR, G_VB, G_RB = 0, 256, 512, 576, 1088
NCG = 1600
NCE = NCA + NCG
NEG = -1e30

DBG_MIX = os.environ.get("KDBG_MIX", "all")
DBG_GLA = os.environ.get("KDBG_GLA", "1") == "1"
DBG_DEPTH = int(os.environ.get("KDBG_DEPTH", str(DEPTH)))


class Tk:
    def __init__(self, ap, name=""):
        self.ap = ap
        self.name = name
        self.psum = False
        self.w = []
        self.r = []

    def __getitem__(self, k):
        return self.ap[k]


class Ctx:
    def __init__(self, nc, es):
        self.nc = nc
        self.es = es
        self.es0 = es
        self.eng = {}
        for nm, obj in (("pe", nc.tensor), ("act", nc.scalar), ("dve", nc.vector),
                        ("pool", nc.gpsimd), ("sp", nc.sync)):
            sem = es.enter_context(nc.semaphore("s_" + nm))
            self.eng[nm] = {"o": obj, "sem": sem, "cnt": 0, "seen": {}, "name": nm}
        self.nsem = 0

    def new_sem(self, name):
        self.nsem += 1
        return self.es0.enter_context(self.nc.semaphore(name))

    def sb(self, name, shape, dt):
        self.nsem += 1
        name = f"{name}_{self.nsem}"
        t = self.es.enter_context(self.nc.sbuf_tensor(name, shape, dt))
        return Tk(t, name)

    def ps(self, name, shape, dt=F32):
        t = self.es.enter_context(self.nc.psum_tensor(name, shape, dt))
        tk = Tk(t, name)
        tk.psum = True
        return tk

    def dram(self, name, shape, dt):
        t = self.nc.dram_tensor(name, shape, dt)
        return Tk(t.ap(), name)

    def _waits(self, e, reads, writes):
        need = {}
        toks = []
        for t in reads:
            toks.extend(t.w)
            if t.psum:
                toks.extend(r for r in t.r if r[2] != e["name"])
        for t in writes:
            toks.extend(t.w)
            toks.extend(t.r)
        for (sem, val, owner) in toks:
            if owner == "pe" and e["name"] == "pe":
                continue
            k = id(sem)
            if e["seen"].get(k, 0) >= val:
                continue
            if k not in need or need[k][1] < val:
                need[k] = (sem, val, owner)
        for k, (sem, val, owner) in need.items():
            if owner in self.eng and val > self.eng[owner]["cnt"] and owner != "dma":
                raise RuntimeError(f"wait on pending token {owner} {val}")
            e["o"].wait_ge(sem, val)
            e["seen"][k] = val

    def op(self, en, fn, reads=(), writes=(), inc=True):
        e = self.eng[en]
        self._waits(e, reads, writes)
        ins = fn(e["o"])
        if inc:
            e["cnt"] += 1
            ins.then_inc(e["sem"], 1)
            tok = (e["sem"], e["cnt"], en)
        else:
            tok = (e["sem"], e["cnt"] + 1, en)
        for t in writes:
            t.w = [tok]
            t.r = []
        for t in reads:
            t.r.append(tok)
        return ins

    def barrier(self):
        for en, e in self.eng.items():
            for on, oe in self.eng.items():
                if on == en or oe["cnt"] == 0:
                    continue
                if e["seen"].get(id(oe["sem"]), 0) < oe["cnt"]:
                    e["o"].wait_ge(oe["sem"], oe["cnt"])
                    e["seen"][id(oe["sem"])] = oe["cnt"]

    def dma(self, en, out_t, out_ap, in_t, in_ap, stream, partial=False):
        e = self.eng[en]
        if partial:
            sv = out_t.w
            out_t.w = []
            self._waits(e, [in_t], [out_t])
            out_t.w = sv
        else:
            self._waits(e, [in_t], [out_t])
        ins = e["o"].dma_start(out=out_ap, in_=in_ap)
        stream["cnt"] += 16
        ins.then_inc(stream["sem"], 16)
        tok = (stream["sem"], stream["cnt"], "dma")
        if partial:
            out_t.w = out_t.w + [tok]
        else:
            out_t.w = [tok]
        out_t.r = []
        in_t.r.append(tok)
        return ins

    def stream(self, name):
        return {"sem": self.new_sem(name), "cnt": 0}

    def collective(self, cin, cout, stream, groups=None):
        e = self.eng["pool"]
        self._waits(e, [cin], [cout])
        ins = self.nc.gpsimd.collective_compute(
            "AllGather", ALU.bypass, replica_groups=groups or [[0, 1], [2, 3], [4, 5], [6, 7]],
            ins=[cin.ap.opt()], outs=[cout.ap.opt()])
        stream["cnt"] += 1
        ins.then_inc(stream["sem"])
        tok = (stream["sem"], stream["cnt"], "dma")
        cout.w = [tok]
        cout.r = []
        cin.r.append(tok)

    def wait_tile(self, en, t):
        e = self.eng[en]
        self._waits(e, [t], [t])


class Ring:
    def __init__(self, tiles):
        self.tiles = tiles
        self.i = 0

    def next(self):
        t = self.tiles[self.i % len(self.tiles)]
        self.i += 1
        return t


def build_program():
    nc = bass.Bass("TRN2", target_bir_lowering=False)

    def din(name, shape, dt=F32):
        return Tk(nc.dram_tensor(name, list(shape), dt, kind="ExternalInput").ap(), name)

    def dout(name, shape, dt=F32):
        return Tk(nc.dram_tensor(name, list(shape), dt, kind="ExternalOutput").ap(), name)

    xin = din("xin", [128, KC, NT])
    cvt = din("cvt", [128, KC, 2])
    w_ada = din("w_ada", [DEPTH, D, 3072])
    b_ada_s = din("b_ada_s", [128, DEPTH * 24, 1])
    nmix2 = din("nmix2", [DEPTH, 128, KC, 2])
    nffn2 = din("nffn2", [DEPTH, 128, KC, 2])
    nfin = din("nfin", [128, KC])
    w_ffn_in = din("w_ffn_in", [DEPTH, D, 2 * DFF])
    w_ffn_out = din("w_ffn_out", [DEPTH, DFF, D])
    ident_f = din("ident_f", [128, 128])
    w_in_odd = din("w_in_odd", [2, D, 2 * D])
    w_out_odd = din("w_out_odd", [2, D, D])
    w_out_even = din("w_out_even", [2, D, D])
    w_gate_a = din("w_gate_a", [2, 2, 4, 256, 256])
    w_gate_x = din("w_gate_x", [2, 2, 4, 256, 256])
    lruv = din("lruv", [2, 128, KC, 14])
    selm = din("selm", [128, 2])
    w_ev = din("w_ev", [2, D, NCE])
    rope_d = din("rope_d", [64, 2, TS])
    ckT = din("ckT", [2, 2, 64, 256])
    walp = din("walp", [2, 64, 256])
    glav = din("glav", [2, 64, 520])
    gmask = din("gmask", [2, 64, 512])
    sgl = din("sgl", [2, 2, 4, 64, 128])
    gla_o = dout("gla_o", [NPS, 2, 2, 4, 64, 128])
    cvaug = din("cvaug", [2, 2, 128, 130])
    sinkb = din("sinkb", [2, 128, 8])
    maskT = din("maskT", [3, 128, 512])
    nk_o = dout("nk_o", [NPS, 2, PL, 128])
    nv_o = dout("nv_o", [NPS, 2, PL, 128])

    y_out = dout("y", [128, KC, NT])

    with ExitStack() as es:
        cx = Ctx(nc, es)
        sp_st = {}

        def stm(name):
            if name not in sp_st:
                sp_st[name] = cx.stream("st_" + name)
            return sp_st[name]

        def drain():
            e = cx.eng["sp"]
            for st in list(sp_st.values()):
                k = id(st["sem"])
                if st["cnt"] > 0 and e["seen"].get(k, 0) < st["cnt"]:
                    nc.sync.wait_ge(st["sem"], st["cnt"])
                    e["seen"][k] = st["cnt"]
            e["cnt"] += 1
            nc.sync.nop().then_inc(e["sem"], 1)
            cx.barrier()
            for oe in cx.eng.values():
                for st in sp_st.values():
                    if st["cnt"] > 0:
                        oe["seen"][id(st["sem"])] = max(oe["seen"].get(id(st["sem"]), 0), st["cnt"])

        xs_full = cx.dram("xs", [128, KC, NT], F32)
        xs_t = [Tk(xs_full.ap[:, :, t * TILE:(t + 1) * TILE], f"xs{t}") for t in range(NTILE)]
        s_in_d = [cx.dram(f"s_in{i}", [11, 128, KC, 512], BF16) for i in range(2)]
        s_out_d = [cx.dram(f"s_out{i}", [KC, 128, FC, 128], BF16) for i in range(2)]
        s_in = [[Tk(s_in_d[i].ap[q], f"s_in{i}_{q}") for q in range(11)] for i in range(2)]
        s_out = [[Tk(s_out_d[i].ap[q], f"s_out{i}_{q}") for q in range(KC)] for i in range(2)]

        ident = cx.sb("ident", [128, 128], F32)
        ident_b = cx.sb("ident_b", [128, 128], BF16)
        onesD = cx.sb("onesD", [128, 128], BF16)
        nfin_sb = cx.sb("nfin_sb", [128, KC], F32)
        cx.dma("sp", ident, ident[:, :], ident_f, ident_f[:, :], stm("c0"))
        cx.dma("sp", nfin_sb, nfin_sb[:, :], nfin, nfin[:, :], stm("c1"))
        cx.op("dve", lambda v: v.memset(onesD[:, :], 1.0 / D), writes=[onesD])
        epsb = cx.sb("epsb", [128, 1], F32)
        cx.op("dve", lambda v: v.memset(epsb[:, :], EPS), writes=[epsb])
        cx.op("dve", lambda v: v.tensor_copy(out=ident_b[:, :], in_=ident[:, :]), reads=[ident], writes=[ident_b])

        psum = Ring([cx.ps(f"ps{i}", [128, 512]) for i in range(6)])
        accb = [cx.ps(f"acc{i}", [128, 512]) for i in range(2)]

        mod = []
        for l in range(DEPTH):
            mod.append({k: cx.sb(f"mod_{k}{l}", [128, KC, 2], F32) for k in ("A1", "B1", "G1", "A2", "B2", "G2")})

        with ExitStack() as es2:
            cx.es = es2
            cv = cx.sb("cv", [128, KC, 2], F32)
            cth = cx.sb("cth", [128, KC, 2], F32)
            csl = cx.sb("csl", [128, KC, 2], F32)
            cx.dma("sp", cv, cv[:, :, :], cvt, cvt[:, :, :], stm("c2"))
            cx.op("act", lambda a: a.activation(out=cth[:, :, :], in_=cv[:, :, :], func=AF.Tanh, scale=0.5),
                  reads=[cv], writes=[cth])
            cx.op("dve", lambda v: v.scalar_tensor_tensor(out=csl[:, :, :], in0=cth[:, :, :], scalar=1.0,
                                                          in1=cv[:, :, :], op0=ALU.add, op1=ALU.mult),
                  reads=[cth, cv], writes=[csl])
            cx.op("dve", lambda v: v.tensor_scalar(out=csl[:, :, :], in0=csl[:, :, :], scalar1=0.5, scalar2=None,
                                                   op0=ALU.mult),
                  reads=[csl], writes=[csl])
            wada_ring = Ring([cx.sb(f"wada{i}", [128, KC, 768], F32) for i in range(2)])
            bs = cx.sb("bs", [128, DEPTH * 24, 1], F32)
            cx.dma("sp", bs, bs[:, :, :], b_ada_s, b_ada_s[:, :, :], stm("c3"))
            pa = psum.next()
            wi = 0
            for l in range(DEPTH):
                wsrc = w_ada[l].rearrange("(kc p) n -> p kc n", p=128)
                for g in range(4):
                    wt = wada_ring.next()
                    cx.dma("sp", wt, wt[:, :, :], w_ada, wsrc[:, :, g * 768:(g + 1) * 768], stm(f"wada_s{wi % 2}"))
                    wi += 1
                    for mm in range(6):
                        o2 = (l * 24 + g * 6 + mm) * 2
                        for kc in range(KC):
                            cx.op("pe", lambda t, mm=mm, kc=kc, wt=wt, o2=o2: t.matmul(
                                pa[:, o2:o2 + 2], lhsT=wt[:, kc, mm * 128:(mm + 1) * 128], rhs=csl[:, kc, :],
                                start=(kc == 0), stop=(kc == KC - 1)),
                                reads=[wt, csl], writes=[pa], inc=(kc == KC - 1))
            part = cx.sb("part", [128, DEPTH * 24, 2], F32)
            cx.op("dve", lambda v: v.tensor_tensor(
                out=part[:, :, :], in0=pa[:, 0:DEPTH * 48].rearrange("p (a b) -> p a b", b=2),
                in1=bs[:, :, :].to_broadcast([128, DEPTH * 24, 2]), op=ALU.add), reads=[pa, bs], writes=[part])
            acin = cx.dram("acin", [128, DEPTH * 48], F32)
            acout = cx.dram("acout", [2 * 128, DEPTH * 48], F32)
            cx.dma("sp", acin, acin[:, :], part, part[:, :, :].rearrange("p a b -> p (a b)"), stm("xin_b"))
            cx.collective(acin, acout, stm("xcc"))
            all2 = cx.sb("all2", [128, 2, DEPTH * 48], F32)
            cx.dma("sp", all2, all2[:, :, :], acout, acout.ap.rearrange("(r p) f -> p r f", p=128), stm("xout_b"))
            ada = cx.sb("ada", [128, 48, 2], F32)
            nm = cx.sb("nm", [128, KC, 2], F32)
            nf = cx.sb("nf", [128, KC, 2], F32)
            for l in range(DBG_DEPTH):
                cx.dma("sp", nm, nm[:, :, :], nmix2, nmix2[l], stm("c4"))
                cx.dma("sp", nf, nf[:, :, :], nffn2, nffn2[l], stm("c5"))
                cx.op("dve", lambda v, l=l: v.tensor_copy(
                    out=ada[:, :, :].rearrange("p (r m) j -> p r (m j)", r=2), in_=all2[:, :, l * 48:(l + 1) * 48]),
                    reads=[all2], writes=[ada])
                md = mod[l]
                for (An, Bn, Gn, base, gn) in (("A1", "B1", "G1", 0, nm), ("A2", "B2", "G2", 24, nf)):
                    cx.op("dve", lambda v, An=An, base=base, gn=gn: v.scalar_tensor_tensor(
                        out=md[An][:, :, :], in0=ada[:, base + 8:base + 16, :], scalar=1.0, in1=gn[:, :, :],
                        op0=ALU.add, op1=ALU.mult), reads=[ada, gn], writes=[md[An]])
                    cx.op("dve", lambda v, Bn=Bn, base=base: v.tensor_copy(out=md[Bn][:, :, :],
                                                                           in_=ada[:, base:base + 8, :]),
                          reads=[ada], writes=[md[Bn]])
                    cx.op("dve", lambda v, Gn=Gn, base=base: v.tensor_copy(out=md[Gn][:, :, :],
                                                                           in_=ada[:, base + 16:base + 24, :]),
                          reads=[ada], writes=[md[Gn]])
            for t in range(NTILE):
                cx.dma("sp", xs_t[t], xs_t[t][:, :, :], xin, xin[:, :, t * TILE:(t + 1) * TILE], stm(f"xin_s{t % 2}"))
            drain()
        cx.es = es

        def prep_ffn_weights(l):
            si, so = s_in[l % 2], s_out[l % 2]
            wsrc = w_ffn_in[l].rearrange("(kc p) n -> p kc n", p=128)
            st = stm(f"pw{l % 2}")
            for q in range(11):
                cx.dma("pool", si[q], si[q][:, :, 0:256], w_ffn_in, wsrc[:, :, q * 256:(q + 1) * 256], st)
                cx.dma("pool", si[q], si[q][:, :, 256:512], w_ffn_in,
                       wsrc[:, :, DFF + q * 256:DFF + (q + 1) * 256], st, partial=True)
            osrc = w_ffn_out[l].rearrange("(fc p) n -> p fc n", p=128)
            for oc in range(KC):
                cx.dma("pool", so[oc], so[oc][:, :, :], w_ffn_out, osrc[:, :, oc * 128:(oc + 1) * 128], st)
            for tk in si + so:
                tk.w = [(st["sem"], st["cnt"], "dma")]

        selm_sb = cx.sb("selm_sb", [128, 2], F32)
        cx.dma("sp", selm_sb, selm_sb[:, :], selm, selm[:, :], stm("c6"))
        xcount = [0]

        def exchange(src, src_ap, P, F_, dt, both, oth, oth_ap, tmp):
            i = xcount[0]
            xcount[0] += 1
            cin = cx.dram(f"xcin{i}", [P, F_], dt)
            cout = cx.dram(f"xcout{i}", [2 * P, F_], dt)
            cx.dma("sp", cin, cin[:, :], src, src_ap, stm("xin_b"))
            cx.collective(cin, cout, stm("xcc"))
            cx.dma("sp", both, both[:, :, :], cout, cout.ap.rearrange("(r p) f -> p r f", p=P), stm("xout_b"))
            cx.op("dve", lambda v: v.tensor_scalar(out=tmp[:, :], in0=both[:, 0, :], scalar1=selm_sb[0:P, 0:1],
                                                   scalar2=None, op0=ALU.mult), reads=[both, selm_sb], writes=[tmp])
            cx.op("dve", lambda v: v.scalar_tensor_tensor(out=oth_ap, in0=both[:, 1, :], scalar=selm_sb[0:P, 1:2],
                                                          in1=tmp[:, :], op0=ALU.mult, op1=ALU.add),
                  reads=[both, selm_sb, tmp], writes=[oth])

        class Phase:
            def __enter__(self_):
                self_.st = ExitStack()
                self_.st.__enter__()
                cx.es = self_.st
                return self_

            def __exit__(self_, *a):
                drain()
                cx.es = es
                return self_.st.__exit__(*a)


        seqs = [(0, TS, True)] + [(TS + PL * i, PL, False) for i in range(NPS)]
        u_d = cx.dram("u_d", [128, KC, NT], F32)
        gg_d = cx.dram("gg_d", [128, KC, NT], BF16)
        z_d = cx.dram("z_d", [128, KC, NT], BF16)
        hsf_d = cx.dram("hsf_d", [128, KC, NT], F32)
        xcs_d = cx.dram("xcs_d", [128, KC, NT], F32)
        q_d = cx.dram("q_d", [64, 8, NT], BF16)
        of_d = cx.dram("of_d", [NT // 64, 64, 512], F32)
        lru_o = dout("lru_o", [128, NPS, 2, 2, KC])

        def mk_norm():
            x_ring = Ring([cx.sb(f"xt{i}", [128, KC, TILE], F32) for i in range(2)])
            sq = cx.sb("sq", [128, KC, TILE], BF16)
            rstd = cx.sb("rstd", [128, TILE], F32)
            rsq = cx.sb("rsq", [128, TILE], F32)
            tmpn = Ring([cx.sb(f"tmpn{i}", [128, TILE], F32) for i in range(2)])
            h_ring = Ring([cx.sb(f"h{i}", [128, KC, TILE], BF16) for i in range(2)])
            cnt = {"x": 0}

            def load_x(t):
                xt = x_ring.next()
                xt.slot = cnt["x"] % 2
                cx.dma("sp", xt, xt[:, :, :], xs_t[t], xs_t[t][:, :, :], stm(f"xload{xt.slot}"))
                cnt["x"] += 1
                return xt

            def store_x(t, xt):
                cx.dma("pool", xs_t[t], xs_t[t][:, :, :], xt, xt[:, :, :], stm(f"xstore{xt.slot}"))

            def stats(xt):
                cx.op("act", lambda a: a.activation(out=sq[:, :, :].rearrange("p a b -> p (a b)"),
                                                    in_=xt[:, :, :].rearrange("p a b -> p (a b)"), func=AF.Square),
                      reads=[xt], writes=[sq])
                pss = psum.next()
                for c in range(KC):
                    cx.op("pe", lambda t_, c=c: t_.matmul(pss[:, :], lhsT=onesD[:, :], rhs=sq[:, c, :],
                                                          start=(c == 0), stop=(c == KC - 1)),
                          reads=[onesD, sq], writes=[pss], inc=(c == KC - 1))
                cx.op("act", lambda a: a.activation(out=rsq[:, :], in_=pss[:, :], func=AF.Sqrt, bias=epsb[:, 0:1],
                                                    scale=1.0), reads=[pss, epsb], writes=[rsq])
                cx.op("dve", lambda v: v.reciprocal(out=rstd[:, :], in_=rsq[:, :]), reads=[rsq], writes=[rstd])
                return rstd

            def norm_mod(xt, A, B, j):
                stats(xt)
                h = h_ring.next()
                for c in range(KC):
                    tm = tmpn.next()
                    cx.op("dve", lambda v, c=c, tm=tm: v.scalar_tensor_tensor(
                        out=tm[:, :], in0=xt[:, c, :], scalar=A[:, c, j:j + 1], in1=rstd[:, :],
                        op0=ALU.mult, op1=ALU.mult), reads=[xt, A, rstd], writes=[tm])
                    cx.op("act", lambda a, c=c, tm=tm: a.activation(
                        out=h[:, c, :], in_=tm[:, :], func=AF.Identity, bias=B[:, c, j:j + 1], scale=1.0),
                        reads=[tm, B], writes=[h])
                return h
            return load_x, store_x, stats, norm_mod

        def tile_j(t):
            return 0 if t * TILE < TS else 1

        def odd_phase1(l):
            o = l // 2
            with Phase():
                load_x, store_x, stats, norm_mod = mk_norm()
                wio = cx.sb("wio", [128, KC, 2 * D], BF16)
                wsrc = w_in_odd[o].rearrange("(kc p) n -> p kc n", p=128)
                cx.dma("pool", wio, wio[:, :, 0:D], w_in_odd, wsrc[:, :, 0:D], stm("wio"))
                cx.dma("pool", wio, wio[:, :, D:2 * D], w_in_odd, wsrc[:, :, D:2 * D], stm("wio"), partial=True)
                ggs = Ring([cx.sb(f"ggs{i}", [128, KC, TILE], BF16) for i in range(2)])
                us = Ring([cx.sb(f"us{i}", [128, KC, TILE], F32) for i in range(2)])
                dummy = Tk(None, "dummy")
                nxt_h = norm_mod(load_x(0), mod[l]["A1"], mod[l]["B1"], tile_j(0))
                for t in range(NTILE):
                    h1 = nxt_h
                    if t + 1 < NTILE:
                        nxt_h = norm_mod(load_x(t + 1), mod[l]["A1"], mod[l]["B1"], tile_j(t + 1))
                    gt, ut = ggs.next(), us.next()
                    for c in range(KC):
                        pg = psum.next()
                        for kc in range(KC):
                            cx.op("pe", lambda t_, kc=kc, c=c, pg=pg: t_.matmul(
                                pg[:, :], lhsT=wio[:, kc, c * 128:(c + 1) * 128], rhs=h1[:, kc, :],
                                start=(kc == 0), stop=(kc == KC - 1)), reads=[wio, h1], writes=[pg],
                                inc=(kc == KC - 1))
                        cx.op("act", lambda a, c=c, pg=pg, gt=gt: a.activation(
                            out=gt[:, c, :], in_=pg[:, :], func=AF.Gelu_apprx_tanh), reads=[pg], writes=[gt])
                        pu = psum.next()
                        for kc in range(KC):
                            cx.op("pe", lambda t_, kc=kc, c=c, pu=pu: t_.matmul(
                                pu[:, :], lhsT=wio[:, kc, D + c * 128:D + (c + 1) * 128], rhs=h1[:, kc, :],
                                start=(kc == 0), stop=(kc == KC - 1)), reads=[wio, h1], writes=[pu],
                                inc=(kc == KC - 1))
                        cx.op("dve", lambda v, c=c, pu=pu, ut=ut: v.tensor_copy(out=ut[:, c, :], in_=pu[:, :]),
                              reads=[pu], writes=[ut])
                    sl = slice(t * TILE, (t + 1) * TILE)
                    cx.dma("sp", dummy, gg_d[:, :, sl], gt, gt[:, :, :], stm(f"ggst{t % 2}"), partial=True)
                    cx.dma("sp", dummy, u_d[:, :, sl], ut, ut[:, :, :], stm(f"ust{t % 2}"), partial=True)
                    dummy.w = []
                    dummy.r = []

        def odd_phase2(l):
            o = l // 2
            LM = TS
            with Phase():
                lv = cx.sb("lv", [128, KC, 14], F32)
                cx.dma("sp", lv, lv[:, :, :], lruv, lruv[o], stm("lv"))
                der = cx.sb("der", [128, KC, 8], F32)
                tmpv = cx.sb("tmpv", [128, KC, 2], F32)
                for d in range(2):
                    b0 = 6 + 4 * d
                    cx.op("dve", lambda v, d=d, b0=b0: v.tensor_scalar(
                        out=der[:, :, 4 * d:4 * d + 2], in0=lv[:, :, b0:b0 + 2], scalar1=-1.0, scalar2=None,
                        op0=ALU.mult), reads=[lv], writes=[der])
                    cx.op("act", lambda a, b0=b0: a.activation(out=tmpv[:, :, 0:1], in_=lv[:, :, b0 + 2:b0 + 3],
                                                               func=AF.Exp, scale=-1.0), reads=[lv], writes=[tmpv])
                    cx.op("act", lambda a: a.activation(out=tmpv[:, :, 1:2], in_=tmpv[:, :, 0:1], func=AF.Ln,
                                                        bias=1.0, scale=1.0), reads=[tmpv], writes=[tmpv])
                    cx.op("dve", lambda v, d=d: v.tensor_scalar(
                        out=der[:, :, 4 * d + 2:4 * d + 3], in0=tmpv[:, :, 1:2], scalar1=-8.0, scalar2=None,
                        op0=ALU.mult), reads=[tmpv], writes=[der])
                    cx.op("dve", lambda v, d=d: v.tensor_scalar(
                        out=der[:, :, 4 * d + 3:4 * d + 4], in0=tmpv[:, :, 1:2], scalar1=-16.0, scalar2=None,
                        op0=ALU.mult), reads=[tmpv], writes=[der])
                wg = cx.sb("wg", [128, 32, 256], BF16)
                gi = 0
                for gate, wsrc_t in ((0, w_gate_a), (1, w_gate_x)):
                    for d in range(2):
                        for nb in range(4):
                            idx = ((gate * 2 + d) * 4 + nb) * 2
                            cx.dma("pool", wg, wg[:, idx:idx + 2, :], wsrc_t,
                                   wsrc_t[o, d, nb].rearrange("(k p) c -> p k c", p=128), stm("wg"),
                                   partial=(gi > 0))
                            gi += 1
                ud = Tk(None, "ud")
                zd = Tk(None, "zd")
                hd = Tk(None, "hd")
                xcd = Tk(None, "xcd")
                uh_src = cx.sb("uh_src", [128, KC, 2], F32)
                uh_oth = cx.sb("uh_oth", [128, KC, 2], F32)
                uh_both = cx.sb("uh_both", [128, 2, 2 * KC], F32)
                uh_tmp = cx.sb("uh_tmp", [128, 2 * KC], F32)
                cx.dma("sp", uh_src, uh_src[:, :, :], ud, u_d[:, :, TS - 2:TS], stm("uhl"))
                exchange(uh_src, uh_src[:, :, :].rearrange("p a b -> p (a b)"), 128, 2 * KC, F32, uh_both, uh_oth,
                         uh_oth[:, :, :].rearrange("p a b -> p (a b)"), uh_tmp)
                hfin = cx.sb("hfin", [128, KC], F32)
                hoth = cx.sb("hoth", [128, KC], F32)
                hx_both = cx.sb("hx_both", [128, 2, KC], F32)
                hx_tmp = cx.sb("hx_tmp", [128, KC], F32)
                ub = cx.sb("ub", [128, LM + 4], F32)
                xc_ring = Ring([([cx.sb(f"xc{i}{m}", [128, LM], F32) for m in range(2)],
                                 [cx.sb(f"xc16{i}{m}", [128, LM], BF16) for m in range(2)]) for i in range(2)])
                ab_ring = Ring([(cx.sb(f"a_t{i}", [128, LM], F32), cx.sb(f"b_t{i}", [128, LM], F32),
                                 cx.sb(f"t1_{i}", [128, LM], F32)) for i in range(2)])
                hs_ring = Ring([[cx.sb(f"hs{i}{d}", [128, LM], F32) for d in range(2)] for i in range(2)])
                gz_ring = Ring([cx.sb(f"gz{i}", [128, LM], BF16) for i in range(2)])
                zt_ring = Ring([cx.sb(f"zt{i}", [128, LM], BF16) for i in range(2)])
                lst = cx.sb("lst", [128, NPS, 2, KC], F32)

                def conv_part(nb, si_, s0, L, is_s, d):
                    xc, xc16 = xc_ring.next()
                    rs_ = xc_ring.i % 2
                    for m in range(2):
                        c = nb * 2 + m
                        if d == 1:
                            X = xc[m]
                            cx.dma("sp", X, X[:, 0:L], xcd, xcs_d[:, c, s0:s0 + L], stm(f"xcl{rs_}{m}"))
                            xcd.r = []
                            cx.op("act", lambda a, X=X, m=m: a.copy(out=xc16[m][:, 0:L], in_=X[:, 0:L]),
                                  reads=[X], writes=[xc16[m]])
                            continue
                        cx.op("pool", lambda g_: g_.memset(ub[:, 0:2], 0.0), writes=[ub])
                        if is_s:
                            cx.op("pool", lambda g_, c=c: g_.tensor_copy(out=ub[:, L + 2:L + 3], in_=uh_oth[:, c, 1:2]),
                                  reads=[uh_oth], writes=[ub])
                            cx.op("pool", lambda g_, c=c: g_.tensor_copy(out=ub[:, L + 3:L + 4], in_=uh_oth[:, c, 0:1]),
                                  reads=[uh_oth], writes=[ub])
                        else:
                            cx.op("pool", lambda g_: g_.memset(ub[:, L + 2:L + 4], 0.0), writes=[ub])
                        cx.dma("sp", ub, ub[:, 2:2 + L], ud, u_d[:, c, s0:s0 + L], stm("ubl"))
                        X = xc[m]
                        cx.op("dve", lambda v, c=c, X=X: v.tensor_scalar(
                            out=X[:, 0:L], in0=ub[:, 0:L], scalar1=lv[:, c, 0:1], scalar2=lv[:, c, 5:6],
                            op0=ALU.mult, op1=ALU.add), reads=[ub, lv], writes=[X])
                        for i in range(1, 5):
                            cx.op("dve", lambda v, c=c, X=X, i=i: v.scalar_tensor_tensor(
                                out=X[:, 0:L], in0=ub[:, i:i + L], scalar=lv[:, c, i:i + 1], in1=X[:, 0:L],
                                op0=ALU.mult, op1=ALU.add), reads=[ub, lv, X], writes=[X])
                        cx.op("act", lambda a, X=X, m=m: a.copy(out=xc16[m][:, 0:L], in_=X[:, 0:L]),
                              reads=[X], writes=[xc16[m]])
                        cx.dma("pool", xcd, xcs_d[:, c, s0:s0 + L], X, X[:, 0:L], stm(f"xcst{rs_}{m}"))
                        xcd.w, xcd.r = [], []
                    return xc, xc16

                def gate_part(nb, si_, s0, L, is_s, d, xc, xc16):
                    for m in range(2):
                        c = nb * 2 + m
                        ts_ = min(512, L)
                        a_t, b_t, t1 = ab_ring.next()
                        hs = hs_ring.next()
                        ri_ = hs_ring.i % 2
                        if d == 1:
                            gz, zt = gz_ring.next(), zt_ring.next()
                            cx.dma("sp", hs[0], hs[0][:, 0:L], hd, hsf_d[:, c, s0:s0 + L], stm(f"hsld{ri_}"))
                            cx.dma("sp", gz, gz[:, 0:L], ud, gg_d[:, c, s0:s0 + L], stm(f"gzl{ri_}"))
                            hd.r, ud.r = [], []
                        for s_ in range(L // ts_):
                            sl = slice(s_ * ts_, (s_ + 1) * ts_)
                            pa, px = psum.next(), psum.next()
                            for gate, pp in ((0, pa), (1, px)):
                                for k2 in range(2):
                                    idx = ((gate * 2 + d) * 4 + nb) * 2 + k2
                                    cx.op("pe", lambda t_, idx=idx, pp=pp, k2=k2, m=m, sl=sl: t_.matmul(
                                        pp[:, 0:ts_], lhsT=wg[:, idx, m * 128:(m + 1) * 128],
                                        rhs=xc16[k2][:, sl], start=(k2 == 0), stop=(k2 == 1)),
                                        reads=[wg, xc16[k2]], writes=[pp], inc=(k2 == 1))
                            cx.op("act", lambda a, sl=sl: a.activation(
                                out=a_t[:, sl], in_=pa[:, 0:ts_], func=AF.Exp, scale=-1.0,
                                bias=der[:, c, 4 * d:4 * d + 1]), reads=[pa, der], writes=[a_t])
                            cx.op("act", lambda a, sl=sl: a.activation(
                                out=b_t[:, sl], in_=px[:, 0:ts_], func=AF.Exp, scale=-1.0,
                                bias=der[:, c, 4 * d + 1:4 * d + 2]), reads=[px, der], writes=[b_t])
                        A, B, T1 = a_t[:, 0:L], b_t[:, 0:L], t1[:, 0:L]
                        cx.op("act", lambda a: a.activation(out=A, in_=A, func=AF.Ln, scale=1.0, bias=1.0),
                              reads=[a_t], writes=[a_t])
                        cx.op("act", lambda a: a.activation(out=A, in_=A, func=AF.Exp, scale=-1.0),
                              reads=[a_t], writes=[a_t])
                        cx.op("act", lambda a: a.activation(out=T1, in_=A, func=AF.Exp,
                                                            scale=der[:, c, 4 * d + 3:4 * d + 4]),
                              reads=[a_t, der], writes=[t1])
                        cx.op("act", lambda a: a.activation(out=T1, in_=T1, func=AF.Ln, scale=-1.0, bias=1.0),
                              reads=[t1], writes=[t1])
                        cx.op("act", lambda a: a.activation(out=T1, in_=T1, func=AF.Exp, scale=0.5),
                              reads=[t1], writes=[t1])
                        cx.op("act", lambda a: a.activation(out=A, in_=A, func=AF.Exp,
                                                            scale=der[:, c, 4 * d + 2:4 * d + 3]),
                              reads=[a_t, der], writes=[a_t])
                        cx.op("dve", lambda v: v.tensor_scalar(out=B, in0=B, scalar1=1.0, scalar2=None, op0=ALU.add),
                              reads=[b_t], writes=[b_t])
                        cx.op("dve", lambda v: v.reciprocal(out=B, in_=B), reads=[b_t], writes=[b_t])
                        cx.op("dve", lambda v, m=m: v.tensor_tensor(out=B, in0=B, in1=xc[m][:, 0:L], op=ALU.mult),
                              reads=[b_t, xc[m]], writes=[b_t])
                        cx.op("dve", lambda v: v.tensor_tensor(out=B, in0=B, in1=T1, op=ALU.mult),
                              reads=[b_t, t1], writes=[b_t])
                        H = hs[d]
                        if d == 0:
                            init = lv[:, c, 9:10] if is_s else 0.0
                            cx.op("dve", lambda v, H=H, init=init: v.tensor_tensor_scan(
                                out=H[:, 0:L], data0=a_t[:, 0:L], data1=b_t[:, 0:L], initial=init,
                                op0=ALU.mult, op1=ALU.add), reads=[a_t, b_t, lv], writes=[H])
                            cx.dma("pool", hd, hsf_d[:, c, s0:s0 + L], H, H[:, 0:L], stm(f"hsst{ri_}"))
                            hd.w, hd.r = [], []
                            if is_s:
                                cx.op("pool", lambda g_, H=H, c=c: g_.tensor_copy(
                                    out=hfin[:, c:c + 1], in_=H[:, L - 1:L]), reads=[H], writes=[hfin])
                            else:
                                cx.op("pool", lambda g_, H=H, c=c: g_.tensor_copy(
                                    out=lst[:, si_ - 1, 0, c:c + 1], in_=H[:, L - 1:L]), reads=[H], writes=[lst])
                        else:
                            init = hoth[:, c:c + 1] if is_s else 0.0
                            cx.op("dve", lambda v, H=H, init=init: v.tensor_tensor_scan(
                                out=H[:, 0:L][:, ::-1], data0=a_t[:, 0:L][:, ::-1], data1=b_t[:, 0:L][:, ::-1],
                                initial=init, op0=ALU.mult, op1=ALU.add), reads=[a_t, b_t, hoth], writes=[H])
                            if not is_s:
                                cx.op("pool", lambda g_, H=H, c=c: g_.tensor_copy(
                                    out=lst[:, si_ - 1, 1, c:c + 1], in_=H[:, 0:1]), reads=[H], writes=[lst])
                            cx.op("dve", lambda v: v.tensor_tensor(out=hs[0][:, 0:L], in0=hs[0][:, 0:L],
                                                                   in1=hs[1][:, 0:L], op=ALU.add),
                                  reads=[hs[0], hs[1]], writes=[hs[0]])
                            cx.op("dve", lambda v: v.tensor_tensor(out=zt[:, 0:L], in0=hs[0][:, 0:L],
                                                                   in1=gz[:, 0:L], op=ALU.mult),
                                  reads=[hs[0], gz], writes=[zt])
                            cx.dma("pool", zd, z_d[:, c, s0:s0 + L], zt, zt[:, 0:L], stm(f"zst{ri_}"))
                            zd.w, zd.r, hd.r = [], [], []
                        ud.r = []

                items = [(nb, si_, s0, L, is_s) for nb in range(4) for si_, (s0, L, is_s) in enumerate(seqs)]
                for d in range(2):
                    nxt_c = conv_part(*items[0], d)
                    for k_, it in enumerate(items):
                        cur_c = nxt_c
                        if k_ + 1 < len(items):
                            nxt_c = conv_part(*items[k_ + 1], d)
                        gate_part(*it, d, *cur_c)
                    if d == 0:
                        exchange(hfin, hfin[:, :], 128, KC, F32, hx_both, hoth, hoth[:, :], hx_tmp)
                        drain()
                cx.dma("sp", lru_o, lru_o[:, :, o, :, :], lst, lst[:, :, :, :], stm("lruo"))

        def even_attn(l):
            e = l // 2
            with Phase():
                load_x, store_x, stats, norm_mod = mk_norm()
                wea = cx.sb("wea", [128, KC, NCA], BF16)
                wsrc = w_ev[e].rearrange("(kc p) n -> p kc n", p=128)
                cx.dma("pool", wea, wea[:, :, 0:1024], w_ev, wsrc[:, :, 0:1024], stm("wea"))
                cx.dma("pool", wea, wea[:, :, 1024:NCA], w_ev, wsrc[:, :, 1024:NCA], stm("wea"), partial=True)
                kT = cx.sb("kT", [64, 2, NT + 128], BF16)
                vaug = cx.sb("vaug", [128, NBLK + 1, 130], BF16)
                cx.op("dve", lambda g_: g_.memset(vaug[:, :, :], 1.0), writes=[vaug])
                kc_sb = cx.sb("kc_sb", [64, 2, 256], BF16)
                vc_sb = cx.sb("vc_sb", [128, 2, 130], BF16)
                kc_f = cx.sb("kc_f", [64, 2, 256], F32)
                vc_f = cx.sb("vc_f", [128, 2, 130], F32)
                msk_f = cx.sb("msk_f", [128, 3, 512], F32)
                msk = cx.sb("msk", [128, 3, 512], BF16)
                for k_ in range(2):
                    cx.dma("sp", kc_f, kc_f[:, k_, :], ckT, ckT[e, k_], stm("kcl"), partial=(k_ > 0))
                    cx.dma("sp", vc_f, vc_f[:, k_, :], cvaug, cvaug[e, k_], stm("vcl"), partial=(k_ > 0))
                for k_ in range(3):
                    cx.dma("sp", msk_f, msk_f[:, k_, :], maskT, maskT[k_], stm("mskl"), partial=(k_ > 0))
                cx.op("dve", lambda v: v.tensor_copy(out=kc_sb[:, :, :], in_=kc_f[:, :, :]), reads=[kc_f], writes=[kc_sb])
                cx.op("dve", lambda v: v.tensor_copy(out=vc_sb[:, :, :], in_=vc_f[:, :, :]), reads=[vc_f], writes=[vc_sb])
                cx.op("dve", lambda v: v.tensor_copy(out=msk[:, :, :], in_=msk_f[:, :, :]), reads=[msk_f], writes=[msk])
                snk = cx.sb("snk", [128, 8], F32)
                esink = cx.sb("esink", [128, 8], F32)
                cx.dma("sp", snk, snk[:, :], sinkb, sinkb[e], stm("snkl"))
                cx.op("act", lambda a: a.activation(out=esink[:, :], in_=snk[:, :], func=AF.Exp),
                      reads=[snk], writes=[esink])
                rc_ring = Ring([cx.sb(f"rc{i}", [64, 2, TILE], F32) for i in range(2)])
                t1r = Ring([cx.sb(f"t1r{i}", [64, TILE], F32) for i in range(2)])
                t2r = Ring([cx.sb(f"t2r{i}", [64, TILE], F32) for i in range(2)])
                qst_ring = Ring([cx.sb(f"qst{i}", [64, 8, TILE], BF16) for i in range(2)])
                kvst = Ring([cx.sb(f"kvst{i}", [128, 128], F32) for i in range(4)])
                qd = Tk(None, "qd")
                cn = {"rc": 0, "q": 0, "kv": 0, "qb": 0, "oa": 0}

                def proj_fm(h1, col0, pcol0, is_s, rc, dt_, dap):
                    pq = psum.next()
                    for kc in range(KC):
                        cx.op("pe", lambda t_, kc=kc: t_.matmul(
                            pq[0:64, :], lhsT=wea[:, kc, col0:col0 + 64], rhs=h1[:, kc, :],
                            start=(kc == 0), stop=(kc == KC - 1)), reads=[wea, h1], writes=[pq], inc=(kc == KC - 1))
                    if not is_s:
                        cx.op("act", lambda a: a.copy(out=dap, in_=pq[0:64, :]), reads=[pq], writes=[dt_])
                        return
                    pp = psum.next()
                    for kc in range(KC):
                        cx.op("pe", lambda t_, kc=kc: t_.matmul(
                            pp[0:64, :], lhsT=wea[:, kc, pcol0:pcol0 + 64], rhs=h1[:, kc, :],
                            start=(kc == 0), stop=(kc == KC - 1)), reads=[wea, h1], writes=[pp], inc=(kc == KC - 1))
                    t1, t2 = t1r.next(), t2r.next()
                    cx.op("dve", lambda v: v.tensor_tensor(out=t1[:, :], in0=pq[0:64, :], in1=rc[:, 0, :], op=ALU.mult),
                          reads=[pq, rc], writes=[t1])
                    cx.op("dve", lambda v: v.tensor_tensor(out=t2[:, :], in0=pp[0:64, :], in1=rc[:, 1, :], op=ALU.mult),
                          reads=[pp, rc], writes=[t2])
                    cx.op("dve", lambda g_: g_.tensor_tensor(out=dap, in0=t1[:, :], in1=t2[:, :], op=ALU.add),
                          reads=[t1, t2], writes=[dt_])

                nxt_h = norm_mod(load_x(0), mod[l]["A1"], mod[l]["B1"], tile_j(0))
                for t in range(NTILE):
                    is_s = t * TILE < TS
                    h1 = nxt_h
                    if t + 1 < NTILE:
                        nxt_h = norm_mod(load_x(t + 1), mod[l]["A1"], mod[l]["B1"], tile_j(t + 1))
                    rc = None
                    if is_s:
                        rc = rc_ring.next()
                        cx.dma("sp", rc, rc[:, :, :], rope_d, rope_d[:, :, t * TILE:(t + 1) * TILE],
                               stm(f"rc{cn['rc'] % 2}"))
                        cn["rc"] += 1
                    qst = qst_ring.next()
                    for h in range(8):
                        proj_fm(h1, C_Q + h * 64, C_PQ + h * 64, is_s, rc, qst, qst[:, h, :])
                    cx.dma("sp", qd, q_d[:, :, t * TILE:(t + 1) * TILE], qst, qst[:, :, :], stm(f"qst{cn['q'] % 2}"))
                    cn["q"] += 1
                    qd.w, qd.r = [], []
                    for kv in range(2):
                        proj_fm(h1, C_K + kv * 64, C_PK + kv * 64, is_s, rc, kT, kT[:, kv, t * TILE:(t + 1) * TILE])
                    for b in range(4):
                        blk = t * 4 + b
                        pv = psum.next()
                        for kc in range(KC):
                            cx.op("pe", lambda t_, kc=kc: t_.matmul(
                                pv[:, 0:128], lhsT=h1[:, kc, b * 128:(b + 1) * 128], rhs=wea[:, kc, C_V:C_V + 128],
                                start=(kc == 0), stop=(kc == KC - 1)), reads=[wea, h1], writes=[pv],
                                inc=(kc == KC - 1))
                        cx.op("act", lambda a: a.copy(
                            out=vaug[:, blk, :].rearrange("p (k d) -> p k d", k=2)[:, :, 0:64],
                            in_=pv[:, 0:128].rearrange("p (k d) -> p k d", k=2)), reads=[pv], writes=[vaug])
                        if not is_s:
                            sq_i, r0 = (blk - NBS) // 2, ((blk - NBS) % 2) * 128
                            vs = kvst.next()
                            cx.op("act", lambda v: v.copy(out=vs[:, :], in_=pv[:, 0:128]), reads=[pv], writes=[vs])
                            cx.dma("sp", nv_o, nv_o[sq_i, e, r0:r0 + 128, :], vs, vs[:, :], stm(f"kvo{cn['kv'] % 4}"))
                            cn["kv"] += 1
                            pk = psum.next()
                            for kc in range(KC):
                                cx.op("pe", lambda t_, kc=kc: t_.matmul(
                                    pk[:, 0:128], lhsT=h1[:, kc, b * 128:(b + 1) * 128],
                                    rhs=wea[:, kc, C_K:C_K + 128], start=(kc == 0), stop=(kc == KC - 1)),
                                    reads=[wea, h1], writes=[pk], inc=(kc == KC - 1))
                            ks = kvst.next()
                            cx.op("dve", lambda v: v.tensor_copy(out=ks[:, :], in_=pk[:, 0:128]), reads=[pk], writes=[ks])
                            cx.dma("sp", nk_o, nk_o[sq_i, e, r0:r0 + 128, :], ks, ks[:, :], stm(f"kvo{cn['kv'] % 4}"))
                            cn["kv"] += 1
                xk_both = cx.sb("xk_both", [64, 2, 256], BF16)
                xk_tmp = cx.sb("xk_tmp", [64, 256], BF16)
                xk_src = cx.sb("xk_src", [64, 2, 128], BF16)
                cx.op("dve", lambda v: v.tensor_copy(out=xk_src[:, :, :], in_=kT[:, :, TS - 128:TS]), reads=[kT],
                      writes=[xk_src])
                xk_oth = cx.sb("xk_oth", [64, 2, 128], BF16)
                exchange(xk_src, xk_src[:, :, :].rearrange("p a b -> p (a b)"), 64, 256, BF16, xk_both, xk_oth,
                         xk_oth[:, :, :].rearrange("p a b -> p (a b)"), xk_tmp)
                cx.op("dve", lambda v: v.tensor_copy(out=kT[:, :, NT:NT + 128], in_=xk_oth[:, :, :]), reads=[xk_oth],
                      writes=[kT])
                xv_both = cx.sb("xv_both", [128, 2, 130], BF16)
                xv_tmp = cx.sb("xv_tmp", [128, 130], BF16)
                xv_src = cx.sb("xv_src", [128, 130], BF16)
                xv_oth = cx.sb("xv_oth", [128, 130], BF16)
                cx.op("dve", lambda v: v.tensor_copy(out=xv_src[:, :], in_=vaug[:, NBS - 1, :]), reads=[vaug],
                      writes=[xv_src])
                exchange(xv_src, xv_src[:, :], 128, 130, BF16, xv_both, xv_oth, xv_oth[:, :], xv_tmp)
                cx.op("dve", lambda v: v.tensor_copy(out=vaug[:, NBLK, :], in_=xv_oth[:, :]), reads=[xv_oth],
                      writes=[vaug])
                drain()
                qb_ring = Ring([cx.sb(f"qb{i}", [64, 8, 128], BF16) for i in range(2)])
                pT_ring = Ring([cx.sb(f"pT{i}", [128, 512], BF16) for i in range(4)])
                otok_ring = Ring([cx.sb(f"otok{i}", [128, 512], BF16) for i in range(2)])
                oaT_ring = Ring([cx.sb(f"oaT{i}", [128, 4, 128], BF16) for i in range(2)])
                den = cx.sb("den", [128, 8], F32)
                rden = cx.sb("rden", [128, 8, 1], F32)
                zd = Tk(None, "zd3")
                for blk in range(NBLK):
                    qb = qb_ring.next()
                    cx.dma("sp", qb, qb[:, :, :], qd, q_d[:, :, blk * 128:(blk + 1) * 128], stm(f"qb{cn['qb'] % 2}"))
                    cn["qb"] += 1
                    if blk < NBS:
                        keys = [("c", 0, None), ("c", 1, None)]
                        if blk > 0:
                            keys.append(("w", blk - 1, 0))
                        keys.append(("w", blk, None))
                        if blk < NBS - 1:
                            keys.append(("w", blk + 1, 1))
                        elif os.environ.get("KDBG_NOHALO", "0") == "0":
                            keys.append(("h", NBLK, 2))
                    else:
                        b0 = NBS + ((blk - NBS) // 2) * 2
                        keys = [("w", b0, None), ("w", b0 + 1, None)]
                    def scores(kvh, kind, kb, mk):
                        ps_ = psum.next()
                        kt_t = kc_sb if kind == "c" else kT
                        if kind == "c":
                            kap = kc_sb[:, kvh, kb * 128:(kb + 1) * 128]
                        elif kind == "h":
                            kap = kT[:, kvh, NT:NT + 128]
                        else:
                            kap = kT[:, kvh, kb * 128:(kb + 1) * 128]
                        for g in range(4):
                            h = kvh * 4 + g
                            cx.op("pe", lambda t_, g=g, h=h: t_.matmul(
                                ps_[:, g * 128:(g + 1) * 128], lhsT=kap, rhs=qb[:, h, :],
                                start=(g == 0), stop=(g == 3 and mk is None), skip_group_check=True),
                                reads=[kt_t, qb], writes=[ps_], inc=(g == 3 and mk is None))
                        if mk is not None:
                            cx.op("pe", lambda t_: t_.matmul(ps_[:, :], lhsT=ident_b[:, :], rhs=msk[:, mk, :],
                                                             start=False, stop=True, skip_group_check=True),
                                  reads=[ident_b, msk], writes=[ps_])
                        pT = pT_ring.next()
                        cx.op("act", lambda a: a.activation(out=pT[:, :], in_=ps_[:, :], func=AF.Exp, scale=0.125),
                              reads=[ps_], writes=[pT])
                        return pT

                    work = [(kvh, ki, kk) for kvh in range(2) for ki, kk in enumerate(keys)]
                    pend = scores(work[0][0], *work[0][2])
                    for wi, (kvh, ki, (kind, kb, mk)) in enumerate(work):
                        pT = pend
                        if wi + 1 < len(work):
                            pend = scores(work[wi + 1][0], *work[wi + 1][2])
                        ob = accb[kvh]
                        v_t = vc_sb if kind == "c" else vaug
                        for g in range(4):
                            vap = (vc_sb[:, kb, kvh * 65:(kvh + 1) * 65] if kind == "c"
                                   else vaug[:, kb, kvh * 65:(kvh + 1) * 65])
                            last = (ki == len(keys) - 1)
                            cx.op("pe", lambda t_, g=g, vap=vap: t_.matmul(
                                ob[:, g * 65:(g + 1) * 65], lhsT=pT[:, g * 128:(g + 1) * 128], rhs=vap,
                                start=(ki == 0 and g == 0), stop=last, skip_group_check=True),
                                reads=[pT, v_t], writes=[ob], inc=(g == 3))
                    otok = otok_ring.next()
                    for kvh in range(2):
                        ob = accb[kvh]
                        o3 = ob[:, 0:260].rearrange("p (g d) -> p g d", g=4)
                        cx.op("dve", lambda v, kvh=kvh, o3=o3: v.tensor_tensor(
                            out=den[:, kvh * 4:(kvh + 1) * 4], in0=o3[:, :, 64], in1=esink[:, kvh * 4:(kvh + 1) * 4],
                            op=ALU.add), reads=[ob, esink], writes=[den])
                        cx.op("dve", lambda v, kvh=kvh: v.reciprocal(
                            out=rden[:, kvh * 4:(kvh + 1) * 4, :].rearrange("p a b -> p (a b)"),
                            in_=den[:, kvh * 4:(kvh + 1) * 4]), reads=[den], writes=[rden])
                        cx.op("dve", lambda v, kvh=kvh, o3=o3: v.tensor_tensor(
                            out=otok[:, kvh * 256:(kvh + 1) * 256].rearrange("p (g d) -> p g d", g=4),
                            in0=o3[:, :, 0:64], in1=rden[:, kvh * 4:(kvh + 1) * 4, :].to_broadcast([128, 4, 64]),
                            op=ALU.mult), reads=[ob, rden], writes=[otok])
                    ptr = psum.next()
                    ptb = ptr.ap[:, :].bitcast(BF16)
                    for cc in range(4):
                        cx.op("pe", lambda t_, cc=cc: t_.transpose(
                            ptb[:, cc * 128:(cc + 1) * 128], otok[:, cc * 128:(cc + 1) * 128], ident_b[:, :]),
                            reads=[otok, ident_b], writes=[ptr], inc=(cc == 3))
                    oaT = oaT_ring.next()
                    cx.op("act", lambda a: a.copy(out=oaT[:, :, :].rearrange("p a b -> p (a b)"), in_=ptb[:, 0:512]),
                          reads=[ptr], writes=[oaT])
                    cx.dma("sp", zd, z_d[:, 0:4, blk * 128:(blk + 1) * 128], oaT, oaT[:, :, :], stm(f"oast{cn['oa'] % 2}"))
                    cn["oa"] += 1
                    zd.w, zd.r, qd.r = [], [], []

        def even_gla(l):
            e = l // 2
            X = mybir.AxisListType.X
            with Phase():
                load_x, store_x, stats, norm_mod = mk_norm()
                wgl = cx.sb("wgl", [128, KC, NCG], BF16)
                wsrc = w_ev[e].rearrange("(kc p) n -> p kc n", p=128)
                cx.dma("pool", wgl, wgl[:, :, 0:800], w_ev, wsrc[:, :, NCA:NCA + 800], stm("wgl"))
                cx.dma("pool", wgl, wgl[:, :, 800:NCG], w_ev, wsrc[:, :, NCA + 800:NCE], stm("wgl"), partial=True)
                wal = cx.sb("wal", [64, 256], F32)
                gv = cx.sb("gv", [64, 8 + 512], F32)
                cx.dma("sp", wal, wal[:, :], walp, walp[e], stm("wal"))
                cx.dma("sp", gv, gv[:, :], glav, glav[e], stm("gvl"))
                nbal = cx.sb("nbal", [64, 8], F32)
                cx.op("dve", lambda v: v.tensor_scalar(out=nbal[:, :], in0=gv[:, 0:8], scalar1=-1.0, scalar2=None,
                                                       op0=ALU.mult), reads=[gv], writes=[nbal])
                gm = cx.sb("gm", [64, 2, 512], BF16)
                cx.dma("pool", gm, gm[:, :, :], gmask, gmask.ap.rearrange("d p n -> p d n"), stm("gml"))
                rm = cx.sb("rm", [64, 2, 4 * TILE], BF16)
                cx.op("dve", lambda v: v.memset(rm[:, :, :], 1.0), writes=[rm])
                cx.op("dve", lambda v: v.memset(rm[:, 0, 0:4 * TILE:64], 0.0), writes=[rm])
                cx.op("dve", lambda v: v.memset(rm[:, 1, 63:4 * TILE:64], 0.0), writes=[rm])
                lrT = cx.sb("lrT", [64, TILE], F32)
                e1 = cx.sb("e1", [64, TILE], F32)
                Lall = cx.sb("Lall", [64, 4, TILE], F32)
                Lc = cx.sb("Lc", [64, 4, TILE], F32)
                Eq = cx.sb("Eq", [64, 4, TILE], F32)
                Ek = cx.sb("Ek", [64, 4, TILE], F32)
                q_e = cx.sb("q_e", [64, 4, TILE], BF16)
                k_e = cx.sb("k_e", [64, 4, TILE], BF16)
                vt = cx.sb("vt", [64, 8, 512], BF16)
                sr = cx.sb("sr", [64, 8, 512], BF16)
                keT = cx.sb("keT", [64, 8, 256], BF16)
                atm_r = Ring([cx.sb(f"atm{i}", [64, 512], BF16) for i in range(2)])
                T_r = Ring([cx.sb(f"Tt{i}", [64, 4, 128], F32) for i in range(2)])
                U_r = Ring([cx.sb(f"Ut{i}", [64, 4, 128], F32) for i in range(2)])
                Ub_r = Ring([cx.sb(f"Ub{i}", [64, 4, 128], BF16) for i in range(2)])
                ofs_r = Ring([cx.sb(f"ofs{i}", [64, 512], F32) for i in range(2)])
                o_r = Ring([cx.sb(f"ot{i}", [64, 4, 128], F32) for i in range(2)])
                osq = cx.sb("osq", [64, 4, 128], F32)
                ss = cx.sb("ss", [64, 4], F32)
                rs = cx.sb("rs", [64, 4, 1], F32)
                gs = cx.sb("gs", [64, 512], F32)
                ob_r = Ring([cx.sb(f"obk{i}", [64, 512], BF16) for i in range(2)])
                obT_r = Ring([cx.sb(f"obT{i}", [128, 4, 64], BF16) for i in range(2)])
                ofd, zd = Tk(None, "ofd"), Tk(None, "zd5")
                cn = {"of": 0, "ol": 0, "z": 0}
                state = {}

                def tile_chunks(t):
                    if t * TILE < TS:
                        return [(ci, 0, t * 8 + ci, TS // 64) for ci in range(8)]
                    s0 = 1 + (t - TS // TILE) * 2
                    return [(ci, s0 + ci // 4, ci % 4, 4) for ci in range(8)]

                def sweep_tile(t, d, h1):
                    pl = psum.next()
                    for kc in range(KC):
                        cx.op("pe", lambda t_, kc=kc: t_.matmul(
                            pl[0:64, :], lhsT=wgl[:, kc, G_LR:G_LR + 64], rhs=h1[:, kc, :],
                            start=(kc == 0), stop=(kc == KC - 1)), reads=[wgl, h1], writes=[pl], inc=(kc == KC - 1))
                    cx.op("act", lambda a: a.copy(out=lrT[:, :], in_=pl[0:64, :]), reads=[pl], writes=[lrT])
                    for h in range(4):
                        yl = psum.next()
                        cx.op("pe", lambda t_, h=h: t_.matmul(
                            yl[0:64, :], lhsT=wal[32 * d:32 * d + 16, h * 64:(h + 1) * 64],
                            rhs=lrT[32 * d:32 * d + 16, :], start=True, stop=True), reads=[wal, lrT], writes=[yl])
                        cx.op("act", lambda a, h=h: a.activation(out=e1[:, :], in_=yl[0:64, :], func=AF.Exp, scale=-1.0,
                                                                 bias=nbal[:, d * 4 + h:d * 4 + h + 1]),
                              reads=[yl, nbal], writes=[e1])
                        cx.op("act", lambda a, h=h: a.activation(out=Lall[:, h, :], in_=e1[:, :], func=AF.Ln, bias=1.0,
                                                                 scale=1.0), reads=[e1], writes=[Lall])
                    Lf = Lall[:, :, :].rearrange("p a b -> p (a b)")
                    Lcf = Lc[:, :, :].rearrange("p a b -> p (a b)")
                    if d == 0:
                        cx.op("dve", lambda v: v.tensor_tensor_scan(out=Lcf, data0=rm[:, 0, :], data1=Lf, initial=0.0,
                                                                    op0=ALU.mult, op1=ALU.add),
                              reads=[rm, Lall], writes=[Lc])
                    else:
                        cx.op("dve", lambda v: v.tensor_tensor_scan(out=Lcf[:, ::-1], data0=rm[:, 1, :][:, ::-1],
                                                                    data1=Lf[:, ::-1], initial=0.0,
                                                                    op0=ALU.mult, op1=ALU.add),
                              reads=[rm, Lall], writes=[Lc])
                    cx.op("act", lambda a: a.activation(out=Eq[:, :, :].rearrange("p a b -> p (a b)"), in_=Lcf,
                                                        func=AF.Exp, scale=-1.0 / 16), reads=[Lc], writes=[Eq])
                    cx.op("act", lambda a: a.activation(out=Ek[:, :, :].rearrange("p a b -> p (a b)"), in_=Lcf,
                                                        func=AF.Exp, scale=1.0 / 16), reads=[Lc], writes=[Ek])
                    for h in range(4):
                        pq = psum.next()
                        for kc in range(KC):
                            cx.op("pe", lambda t_, kc=kc, h=h: t_.matmul(
                                pq[0:64, :], lhsT=wgl[:, kc, G_QB + h * 64:G_QB + (h + 1) * 64], rhs=h1[:, kc, :],
                                start=(kc == 0), stop=(kc == KC - 1)), reads=[wgl, h1], writes=[pq], inc=(kc == KC - 1))
                        cx.op("dve", lambda v, h=h: v.scalar_tensor_tensor(
                            out=q_e[:, h, :], in0=pq[0:64, :], scalar=0.125, in1=Eq[:, h, :], op0=ALU.mult,
                            op1=ALU.mult), reads=[pq, Eq], writes=[q_e])
                        pk = psum.next()
                        for kc in range(KC):
                            cx.op("pe", lambda t_, kc=kc, h=h: t_.matmul(
                                pk[0:64, :], lhsT=wgl[:, kc, G_KB + h * 64:G_KB + (h + 1) * 64], rhs=h1[:, kc, :],
                                start=(kc == 0), stop=(kc == KC - 1)), reads=[wgl, h1], writes=[pk], inc=(kc == KC - 1))
                        cx.op("dve", lambda v, h=h: v.tensor_tensor(
                            out=k_e[:, h, :], in0=pk[0:64, :], in1=Ek[:, h, :], op=ALU.mult),
                            reads=[pk, Ek], writes=[k_e])
                    for ci in range(8):
                        c0 = ci * 64
                        pv = psum.next()
                        for kc in range(KC):
                            cx.op("pe", lambda t_, kc=kc: t_.matmul(
                                pv[0:64, :], lhsT=h1[:, kc, c0:c0 + 64], rhs=wgl[:, kc, G_VB:G_VB + 512],
                                start=(kc == 0), stop=(kc == KC - 1)), reads=[wgl, h1], writes=[pv], inc=(kc == KC - 1))
                        cx.op("act", lambda a: a.copy(out=vt[:, ci, :], in_=pv[0:64, :]), reads=[pv], writes=[vt])
                        if d == 1:
                            pr = psum.next()
                            for kc in range(KC):
                                cx.op("pe", lambda t_, kc=kc: t_.matmul(
                                    pr[0:64, :], lhsT=h1[:, kc, c0:c0 + 64], rhs=wgl[:, kc, G_RB:G_RB + 512],
                                    start=(kc == 0), stop=(kc == KC - 1)), reads=[wgl, h1], writes=[pr],
                                    inc=(kc == KC - 1))
                            cx.op("act", lambda a: a.activation(out=sr[:, ci, :], in_=pr[0:64, :], func=AF.Silu),
                                  reads=[pr], writes=[sr])
                        ptk = psum.next()
                        ptkb = ptk.ap[:, :].bitcast(BF16)
                        for h in range(4):
                            cx.op("pe", lambda t_, h=h: t_.transpose(
                                ptkb[0:64, h * 64:(h + 1) * 64], k_e[:, h, c0:c0 + 64], ident_b[0:64, 0:64]),
                                reads=[k_e, ident_b], writes=[ptk], inc=(h == 3))
                        cx.op("act", lambda a: a.copy(out=keT[:, ci, :], in_=ptkb[0:64, 0:256]), reads=[ptk], writes=[keT])
                    order = tile_chunks(t)
                    if d == 1:
                        order = order[::-1]
                    atm = None
                    for (ci, sq_i, pos, nch) in order:
                        c0 = ci * 64
                        cp = ci % 2
                        first = (pos == 0) if d == 0 else (pos == nch - 1)
                        last = (pos == nch - 1) if d == 0 else (pos == 0)
                        need_at = (cp == 0) if d == 0 else (cp == 1)
                        if need_at:
                            pat = psum.next()
                            cb = ci - cp
                            for cq in range(2):
                                cc0 = (cb + cq) * 64
                                for h in range(4):
                                    cx.op("pe", lambda t_, h=h, cq=cq, cc0=cc0: t_.matmul(
                                        pat[0:64, cq * 256 + h * 64:cq * 256 + (h + 1) * 64],
                                        lhsT=k_e[:, h, cc0:cc0 + 64], rhs=q_e[:, h, cc0:cc0 + 64],
                                        start=(cq == 0 and h == 0), stop=(cq == 1 and h == 3), skip_group_check=True),
                                        reads=[k_e, q_e], writes=[pat], inc=(cq == 1 and h == 3))
                            atm = atm_r.next()
                            cx.op("dve", lambda v: v.tensor_tensor(out=atm[:, :], in0=pat[0:64, :], in1=gm[:, d, :],
                                                                   op=ALU.mult), reads=[pat, gm], writes=[atm])
                        if first:
                            U = U_r.next()
                            Ub = Ub_r.next()
                            if sq_i == 0 and d == 0:
                                cx.dma("sp", U, U[:, :, :], sgl, sgl[e, d].rearrange("h k v -> k h v"), stm("sgl"))
                            elif sq_i == 0:
                                cx.op("dve", lambda v: v.tensor_copy(out=U[:, :, :], in_=Uoth[:, :, :]), reads=[Uoth],
                                      writes=[U])
                            else:
                                cx.op("dve", lambda v: v.memset(U[:, :, :], 0.0), writes=[U])
                            cx.op("act", lambda a: a.copy(out=Ub[:, :, :], in_=U[:, :, :]), reads=[U], writes=[Ub])
                            state["U"], state["Ub"] = U, Ub
                        U, Ub = state["U"], state["Ub"]
                        pkv = psum.next()
                        for h in range(4):
                            cx.op("pe", lambda t_, h=h: t_.matmul(
                                pkv[0:64, h * 128:(h + 1) * 128], lhsT=keT[:, ci, h * 64:(h + 1) * 64],
                                rhs=vt[:, ci, h * 128:(h + 1) * 128], start=(h == 0), stop=(h == 3),
                                skip_group_check=True), reads=[keT, vt], writes=[pkv], inc=(h == 3))
                        po = psum.next()
                        for h in range(4):
                            cx.op("pe", lambda t_, h=h: t_.matmul(
                                po[0:64, h * 128:(h + 1) * 128], lhsT=atm[:, cp * 256 + h * 64:cp * 256 + (h + 1) * 64],
                                rhs=vt[:, ci, h * 128:(h + 1) * 128], start=(h == 0), stop=False,
                                skip_group_check=True), reads=[atm, vt], writes=[po], inc=False)
                            cx.op("pe", lambda t_, h=h: t_.matmul(
                                po[0:64, h * 128:(h + 1) * 128], lhsT=q_e[:, h, c0:c0 + 64], rhs=Ub[:, h, :],
                                start=False, stop=(h == 3), skip_group_check=True),
                                reads=[q_e, Ub], writes=[po], inc=(h == 3))
                        Tt = T_r.next()
                        cx.op("dve", lambda v: v.tensor_tensor(out=Tt[:, :, :].rearrange("p a b -> p (a b)"),
                                                               in0=pkv[0:64, :],
                                                               in1=U[:, :, :].rearrange("p a b -> p (a b)"), op=ALU.add),
                              reads=[pkv, U], writes=[Tt])
                        U2, Ub2 = U_r.next(), Ub_r.next()
                        dcol = c0 + 63 if d == 0 else c0
                        cx.op("dve", lambda v: v.tensor_tensor(
                            out=U2[:, :, :], in0=Tt[:, :, :], in1=Eq[:, :, dcol:dcol + 1].to_broadcast([64, 4, 128]),
                            op=ALU.mult), reads=[Tt, Eq], writes=[U2])
                        cx.op("act", lambda a: a.copy(out=Ub2[:, :, :], in_=U2[:, :, :]), reads=[U2], writes=[Ub2])
                        state["U"], state["Ub"] = U2, Ub2
                        if last and sq_i > 0:
                            cx.dma("sp", gla_o, gla_o[sq_i - 1, e, d].rearrange("h k v -> k h v"), U2, U2[:, :, :],
                                   stm("glao"))
                        if last and sq_i == 0 and d == 0:
                            cx.op("dve", lambda v: v.tensor_copy(out=Ufin[:, :, :], in_=U2[:, :, :]), reads=[U2],
                                  writes=[Ufin])
                        gch = (t * TILE + c0) // 64
                        if d == 0:
                            ofs = ofs_r.next()
                            cx.op("act", lambda a: a.copy(out=ofs[:, :], in_=po[0:64, :]), reads=[po], writes=[ofs])
                            cx.dma("sp", ofd, of_d[gch], ofs, ofs[:, :], stm(f"ofst{cn['of'] % 2}"))
                            cn["of"] += 1
                            ofd.w, ofd.r = [], []
                        else:
                            ofs = ofs_r.next()
                            cx.dma("sp", ofs, ofs[:, :], ofd, of_d[gch], stm(f"ofl{cn['ol'] % 2}"))
                            cn["ol"] += 1
                            ofd.r = []
                            o = o_r.next()
                            of_ = o[:, :, :].rearrange("p a b -> p (a b)")
                            cx.op("dve", lambda v: v.tensor_tensor(out=of_, in0=po[0:64, :], in1=ofs[:, :], op=ALU.add),
                                  reads=[po, ofs], writes=[o])
                            cx.op("dve", lambda v: v.tensor_tensor(out=osq[:, :, :], in0=o[:, :, :], in1=o[:, :, :],
                                                                   op=ALU.mult), reads=[o], writes=[osq])
                            cx.op("dve", lambda v: v.tensor_reduce(out=ss[:, :], in_=osq[:, :, :], axis=X, op=ALU.add),
                                  reads=[osq], writes=[ss])
                            rs2 = rs[:, :, :].rearrange("p a b -> p (a b)")
                            cx.op("act", lambda a: a.activation(out=rs2, in_=ss[:, :], func=AF.Ln, scale=1.0 / 128,
                                                                bias=epsb[0:64, 0:1]), reads=[ss, epsb], writes=[rs])
                            cx.op("act", lambda a: a.activation(out=rs2, in_=rs2, func=AF.Exp, scale=-0.5),
                                  reads=[rs], writes=[rs])
                            cx.op("dve", lambda v: v.tensor_tensor(
                                out=o[:, :, :], in0=o[:, :, :], in1=rs[:, :, :].to_broadcast([64, 4, 128]),
                                op=ALU.mult), reads=[o, rs], writes=[o])
                            cx.op("dve", lambda v: v.tensor_tensor(out=gs[:, :], in0=gv[:, 8:520], in1=sr[:, ci, :],
                                                                   op=ALU.mult), reads=[gv, sr], writes=[gs])
                            obk = ob_r.next()
                            cx.op("dve", lambda v: v.tensor_tensor(out=obk[:, :], in0=of_, in1=gs[:, :], op=ALU.mult),
                                  reads=[o, gs], writes=[obk])
                            ptr = psum.next()
                            ptb = ptr.ap[:, :].bitcast(BF16)
                            for cc in range(4):
                                cx.op("pe", lambda t_, cc=cc: t_.transpose(
                                    ptb[:, cc * 64:(cc + 1) * 64], obk[:, cc * 128:(cc + 1) * 128], ident_b[0:64, 0:64]),
                                    reads=[obk, ident_b], writes=[ptr], inc=(cc == 3))
                            obT = obT_r.next()
                            cx.op("act", lambda a: a.copy(out=obT[:, :, :].rearrange("p a b -> p (a b)"),
                                                          in_=ptb[:, 0:256]), reads=[ptr], writes=[obT])
                            tk0 = t * TILE + c0
                            cx.dma("sp", zd, z_d[:, 4:8, tk0:tk0 + 64], obT, obT[:, :, :], stm(f"gzst{cn['z'] % 2}"))
                            cn["z"] += 1
                            zd.w, zd.r = [], []

                Ufin = cx.sb("Ufin", [64, 4, 128], F32)
                Uoth = cx.sb("Uoth", [64, 4, 128], F32)
                xu_both = cx.sb("xu_both", [64, 2, 512], F32)
                xu_tmp = cx.sb("xu_tmp", [64, 512], F32)
                def hnorm(t):
                    return norm_mod(load_x(t), mod[l]["A1"], mod[l]["B1"], tile_j(t))

                order0 = list(range(NTILE))
                order1 = list(range(TS // TILE - 1, -1, -1)) + list(range(TS // TILE, NTILE))
                nxt_h = hnorm(order0[0])
                for i_, t in enumerate(order0):
                    h1 = nxt_h
                    nxt_h = hnorm(order0[i_ + 1]) if i_ + 1 < NTILE else hnorm(order1[0])
                    sweep_tile(t, 0, h1)
                exchange(Ufin, Ufin[:, :, :].rearrange("p a b -> p (a b)"), 64, 512, F32, xu_both, Uoth,
                         Uoth[:, :, :].rearrange("p a b -> p (a b)"), xu_tmp)
                drain()
                for i_, t in enumerate(order1):
                    h1 = nxt_h
                    if i_ + 1 < NTILE:
                        nxt_h = hnorm(order1[i_ + 1])
                    sweep_tile(t, 1, h1)

        def even_zero_gla(l):
            with Phase():
                zz = cx.sb("zz", [128, 4, TILE], BF16)
                cx.op("pool", lambda g_: g_.memset(zz[:, :, :], 0.0), writes=[zz])
                zd = Tk(None, "zd4")
                for t in range(NTILE):
                    cx.dma("sp", zd, z_d[:, 4:8, t * TILE:(t + 1) * TILE], zz, zz[:, :, :], stm("zzst"))
                    zd.w, zd.r = [], []

        def ffn_phase(l, has_mix):
            with Phase():
                load_x, store_x, stats, norm_mod = mk_norm()
                act_t = cx.sb("act_t", [128, FC, TILE], BF16)
                sg_ring = Ring([cx.sb(f"sg{i}", [128, TILE], BF16) for i in range(2)])
                win_ring = Ring([cx.sb(f"win{i}", [128, KC, 512], BF16) for i in range(3)])
                wo_ring = Ring([cx.sb(f"wo{i}", [128, FC, 128], BF16) for i in range(3)])
                cnt = {"win": 0, "wo": 0, "z": 0}
                md = mod[l]
                if has_mix:
                    wmo = cx.sb("wmo", [128, KC, D], BF16)
                    wsrc_t = w_out_odd if l % 2 == 1 else w_out_even
                    cx.dma("pool", wmo, wmo[:, :, :], wsrc_t, wsrc_t[l // 2].rearrange("(kc p) n -> p kc n", p=128),
                           stm("wmo"))
                    z_ring = Ring([cx.sb(f"zr{i}", [128, KC, TILE], BF16) for i in range(2)])
                    zd = Tk(None, "zd2")
                si, so = s_in[l % 2], s_out[l % 2]

                def stage_a(t):
                    j = tile_j(t)
                    xt = load_x(t)
                    if has_mix:
                        zr = z_ring.next()
                        cx.dma("sp", zr, zr[:, :, :], zd, z_d[:, :, t * TILE:(t + 1) * TILE], stm(f"zl{cnt['z'] % 2}"))
                        cnt["z"] += 1
                        for oc in range(KC):
                            po = psum.next()
                            for kc in range(KC):
                                cx.op("pe", lambda t_, kc=kc, oc=oc, po=po: t_.matmul(
                                    po[:, :], lhsT=wmo[:, kc, oc * 128:(oc + 1) * 128], rhs=zr[:, kc, :],
                                    start=(kc == 0), stop=(kc == KC - 1)), reads=[wmo, zr], writes=[po],
                                    inc=(kc == KC - 1))
                            cx.op("dve", lambda v, oc=oc, po=po: v.scalar_tensor_tensor(
                                out=xt[:, oc, :], in0=po[:, :], scalar=md["G1"][:, oc, j:j + 1], in1=xt[:, oc, :],
                                op0=ALU.mult, op1=ALU.add), reads=[po, md["G1"], xt], writes=[xt])
                    h2 = norm_mod(xt, md["A2"], md["B2"], j)
                    return xt, h2

                nxt = stage_a(0)
                for t in range(NTILE):
                    j = tile_j(t)
                    xt, h2 = nxt
                    for q in range(11):
                        wt = win_ring.next()
                        cx.dma("pool", wt, wt[:, :, :], si[q], si[q][:, :, :], stm(f"win{cnt['win'] % 3}"))
                        cnt["win"] += 1
                        for jj in range(2):
                            fc = q * 2 + jj
                            pg = psum.next()
                            for kc in range(KC):
                                cx.op("pe", lambda t_, kc=kc, jj=jj, pg=pg, wt=wt: t_.matmul(
                                    pg[:, :], lhsT=wt[:, kc, jj * 128:(jj + 1) * 128], rhs=h2[:, kc, :],
                                    start=(kc == 0), stop=(kc == KC - 1)),
                                    reads=[wt, h2], writes=[pg], inc=(kc == KC - 1))
                            pu = psum.next()
                            for kc in range(KC):
                                cx.op("pe", lambda t_, kc=kc, jj=jj, pu=pu, wt=wt: t_.matmul(
                                    pu[:, :], lhsT=wt[:, kc, 256 + jj * 128:256 + (jj + 1) * 128], rhs=h2[:, kc, :],
                                    start=(kc == 0), stop=(kc == KC - 1)),
                                    reads=[wt, h2], writes=[pu], inc=(kc == KC - 1))
                            sg = sg_ring.next()
                            cx.op("act", lambda a, pg=pg, sg=sg: a.activation(out=sg[:, :], in_=pg[:, :], func=AF.Silu),
                                  reads=[pg], writes=[sg])
                            cx.op("dve", lambda v, fc=fc, pu=pu, sg=sg: v.tensor_tensor(
                                out=act_t[:, fc, :], in0=pu[:, :], in1=sg[:, :], op=ALU.mult),
                                reads=[pu, sg], writes=[act_t])
                    if t + 1 < NTILE:
                        nxt = stage_a(t + 1)
                    for oc in range(KC):
                        wo = wo_ring.next()
                        cx.dma("pool", wo, wo[:, :, :], so[oc], so[oc][:, :, :], stm(f"wo{cnt['wo'] % 3}"))
                        cnt["wo"] += 1
                        po = psum.next()
                        for fc in range(FC):
                            cx.op("pe", lambda t_, fc=fc, po=po, wo=wo: t_.matmul(
                                po[:, :], lhsT=wo[:, fc, :], rhs=act_t[:, fc, :], start=(fc == 0), stop=(fc == FC - 1)),
                                reads=[wo, act_t], writes=[po], inc=(fc == FC - 1))
                        cx.op("dve", lambda v, oc=oc, po=po: v.scalar_tensor_tensor(
                            out=xt[:, oc, :], in0=po[:, :], scalar=md["G2"][:, oc, j:j + 1], in1=xt[:, oc, :],
                            op0=ALU.mult, op1=ALU.add), reads=[po, md["G2"], xt], writes=[xt])
                    store_x(t, xt)

        def final_phase():
            with Phase():
                load_x, store_x, stats, norm_mod = mk_norm()
                yfm_r = Ring([cx.sb(f"yfm{i}", [128, KC, TILE], F32) for i in range(2)])
                for t in range(NTILE):
                    xt = load_x(t)
                    rstd = stats(xt)
                    yfm = yfm_r.next()
                    for c in range(KC):
                        cx.op("dve", lambda v, c=c: v.scalar_tensor_tensor(
                            out=yfm[:, c, :], in0=xt[:, c, :], scalar=nfin_sb[:, c:c + 1], in1=rstd[:, :],
                            op0=ALU.mult, op1=ALU.mult), reads=[xt, nfin_sb, rstd], writes=[yfm])
                    cx.dma("sp", y_out, y_out[:, :, t * TILE:(t + 1) * TILE], yfm, yfm[:, :, :], stm(f"yst{t % 2}"))

        for l in range(DBG_DEPTH):
            prep_ffn_weights(l)
            do_mix = (DBG_MIX == "all") or (DBG_MIX == "odd" and l % 2 == 1) or (DBG_MIX == "even" and l % 2 == 0)
            if do_mix and l % 2 == 1:
                odd_phase1(l)
                odd_phase2(l)
            if do_mix and l % 2 == 0:
                even_attn(l)
                if DBG_GLA:
                    even_gla(l)
                else:
                    even_zero_gla(l)
            ffn_phase(l, do_mix)
        final_phase()
        for st in list(sp_st.values()):
            if st["cnt"] > 0:
                nc.sync.wait_ge(st["sem"], st["cnt"])
        cx.es = es
    return nc


_NC = None


def _fm(v):
    v = np.asarray(v, np.float32)
    lead = v.shape[:-1]
    r = v.reshape(*lead, KC, 128)
    return np.ascontiguousarray(np.moveaxis(r, -1, 0))


def kernel(x_prompt, x_sample, c, cache_k, cache_v, state_gla, state_lru, c_ctx,
           w_ada, b_ada, norm_mix, norm_ffn, w_in_even, attn_sink, w_alpha, b_alpha, gla_gain,
           w_out_even, w_in_odd, conv_w, conv_b, w_gate_a, b_gate_a, w_gate_x, b_gate_x,
           lru_lambda, w_out_odd, w_ffn_in, w_ffn_out, norm_final):
    global _NC
    if _NC is None:
        _NC = build_program()
    nc = _NC
    f = lambda a: np.ascontiguousarray(np.asarray(a, np.float32))
    x_prompt, x_sample = f(x_prompt), f(x_sample)
    w_ada, b_ada = f(w_ada), f(b_ada)
    nmix_fm = np.moveaxis(f(norm_mix).reshape(DEPTH, KC, 128), -1, 1)
    nmix2 = np.ascontiguousarray(np.repeat(nmix_fm[..., None], 2, axis=-1))
    nffn_fm = np.moveaxis(f(norm_ffn).reshape(DEPTH, KC, 128), -1, 1)
    nffn2 = np.ascontiguousarray(np.repeat(nffn_fm[..., None], 2, axis=-1))
    nfin = np.ascontiguousarray(f(norm_final).reshape(KC, 128).T)
    shared = {"nmix2": nmix2, "nffn2": nffn2, "nfin": nfin,
              "w_ffn_in": f(w_ffn_in), "w_ffn_out": f(w_ffn_out), "ident_f": np.eye(128, dtype=np.float32),
              "w_in_odd": f(w_in_odd), "w_out_odd": f(w_out_odd), "w_out_even": f(w_out_even)}
    perm = np.concatenate([np.arange(16, 32), np.arange(0, 16), np.arange(48, 64), np.arange(32, 48)])
    q_idx = np.arange(512)
    pq_idx = (np.arange(8)[:, None] * 64 + perm[None, :]).reshape(-1)
    k_idx = 512 + np.arange(128)
    pk_idx = 512 + np.concatenate([perm, 64 + perm])
    v_idx = 640 + np.arange(128)
    qb_idx = 768 + np.arange(256)
    kb_idx = 1024 + np.arange(256)
    vb_idx = 1280 + np.arange(512)
    rb_idx = 1792 + np.arange(512)
    w_in_even = f(w_in_even)
    w_ev_r = []
    for r in range(2):
        lf, lb = (0, 16) if r == 0 else (16, 0)
        lr_idx = 2304 + np.concatenate([lf + np.arange(16), lf + np.arange(16), lb + np.arange(16), lb + np.arange(16)])
        ev_idx = np.concatenate([q_idx, pq_idx, k_idx, pk_idx, v_idx, qb_idx, kb_idx, lr_idx, vb_idx, rb_idx])
        assert ev_idx.size == NCE
        w_ev_r.append(np.ascontiguousarray(w_in_even[:, :, ev_idx]))
    inv = (10000.0 ** (-np.arange(16, dtype=np.float32) / 16)).astype(np.float32)
    sgn = np.concatenate([-np.ones(16), np.ones(16), -np.ones(16), np.ones(16)]).astype(np.float32)
    rope_r = []
    for r in range(2):
        tok = np.arange(TS) if r == 0 else (2 * TS - 1 - np.arange(TS))
        ar = (tok // 64).astype(np.float32)[:, None] * inv
        ac = (tok % 64).astype(np.float32)[:, None] * inv
        ang = np.concatenate([ar, ar, ac, ac], axis=-1)
        cs = np.stack([np.cos(ang), np.sin(ang) * sgn], axis=0).astype(np.float32)
        rope_r.append(np.ascontiguousarray(cs.transpose(2, 0, 1)))
    ia = np.arange(128)
    mp = np.where(ia[:, None] >= ia[None, :], 0.0, NEG).astype(np.float32)
    mn = np.where(ia[:, None] <= ia[None, :], 0.0, NEG).astype(np.float32)
    mh = np.where(ia[:, None] + ia[None, :] >= 127, 0.0, NEG).astype(np.float32)
    shared["maskT"] = np.ascontiguousarray(np.stack([np.tile(mp, (1, 4)), np.tile(mn, (1, 4)), np.tile(mh, (1, 4))], 0))
    shared["sinkb"] = np.ascontiguousarray(np.broadcast_to(f(attn_sink)[:, None, :], (2, 128, 8)))
    w_alpha, b_alpha, gla_gain, state_gla = f(w_alpha), f(b_alpha), f(gla_gain), f(state_gla)
    ic = np.arange(64)
    gf = (ic[None, :] >= ic[:, None]).astype(np.float32)
    gb = (ic[None, :] <= ic[:, None]).astype(np.float32)
    shared["gmask"] = np.ascontiguousarray(np.stack([np.tile(gf, (1, 8)), np.tile(gb, (1, 8))], 0))
    cache_k, cache_v = f(cache_k), f(cache_v)
    conv_w, conv_b, b_gate_a, b_gate_x, lru_lambda, state_lru = (f(a) for a in (
        conv_w, conv_b, b_gate_a, b_gate_x, lru_lambda, state_lru))
    w_gate_a, w_gate_x = f(w_gate_a), f(w_gate_x)
    zero_tap = np.zeros(D, np.float32)
    rank_in = []
    for r in range(2):
        dl = [0, 1] if r == 0 else [1, 0]
        walp = np.zeros((2, 64, 256), np.float32)
        walp[:, 0:16] = w_alpha[:, dl[0]]
        walp[:, 32:48] = w_alpha[:, dl[1]]
        glav = np.zeros((2, 64, 520), np.float32)
        glav[:, :, 0:8] = b_alpha[:, dl].reshape(2, 2, 4, 64).transpose(0, 3, 1, 2).reshape(2, 64, 8)
        glav[:, :, 8:] = gla_gain[:, None, :]
        sel = np.zeros((128, 2), np.float32)
        sel[:, 1 - r] = 1.0
        rank_in.append({"w_ev": w_ev_r[r], "rope_d": rope_r[r], "walp": walp, "glav": glav, "selm": sel,
                        "w_gate_a": np.ascontiguousarray(w_gate_a[:, dl]),
                        "w_gate_x": np.ascontiguousarray(w_gate_x[:, dl])})
    in_maps = []
    for core in range(8):
        b, r = core // 2, core % 2
        dl = [0, 1] if r == 0 else [1, 0]
        if r == 0:
            xs_ = x_sample[b, 0:TS]
            xp_ = x_prompt[4 * b:4 * b + 2].reshape(NPS * PL, D)
        else:
            xs_ = x_sample[b, TS:2 * TS][::-1]
            xp_ = x_prompt[4 * b + 2:4 * b + 4][:, ::-1].reshape(NPS * PL, D)
        m = dict(shared)
        m.update(rank_in[r])
        cs_ = slice(r * 3072, (r + 1) * 3072)
        m["w_ada"] = np.ascontiguousarray(w_ada[:, :, cs_])
        m["b_ada_s"] = np.ascontiguousarray(b_ada[:, cs_].reshape(DEPTH * 24, 128).T[:, :, None])
        cv = np.stack([f(c)[b], f(c_ctx)], axis=-1)
        m["cvt"] = np.ascontiguousarray(np.moveaxis(cv.reshape(KC, 128, 2), 1, 0))
        lv = []
        for o in range(2):
            taps = [conv_w[o, 0], conv_w[o, 1], conv_w[o, 2], conv_w[o, 3], zero_tap]
            if r == 1:
                taps = taps[::-1]
            vecs = taps + [conv_b[o]]
            for d in dl:
                vecs += [b_gate_a[o, d], b_gate_x[o, d], lru_lambda[o, d], state_lru[b, o, d]]
            lv.append(np.stack(vecs, 0).reshape(14, KC, 128).transpose(2, 1, 0))
        m["lruv"] = np.ascontiguousarray(np.stack(lv, 0))
        ck = cache_k[b]
        m["ckT"] = np.ascontiguousarray(ck.transpose(0, 2, 3, 1))
        m["sgl"] = np.ascontiguousarray(state_gla[b][:, dl])
        cva = np.ones((2, 256, 2, 65), np.float32)
        cva[:, :, :, 0:64] = cache_v[b]
        m["cvaug"] = np.ascontiguousarray(cva.reshape(2, 2, 128, 130))
        xtok = np.concatenate([xs_, xp_], axis=0)
        m["xin"] = np.ascontiguousarray(xtok.reshape(NT, KC, 128).transpose(2, 1, 0))
        in_maps.append(m)
    res = run_bass_kernel_spmd(nc, in_maps, core_ids=list(range(8)))
    y_s = np.zeros((4, 2 * TS, D), np.float32)
    y_p = np.zeros((16, PL, D), np.float32)
    n_lru = np.zeros((16, 2, 2, D), np.float32)
    n_g = np.zeros((16, 2, 2, 4, 64, 128), np.float32)
    n_k = np.zeros((16, 2, PL, 2, 64), np.float32)
    n_v = np.zeros((16, 2, PL, 2, 64), np.float32)
    for core in range(8):
        b, r = core // 2, core % 2
        dl = [0, 1] if r == 0 else [1, 0]
        o_ = res.results[core]
        y = o_["y"].transpose(2, 1, 0).reshape(NT, D)
        p0 = 4 * b + 2 * r
        yp = y[TS:].reshape(NPS, PL, D)
        nk = o_["nk_o"].reshape(NPS, 2, PL, 2, 64)
        nv = o_["nv_o"].reshape(NPS, 2, PL, 2, 64)
        lo = o_["lru_o"].transpose(1, 2, 3, 4, 0).reshape(NPS, 2, 2, D)
        if r == 0:
            y_s[b, 0:TS] = y[:TS]
        else:
            y_s[b, TS:2 * TS] = y[:TS][::-1]
            yp, nk, nv = yp[:, ::-1], nk[:, :, ::-1], nv[:, :, ::-1]
        y_p[p0:p0 + 2] = yp
        n_k[p0:p0 + 2] = nk
        n_v[p0:p0 + 2] = nv
        n_g[p0:p0 + 2] = o_["gla_o"][:, :, dl]
        n_lru[p0:p0 + 2] = lo[:, :, dl]
    return (y_p, y_s, n_k, n_v, n_g, n_lru)
```

```python
import os
from contextlib import ExitStack
import numpy as np
import concourse.bass as bass
import concourse.mybir as mybir
from concourse.bass_utils import run_bass_kernel_spmd

F32 = mybir.dt.float32
BF16 = mybir.dt.bfloat16
ALU = mybir.AluOpType
AF = mybir.ActivationFunctionType

D = 1024
KC = 8
DEPTH = 4
TS = 2048
NPS = 2
PL = 256
NT = TS + NPS * PL
TILE = 512
NTILE = NT // TILE
NBLK = NT // 128
DFF = 2816
FC = 22
EPS = 1e-6
P_EVEN = 2336
NBS = TS // 128
C_Q, C_PQ, C_K, C_PK, C_V = 0, 512, 1024, 1152, 1280
NCA = 1408
G_QB, G_KB, G_LR, G_VB, G_RB = 0, 256, 512, 576, 1088
NCG = 1600
NCE = NCA + NCG
NEG = -1e30

DBG_MIX = os.environ.get("KDBG_MIX", "all")
DBG_GLA = os.environ.get("KDBG_GLA", "1") == "1"
DBG_DEPTH = int(os.environ.get("KDBG_DEPTH", str(DEPTH)))


class Tk:
    def __init__(self, ap, name=""):
        self.ap = ap
        self.name = name
        self.psum = False
        self.w = []
        self.r = []

    def __getitem__(self, k):
        return self.ap[k]


class Ctx:
    def __init__(self, nc, es):
        self.nc = nc
        self.es = es
        self.es0 = es
        self.eng = {}
        for nm, obj in (("pe", nc.tensor), ("act", nc.scalar), ("dve", nc.vector),
                        ("pool", nc.gpsimd), ("sp", nc.sync)):
            sem = es.enter_context(nc.semaphore("s_" + nm))
            self.eng[nm] = {"o": obj, "sem": sem, "cnt": 0, "seen": {}, "name": nm}
        self.nsem = 0

    def new_sem(self, name):
        self.nsem += 1
        return self.es0.enter_context(self.nc.semaphore(name))

    def sb(self, name, shape, dt):
        self.nsem += 1
        name = f"{name}_{self.nsem}"
        t = self.es.enter_context(self.nc.sbuf_tensor(name, shape, dt))
        return Tk(t, name)

    def ps(self, name, shape, dt=F32):
        t = self.es.enter_context(self.nc.psum_tensor(name, shape, dt))
        tk = Tk(t, name)
        tk.psum = True
        return tk

    def dram(self, name, shape, dt):
        t = self.nc.dram_tensor(name, shape, dt)
        return Tk(t.ap(), name)

    def _waits(self, e, reads, writes):
        need = {}
        toks = []
        for t in reads:
            toks.extend(t.w)
            if t.psum:
                toks.extend(r for r in t.r if r[2] != e["name"])
        for t in writes:
            toks.extend(t.w)
            toks.extend(t.r)
        for (sem, val, owner) in toks:
            if owner == "pe" and e["name"] == "pe":
                continue
            k = id(sem)
            if e["seen"].get(k, 0) >= val:
                continue
            if k not in need or need[k][1] < val:
                need[k] = (sem, val, owner)
        for k, (sem, val, owner) in need.items():
            if owner in self.eng and val > self.eng[owner]["cnt"] and owner != "dma":
                raise RuntimeError(f"wait on pending token {owner} {val}")
            e["o"].wait_ge(sem, val)
            e["seen"][k] = val

    def op(self, en, fn, reads=(), writes=(), inc=True):
        e = self.eng[en]
        self._waits(e, reads, writes)
        ins = fn(e["o"])
        if inc:
            e["cnt"] += 1
            ins.then_inc(e["sem"], 1)
            tok = (e["sem"], e["cnt"], en)
        else:
            tok = (e["sem"], e["cnt"] + 1, en)
        for t in writes:
            t.w = [tok]
            t.r = []
        for t in reads:
            t.r.append(tok)
        return ins

    def barrier(self):
        for en, e in self.eng.items():
            for on, oe in self.eng.items():
                if on == en or oe["cnt"] == 0:
                    continue
                if e["seen"].get(id(oe["sem"]), 0) < oe["cnt"]:
                    e["o"].wait_ge(oe["sem"], oe["cnt"])
                    e["seen"][id(oe["sem"])] = oe["cnt"]

    def dma(self, en, out_t, out_ap, in_t, in_ap, stream, partial=False):
        e = self.eng[en]
        if partial:
            sv = out_t.w
            out_t.w = []
            self._waits(e, [in_t], [out_t])
            out_t.w = sv
        else:
            self._waits(e, [in_t], [out_t])
        ins = e["o"].dma_start(out=out_ap, in_=in_ap)
        stream["cnt"] += 16
        ins.then_inc(stream["sem"], 16)
        tok = (stream["sem"], stream["cnt"], "dma")
        if partial:
            out_t.w = out_t.w + [tok]
        else:
            out_t.w = [tok]
        out_t.r = []
        in_t.r.append(tok)
        return ins

    def stream(self, name):
        return {"sem": self.new_sem(name), "cnt": 0}

    def collective(self, cin, cout, stream, groups=None):
        e = self.eng["pool"]
        self._waits(e, [cin], [cout])
        ins = self.nc.gpsimd.collective_compute(
            "AllGather", ALU.bypass, replica_groups=groups or [[0, 1], [2, 3], [4, 5], [6, 7]],
            ins=[cin.ap.opt()], outs=[cout.ap.opt()])
        stream["cnt"] += 1
        ins.then_inc(stream["sem"])
        tok = (stream["sem"], stream["cnt"], "dma")
        cout.w = [tok]
        cout.r = []
        cin.r.append(tok)

    def wait_tile(self, en, t):
        e = self.eng[en]
        self._waits(e, [t], [t])


class Ring:
    def __init__(self, tiles):
        self.tiles = tiles
        self.i = 0

    def next(self):
        t = self.tiles[self.i % len(self.tiles)]
        self.i += 1
        return t


def build_program():
    nc = bass.Bass("TRN2", target_bir_lowering=False)

    def din(name, shape, dt=F32):
        return Tk(nc.dram_tensor(name, list(shape), dt, kind="ExternalInput").ap(), name)

    def dout(name, shape, dt=F32):
        return Tk(nc.dram_tensor(name, list(shape), dt, kind="ExternalOutput").ap(), name)

    xin = din("xin", [128, KC, NT])
    cvt = din("cvt", [128, KC, 2])
    w_ada = din("w_ada", [DEPTH, D, 3072])
    b_ada_s = din("b_ada_s", [128, DEPTH * 24, 1])
    nmix2 = din("nmix2", [DEPTH, 128, KC, 2])
    nffn2 = din("nffn2", [DEPTH, 128, KC, 2])
    nfin = din("nfin", [128, KC])
    w_ffn_in = din("w_ffn_in", [DEPTH, D, 2 * DFF])
    w_ffn_out = din("w_ffn_out", [DEPTH, DFF, D])
    ident_f = din("ident_f", [128, 128])
    w_in_odd = din("w_in_odd", [2, D, 2 * D])
    w_out_odd = din("w_out_odd", [2, D, D])
    w_out_even = din("w_out_even", [2, D, D])
    w_gate_a = din("w_gate_a", [2, 2, 4, 256, 256])
    w_gate_x = din("w_gate_x", [2, 2, 4, 256, 256])
    lruv = din("lruv", [2, 128, KC, 14])
    selm = din("selm", [128, 2])
    w_ev = din("w_ev", [2, D, NCE])
    rope_d = din("rope_d", [64, 2, TS])
    ckT = din("ckT", [2, 2, 64, 256])
    walp = din("walp", [2, 64, 256])
    glav = din("glav", [2, 64, 520])
    gmask = din("gmask", [2, 64, 512])
    sgl = din("sgl", [2, 2, 4, 64, 128])
    gla_o = dout("gla_o", [NPS, 2, 2, 4, 64, 128])
    cvaug = din("cvaug", [2, 2, 128, 130])
    sinkb = din("sinkb", [2, 128, 8])
    maskT = din("maskT", [3, 128, 512])
    nk_o = dout("nk_o", [NPS, 2, PL, 128])
    nv_o = dout("nv_o", [NPS, 2, PL, 128])

    y_out = dout("y", [128, KC, NT])

    with ExitStack() as es:
        cx = Ctx(nc, es)
        sp_st = {}

        def stm(name):
            if name not in sp_st:
                sp_st[name] = cx.stream("st_" + name)
            return sp_st[name]

        def drain():
            e = cx.eng["sp"]
            for st in list(sp_st.values()):
                k = id(st["sem"])
                if st["cnt"] > 0 and e["seen"].get(k, 0) < st["cnt"]:
                    nc.sync.wait_ge(st["sem"], st["cnt"])
                    e["seen"][k] = st["cnt"]
            e["cnt"] += 1
            nc.sync.nop().then_inc(e["sem"], 1)
            cx.barrier()

        xs_full = cx.dram("xs", [128, KC, NT], F32)
        xs_t = [Tk(xs_full.ap[:, :, t * TILE:(t + 1) * TILE], f"xs{t}") for t in range(NTILE)]
        s_in_d = [cx.dram(f"s_in{i}", [11, 128, KC, 512], BF16) for i in range(2)]
        s_out_d = [cx.dram(f"s_out{i}", [KC, 128, FC, 128], BF16) for i in range(2)]
        s_in = [[Tk(s_in_d[i].ap[q], f"s_in{i}_{q}") for q in range(11)] for i in range(2)]
        s_out = [[Tk(s_out_d[i].ap[q], f"s_out{i}_{q}") for q in range(KC)] for i in range(2)]

        ident = cx.sb("ident", [128, 128], F32)
        ident_b = cx.sb("ident_b", [128, 128], BF16)
        onesD = cx.sb("onesD", [128, 128], BF16)
        nfin_sb = cx.sb("nfin_sb", [128, KC], F32)
        cx.dma("sp", ident, ident[:, :], ident_f, ident_f[:, :], stm("c0"))
        cx.dma("sp", nfin_sb, nfin_sb[:, :], nfin, nfin[:, :], stm("c1"))
        cx.op("dve", lambda v: v.memset(onesD[:, :], 1.0 / D), writes=[onesD])
        epsb = cx.sb("epsb", [128, 1], F32)
        cx.op("dve", lambda v: v.memset(epsb[:, :], EPS), writes=[epsb])
        cx.op("dve", lambda v: v.tensor_copy(out=ident_b[:, :], in_=ident[:, :]), reads=[ident], writes=[ident_b])

        psum = Ring([cx.ps(f"ps{i}", [128, 512]) for i in range(6)])
        accb = [cx.ps(f"acc{i}", [128, 512]) for i in range(2)]

        mod = []
        for l in range(DEPTH):
            mod.append({k: cx.sb(f"mod_{k}{l}", [128, KC, 2], F32) for k in ("A1", "B1", "G1", "A2", "B2", "G2")})

        with ExitStack() as es2:
            cx.es = es2
            cv = cx.sb("cv", [128, KC, 2], F32)
            cth = cx.sb("cth", [128, KC, 2], F32)
            csl = cx.sb("csl", [128, KC, 2], F32)
            cx.dma("sp", cv, cv[:, :, :], cvt, cvt[:, :, :], stm("c2"))
            cx.op("act", lambda a: a.activation(out=cth[:, :, :], in_=cv[:, :, :], func=AF.Tanh, scale=0.5),
                  reads=[cv], writes=[cth])
            cx.op("dve", lambda v: v.scalar_tensor_tensor(out=csl[:, :, :], in0=cth[:, :, :], scalar=1.0,
                                                          in1=cv[:, :, :], op0=ALU.add, op1=ALU.mult),
                  reads=[cth, cv], writes=[csl])
            cx.op("dve", lambda v: v.tensor_scalar(out=csl[:, :, :], in0=csl[:, :, :], scalar1=0.5, scalar2=None,
                                                   op0=ALU.mult),
                  reads=[csl], writes=[csl])
            wada_ring = Ring([cx.sb(f"wada{i}", [128, KC, 768], F32) for i in range(2)])
            bs = cx.sb("bs", [128, DEPTH * 24, 1], F32)
            cx.dma("sp", bs, bs[:, :, :], b_ada_s, b_ada_s[:, :, :], stm("c3"))
            pa = psum.next()
            wi = 0
            for l in range(DEPTH):
                wsrc = w_ada[l].rearrange("(kc p) n -> p kc n", p=128)
                for g in range(4):
                    wt = wada_ring.next()
                    cx.dma("sp", wt, wt[:, :, :], w_ada, wsrc[:, :, g * 768:(g + 1) * 768], stm(f"wada_s{wi % 2}"))
                    wi += 1
                    for mm in range(6):
                        o2 = (l * 24 + g * 6 + mm) * 2
                        for kc in range(KC):
                            cx.op("pe", lambda t, mm=mm, kc=kc, wt=wt, o2=o2: t.matmul(
                                pa[:, o2:o2 + 2], lhsT=wt[:, kc, mm * 128:(mm + 1) * 128], rhs=csl[:, kc, :],
                                start=(kc == 0), stop=(kc == KC - 1)),
                                reads=[wt, csl], writes=[pa], inc=(kc == KC - 1))
            part = cx.sb("part", [128, DEPTH * 24, 2], F32)
            cx.op("dve", lambda v: v.tensor_tensor(
                out=part[:, :, :], in0=pa[:, 0:DEPTH * 48].rearrange("p (a b) -> p a b", b=2),
                in1=bs[:, :, :].to_broadcast([128, DEPTH * 24, 2]), op=ALU.add), reads=[pa, bs], writes=[part])
            acin = cx.dram("acin", [128, DEPTH * 48], F32)
            acout = cx.dram("acout", [2 * 128, DEPTH * 48], F32)
            cx.dma("sp", acin, acin[:, :], part, part[:, :, :].rearrange("p a b -> p (a b)"), stm("xin_b"))
            cx.collective(acin, acout, stm("xcc"))
            all2 = cx.sb("all2", [128, 2, DEPTH * 48], F32)
            cx.dma("sp", all2, all2[:, :, :], acout, acout.ap.rearrange("(r p) f -> p r f", p=128), stm("xout_b"))
            ada = cx.sb("ada", [128, 48, 2], F32)
            nm = cx.sb("nm", [128, KC, 2], F32)
            nf = cx.sb("nf", [128, KC, 2], F32)
            for l in range(DBG_DEPTH):
                cx.dma("sp", nm, nm[:, :, :], nmix2, nmix2[l], stm("c4"))
                cx.dma("sp", nf, nf[:, :, :], nffn2, nffn2[l], stm("c5"))
                cx.op("dve", lambda v, l=l: v.tensor_copy(
                    out=ada[:, :, :].rearrange("p (r m) j -> p r (m j)", r=2), in_=all2[:, :, l * 48:(l + 1) * 48]),
                    reads=[all2], writes=[ada])
                md = mod[l]
                for (An, Bn, Gn, base, gn) in (("A1", "B1", "G1", 0, nm), ("A2", "B2", "G2", 24, nf)):
                    cx.op("dve", lambda v, An=An, base=base, gn=gn: v.scalar_tensor_tensor(
                        out=md[An][:, :, :], in0=ada[:, base + 8:base + 16, :], scalar=1.0, in1=gn[:, :, :],
                        op0=ALU.add, op1=ALU.mult), reads=[ada, gn], writes=[md[An]])
                    cx.op("dve", lambda v, Bn=Bn, base=base: v.tensor_copy(out=md[Bn][:, :, :],
                                                                           in_=ada[:, base:base + 8, :]),
                          reads=[ada], writes=[md[Bn]])
                    cx.op("dve", lambda v, Gn=Gn, base=base: v.tensor_copy(out=md[Gn][:, :, :],
                                                                           in_=ada[:, base + 16:base + 24, :]),
                          reads=[ada], writes=[md[Gn]])
            for t in range(NTILE):
                cx.dma("sp", xs_t[t], xs_t[t][:, :, :], xin, xin[:, :, t * TILE:(t + 1) * TILE], stm(f"xin_s{t % 2}"))
            drain()
        cx.es = es

        def prep_ffn_weights(l):
            si, so = s_in[l % 2], s_out[l % 2]
            wsrc = w_ffn_in[l].rearrange("(kc p) n -> p kc n", p=128)
            st = stm(f"pw{l % 2}")
            for q in range(11):
                cx.dma("pool", si[q], si[q][:, :, 0:256], w_ffn_in, wsrc[:, :, q * 256:(q + 1) * 256], st)
                cx.dma("pool", si[q], si[q][:, :, 256:512], w_ffn_in,
                       wsrc[:, :, DFF + q * 256:DFF + (q + 1) * 256], st, partial=True)
            osrc = w_ffn_out[l].rearrange("(fc p) n -> p fc n", p=128)
            for oc in range(KC):
                cx.dma("pool", so[oc], so[oc][:, :, :], w_ffn_out, osrc[:, :, oc * 128:(oc + 1) * 128], st)
            for tk in si + so:
                tk.w = [(st["sem"], st["cnt"], "dma")]

        selm_sb = cx.sb("selm_sb", [128, 2], F32)
        cx.dma("sp", selm_sb, selm_sb[:, :], selm, selm[:, :], stm("c6"))
        xcount = [0]

        def exchange(src, src_ap, P, F_, dt, both, oth, oth_ap, tmp):
            i = xcount[0]
            xcount[0] += 1
            cin = cx.dram(f"xcin{i}", [P, F_], dt)
            cout = cx.dram(f"xcout{i}", [2 * P, F_], dt)
            cx.dma("sp", cin, cin[:, :], src, src_ap, stm("xin_b"))
            cx.collective(cin, cout, stm("xcc"))
            cx.dma("sp", both, both[:, :, :], cout, cout.ap.rearrange("(r p) f -> p r f", p=P), stm("xout_b"))
            cx.op("dve", lambda v: v.tensor_scalar(out=tmp[:, :], in0=both[:, 0, :], scalar1=selm_sb[0:P, 0:1],
                                                   scalar2=None, op0=ALU.mult), reads=[both, selm_sb], writes=[tmp])
            cx.op("dve", lambda v: v.scalar_tensor_tensor(out=oth_ap, in0=both[:, 1, :], scalar=selm_sb[0:P, 1:2],
                                                          in1=tmp[:, :], op0=ALU.mult, op1=ALU.add),
                  reads=[both, selm_sb, tmp], writes=[oth])

        class Phase:
            def __enter__(self_):
                self_.st = ExitStack()
                self_.st.__enter__()
                cx.es = self_.st
                return self_

            def __exit__(self_, *a):
                drain()
                cx.es = es
                return self_.st.__exit__(*a)


        seqs = [(0, TS, True)] + [(TS + PL * i, PL, False) for i in range(NPS)]
        u_d = cx.dram("u_d", [128, KC, NT], F32)
        gg_d = cx.dram("gg_d", [128, KC, NT], BF16)
        z_d = cx.dram("z_d", [128, KC, NT], BF16)
        hsf_d = cx.dram("hsf_d", [128, KC, NT], F32)
        xcs_d = cx.dram("xcs_d", [128, KC, NT], F32)
        q_d = cx.dram("q_d", [64, 8, NT], BF16)
        of_d = cx.dram("of_d", [NT // 64, 64, 512], F32)
        lru_o = dout("lru_o", [128, NPS, 2, 2, KC])

        def mk_norm():
            x_ring = Ring([cx.sb(f"xt{i}", [128, KC, TILE], F32) for i in range(2)])
            sq = cx.sb("sq", [128, KC, TILE], BF16)
            rstd = cx.sb("rstd", [128, TILE], F32)
            rsq = cx.sb("rsq", [128, TILE], F32)
            tmpn = Ring([cx.sb(f"tmpn{i}", [128, TILE], F32) for i in range(2)])
            h_ring = Ring([cx.sb(f"h{i}", [128, KC, TILE], BF16) for i in range(2)])
            cnt = {"x": 0}

            def load_x(t):
                xt = x_ring.next()
                xt.slot = cnt["x"] % 2
                cx.dma("sp", xt, xt[:, :, :], xs_t[t], xs_t[t][:, :, :], stm(f"xload{xt.slot}"))
                cnt["x"] += 1
                return xt

            def store_x(t, xt):
                cx.dma("sp", xs_t[t], xs_t[t][:, :, :], xt, xt[:, :, :], stm(f"xstore{xt.slot}"))

            def stats(xt):
                cx.op("act", lambda a: a.activation(out=sq[:, :, :].rearrange("p a b -> p (a b)"),
                                                    in_=xt[:, :, :].rearrange("p a b -> p (a b)"), func=AF.Square),
                      reads=[xt], writes=[sq])
                pss = psum.next()
                for c in range(KC):
                    cx.op("pe", lambda t_, c=c: t_.matmul(pss[:, :], lhsT=onesD[:, :], rhs=sq[:, c, :],
                                                          start=(c == 0), stop=(c == KC - 1)),
                          reads=[onesD, sq], writes=[pss], inc=(c == KC - 1))
                cx.op("act", lambda a: a.activation(out=rsq[:, :], in_=pss[:, :], func=AF.Sqrt, bias=epsb[:, 0:1],
                                                    scale=1.0), reads=[pss, epsb], writes=[rsq])
                cx.op("dve", lambda v: v.reciprocal(out=rstd[:, :], in_=rsq[:, :]), reads=[rsq], writes=[rstd])
                return rstd

            def norm_mod(xt, A, B, j):
                stats(xt)
                h = h_ring.next()
                for c in range(KC):
                    tm = tmpn.next()
                    cx.op("dve", lambda v, c=c, tm=tm: v.scalar_tensor_tensor(
                        out=tm[:, :], in0=xt[:, c, :], scalar=A[:, c, j:j + 1], in1=rstd[:, :],
                        op0=ALU.mult, op1=ALU.mult), reads=[xt, A, rstd], writes=[tm])
                    cx.op("act", lambda a, c=c, tm=tm: a.activation(
                        out=h[:, c, :], in_=tm[:, :], func=AF.Identity, bias=B[:, c, j:j + 1], scale=1.0),
                        reads=[tm, B], writes=[h])
                return h
            return load_x, store_x, stats, norm_mod

        def tile_j(t):
            return 0 if t * TILE < TS else 1

        def odd_phase1(l):
            o = l // 2
            with Phase():
                load_x, store_x, stats, norm_mod = mk_norm()
                wio = cx.sb("wio", [128, KC, 2 * D], BF16)
                wsrc = w_in_odd[o].rearrange("(kc p) n -> p kc n", p=128)
                cx.dma("pool", wio, wio[:, :, 0:D], w_in_odd, wsrc[:, :, 0:D], stm("wio"))
                cx.dma("pool", wio, wio[:, :, D:2 * D], w_in_odd, wsrc[:, :, D:2 * D], stm("wio"), partial=True)
                ggs = Ring([cx.sb(f"ggs{i}", [128, KC, TILE], BF16) for i in range(2)])
                us = Ring([cx.sb(f"us{i}", [128, KC, TILE], F32) for i in range(2)])
                dummy = Tk(None, "dummy")
                nxt_h = norm_mod(load_x(0), mod[l]["A1"], mod[l]["B1"], tile_j(0))
                for t in range(NTILE):
                    h1 = nxt_h
                    if t + 1 < NTILE:
                        nxt_h = norm_mod(load_x(t + 1), mod[l]["A1"], mod[l]["B1"], tile_j(t + 1))
                    gt, ut = ggs.next(), us.next()
                    for c in range(KC):
                        pg = psum.next()
                        for kc in range(KC):
                            cx.op("pe", lambda t_, kc=kc, c=c, pg=pg: t_.matmul(
                                pg[:, :], lhsT=wio[:, kc, c * 128:(c + 1) * 128], rhs=h1[:, kc, :],
                                start=(kc == 0), stop=(kc == KC - 1)), reads=[wio, h1], writes=[pg],
                                inc=(kc == KC - 1))
                        cx.op("act", lambda a, c=c, pg=pg, gt=gt: a.activation(
                            out=gt[:, c, :], in_=pg[:, :], func=AF.Gelu_apprx_tanh), reads=[pg], writes=[gt])
                        pu = psum.next()
                        for kc in range(KC):
                            cx.op("pe", lambda t_, kc=kc, c=c, pu=pu: t_.matmul(
                                pu[:, :], lhsT=wio[:, kc, D + c * 128:D + (c + 1) * 128], rhs=h1[:, kc, :],
                                start=(kc == 0), stop=(kc == KC - 1)), reads=[wio, h1], writes=[pu],
                                inc=(kc == KC - 1))
                        cx.op("dve", lambda v, c=c, pu=pu, ut=ut: v.tensor_copy(out=ut[:, c, :], in_=pu[:, :]),
                              reads=[pu], writes=[ut])
                    sl = slice(t * TILE, (t + 1) * TILE)
                    cx.dma("sp", dummy, gg_d[:, :, sl], gt, gt[:, :, :], stm(f"ggst{t % 2}"), partial=True)
                    cx.dma("sp", dummy, u_d[:, :, sl], ut, ut[:, :, :], stm(f"ust{t % 2}"), partial=True)
                    dummy.w = []
                    dummy.r = []

        def odd_phase2(l):
            o = l // 2
            LM = TS
            with Phase():
                lv = cx.sb("lv", [128, KC, 14], F32)
                cx.dma("sp", lv, lv[:, :, :], lruv, lruv[o], stm("lv"))
                der = cx.sb("der", [128, KC, 8], F32)
                tmpv = cx.sb("tmpv", [128, KC, 2], F32)
                for d in range(2):
                    b0 = 6 + 4 * d
                    cx.op("dve", lambda v, d=d, b0=b0: v.tensor_scalar(
                        out=der[:, :, 4 * d:4 * d + 2], in0=lv[:, :, b0:b0 + 2], scalar1=-1.0, scalar2=None,
                        op0=ALU.mult), reads=[lv], writes=[der])
                    cx.op("act", lambda a, b0=b0: a.activation(out=tmpv[:, :, 0:1], in_=lv[:, :, b0 + 2:b0 + 3],
                                                               func=AF.Exp, scale=-1.0), reads=[lv], writes=[tmpv])
                    cx.op("act", lambda a: a.activation(out=tmpv[:, :, 1:2], in_=tmpv[:, :, 0:1], func=AF.Ln,
                                                        bias=1.0, scale=1.0), reads=[tmpv], writes=[tmpv])
                    cx.op("dve", lambda v, d=d: v.tensor_scalar(
                        out=der[:, :, 4 * d + 2:4 * d + 3], in0=tmpv[:, :, 1:2], scalar1=-8.0, scalar2=None,
                        op0=ALU.mult), reads=[tmpv], writes=[der])
                    cx.op("dve", lambda v, d=d: v.tensor_scalar(
                        out=der[:, :, 4 * d + 3:4 * d + 4], in0=tmpv[:, :, 1:2], scalar1=-16.0, scalar2=None,
                        op0=ALU.mult), reads=[tmpv], writes=[der])
                wg = cx.sb("wg", [128, 32, 256], BF16)
                gi = 0
                for gate, wsrc_t in ((0, w_gate_a), (1, w_gate_x)):
                    for d in range(2):
                        for nb in range(4):
                            idx = ((gate * 2 + d) * 4 + nb) * 2
                            cx.dma("pool", wg, wg[:, idx:idx + 2, :], wsrc_t,
                                   wsrc_t[o, d, nb].rearrange("(k p) c -> p k c", p=128), stm("wg"),
                                   partial=(gi > 0))
                            gi += 1
                ud = Tk(None, "ud")
                zd = Tk(None, "zd")
                hd = Tk(None, "hd")
                xcd = Tk(None, "xcd")
                uh_src = cx.sb("uh_src", [128, KC, 2], F32)
                uh_oth = cx.sb("uh_oth", [128, KC, 2], F32)
                uh_both = cx.sb("uh_both", [128, 2, 2 * KC], F32)
                uh_tmp = cx.sb("uh_tmp", [128, 2 * KC], F32)
                cx.dma("sp", uh_src, uh_src[:, :, :], ud, u_d[:, :, TS - 2:TS], stm("uhl"))
                exchange(uh_src, uh_src[:, :, :].rearrange("p a b -> p (a b)"), 128, 2 * KC, F32, uh_both, uh_oth,
                         uh_oth[:, :, :].rearrange("p a b -> p (a b)"), uh_tmp)
                hfin = cx.sb("hfin", [128, KC], F32)
                hoth = cx.sb("hoth", [128, KC], F32)
                hx_both = cx.sb("hx_both", [128, 2, KC], F32)
                hx_tmp = cx.sb("hx_tmp", [128, KC], F32)
                ub = cx.sb("ub", [128, LM + 4], F32)
                xc_ring = Ring([([cx.sb(f"xc{i}{m}", [128, LM], F32) for m in range(2)],
                                 [cx.sb(f"xc16{i}{m}", [128, LM], BF16) for m in range(2)]) for i in range(2)])
                ab_ring = Ring([(cx.sb(f"a_t{i}", [128, LM], F32), cx.sb(f"b_t{i}", [128, LM], F32),
                                 cx.sb(f"t1_{i}", [128, LM], F32)) for i in range(2)])
                hs_ring = Ring([[cx.sb(f"hs{i}{d}", [128, LM], F32) for d in range(2)] for i in range(2)])
                gz_ring = Ring([cx.sb(f"gz{i}", [128, LM], BF16) for i in range(2)])
                zt_ring = Ring([cx.sb(f"zt{i}", [128, LM], BF16) for i in range(2)])
                lst = cx.sb("lst", [128, NPS, 2, KC], F32)

                def conv_part(nb, si_, s0, L, is_s, d):
                    xc, xc16 = xc_ring.next()
                    rs_ = xc_ring.i % 2
                    for m in range(2):
                        c = nb * 2 + m
                        if d == 1:
                            X = xc[m]
                            cx.dma("sp", X, X[:, 0:L], xcd, xcs_d[:, c, s0:s0 + L], stm(f"xcl{rs_}{m}"))
                            xcd.r = []
                            cx.op("act", lambda a, X=X, m=m: a.copy(out=xc16[m][:, 0:L], in_=X[:, 0:L]),
                                  reads=[X], writes=[xc16[m]])
                            continue
                        cx.op("pool", lambda g_: g_.memset(ub[:, 0:2], 0.0), writes=[ub])
                        if is_s:
                            cx.op("pool", lambda g_, c=c: g_.tensor_copy(out=ub[:, L + 2:L + 3], in_=uh_oth[:, c, 1:2]),
                                  reads=[uh_oth], writes=[ub])
                            cx.op("pool", lambda g_, c=c: g_.tensor_copy(out=ub[:, L + 3:L + 4], in_=uh_oth[:, c, 0:1]),
                                  reads=[uh_oth], writes=[ub])
                        else:
                            cx.op("pool", lambda g_: g_.memset(ub[:, L + 2:L + 4], 0.0), writes=[ub])
                        cx.dma("sp", ub, ub[:, 2:2 + L], ud, u_d[:, c, s0:s0 + L], stm("ubl"))
                        X = xc[m]
                        cx.op("dve", lambda v, c=c, X=X: v.tensor_scalar(
                            out=X[:, 0:L], in0=ub[:, 0:L], scalar1=lv[:, c, 0:1], scalar2=lv[:, c, 5:6],
                            op0=ALU.mult, op1=ALU.add), reads=[ub, lv], writes=[X])
                        for i in range(1, 5):
                            cx.op("dve", lambda v, c=c, X=X, i=i: v.scalar_tensor_tensor(
                                out=X[:, 0:L], in0=ub[:, i:i + L], scalar=lv[:, c, i:i + 1], in1=X[:, 0:L],
                                op0=ALU.mult, op1=ALU.add), reads=[ub, lv, X], writes=[X])
                        cx.op("act", lambda a, X=X, m=m: a.copy(out=xc16[m][:, 0:L], in_=X[:, 0:L]),
                              reads=[X], writes=[xc16[m]])
                        cx.dma("sp", xcd, xcs_d[:, c, s0:s0 + L], X, X[:, 0:L], stm(f"xcst{rs_}{m}"))
                        xcd.w, xcd.r = [], []
                    return xc, xc16

                def gate_part(nb, si_, s0, L, is_s, d, xc, xc16):
                    for m in range(2):
                        c = nb * 2 + m
                        ts_ = min(512, L)
                        a_t, b_t, t1 = ab_ring.next()
                        hs = hs_ring.next()
                        ri_ = hs_ring.i % 2
                        if d == 1:
                            gz, zt = gz_ring.next(), zt_ring.next()
                            cx.dma("sp", hs[0], hs[0][:, 0:L], hd, hsf_d[:, c, s0:s0 + L], stm(f"hsld{ri_}"))
                            cx.dma("sp", gz, gz[:, 0:L], ud, gg_d[:, c, s0:s0 + L], stm(f"gzl{ri_}"))
                            hd.r, ud.r = [], []
                        for s_ in range(L // ts_):
                            sl = slice(s_ * ts_, (s_ + 1) * ts_)
                            pa, px = psum.next(), psum.next()
                            for gate, pp in ((0, pa), (1, px)):
                                for k2 in range(2):
                                    idx = ((gate * 2 + d) * 4 + nb) * 2 + k2
                                    cx.op("pe", lambda t_, idx=idx, pp=pp, k2=k2, m=m, sl=sl: t_.matmul(
                                        pp[:, 0:ts_], lhsT=wg[:, idx, m * 128:(m + 1) * 128],
                                        rhs=xc16[k2][:, sl], start=(k2 == 0), stop=(k2 == 1)),
                                        reads=[wg, xc16[k2]], writes=[pp], inc=(k2 == 1))
                            cx.op("act", lambda a, sl=sl: a.activation(
                                out=a_t[:, sl], in_=pa[:, 0:ts_], func=AF.Exp, scale=-1.0,
                                bias=der[:, c, 4 * d:4 * d + 1]), reads=[pa, der], writes=[a_t])
                            cx.op("act", lambda a, sl=sl: a.activation(
                                out=b_t[:, sl], in_=px[:, 0:ts_], func=AF.Exp, scale=-1.0,
                                bias=der[:, c, 4 * d + 1:4 * d + 2]), reads=[px, der], writes=[b_t])
                        A, B, T1 = a_t[:, 0:L], b_t[:, 0:L], t1[:, 0:L]
                        cx.op("act", lambda a: a.activation(out=A, in_=A, func=AF.Ln, scale=1.0, bias=1.0),
                              reads=[a_t], writes=[a_t])
                        cx.op("act", lambda a: a.activation(out=A, in_=A, func=AF.Exp, scale=-1.0),
                              reads=[a_t], writes=[a_t])
                        cx.op("act", lambda a: a.activation(out=T1, in_=A, func=AF.Exp,
                                                            scale=der[:, c, 4 * d + 3:4 * d + 4]),
                              reads=[a_t, der], writes=[t1])
                        cx.op("act", lambda a: a.activation(out=T1, in_=T1, func=AF.Ln, scale=-1.0, bias=1.0),
                              reads=[t1], writes=[t1])
                        cx.op("act", lambda a: a.activation(out=T1, in_=T1, func=AF.Exp, scale=0.5),
                              reads=[t1], writes=[t1])
                        cx.op("act", lambda a: a.activation(out=A, in_=A, func=AF.Exp,
                                                            scale=der[:, c, 4 * d + 2:4 * d + 3]),
                              reads=[a_t, der], writes=[a_t])
                        cx.op("dve", lambda v: v.tensor_scalar(out=B, in0=B, scalar1=1.0, scalar2=None, op0=ALU.add),
                              reads=[b_t], writes=[b_t])
                        cx.op("dve", lambda v: v.reciprocal(out=B, in_=B), reads=[b_t], writes=[b_t])
                        cx.op("dve", lambda v, m=m: v.tensor_tensor(out=B, in0=B, in1=xc[m][:, 0:L], op=ALU.mult),
                              reads=[b_t, xc[m]], writes=[b_t])
                        cx.op("dve", lambda v: v.tensor_tensor(out=B, in0=B, in1=T1, op=ALU.mult),
                              reads=[b_t, t1], writes=[b_t])
                        H = hs[d]
                        if d == 0:
                            init = lv[:, c, 9:10] if is_s else 0.0
                            cx.op("dve", lambda v, H=H, init=init: v.tensor_tensor_scan(
                                out=H[:, 0:L], data0=a_t[:, 0:L], data1=b_t[:, 0:L], initial=init,
                                op0=ALU.mult, op1=ALU.add), reads=[a_t, b_t, lv], writes=[H])
                            cx.dma("sp", hd, hsf_d[:, c, s0:s0 + L], H, H[:, 0:L], stm(f"hsst{ri_}"))
                            hd.w, hd.r = [], []
                            if is_s:
                                cx.op("pool", lambda g_, H=H, c=c: g_.tensor_copy(
                                    out=hfin[:, c:c + 1], in_=H[:, L - 1:L]), reads=[H], writes=[hfin])
                            else:
                                cx.op("pool", lambda g_, H=H, c=c: g_.tensor_copy(
                                    out=lst[:, si_ - 1, 0, c:c + 1], in_=H[:, L - 1:L]), reads=[H], writes=[lst])
                        else:
                            init = hoth[:, c:c + 1] if is_s else 0.0
                            cx.op("dve", lambda v, H=H, init=init: v.tensor_tensor_scan(
                                out=H[:, 0:L][:, ::-1], data0=a_t[:, 0:L][:, ::-1], data1=b_t[:, 0:L][:, ::-1],
                                initial=init, op0=ALU.mult, op1=ALU.add), reads=[a_t, b_t, hoth], writes=[H])
                            if not is_s:
                                cx.op("pool", lambda g_, H=H, c=c: g_.tensor_copy(
                                    out=lst[:, si_ - 1, 1, c:c + 1], in_=H[:, 0:1]), reads=[H], writes=[lst])
                            cx.op("dve", lambda v: v.tensor_tensor(out=hs[0][:, 0:L], in0=hs[0][:, 0:L],
                                                                   in1=hs[1][:, 0:L], op=ALU.add),
                                  reads=[hs[0], hs[1]], writes=[hs[0]])
                            cx.op("dve", lambda v: v.tensor_tensor(out=zt[:, 0:L], in0=hs[0][:, 0:L],
                                                                   in1=gz[:, 0:L], op=ALU.mult),
                                  reads=[hs[0], gz], writes=[zt])
                            cx.dma("sp", zd, z_d[:, c, s0:s0 + L], zt, zt[:, 0:L], stm(f"zst{ri_}"))
                            zd.w, zd.r, hd.r = [], [], []
                        ud.r = []

                items = [(nb, si_, s0, L, is_s) for nb in range(4) for si_, (s0, L, is_s) in enumerate(seqs)]
                for d in range(2):
                    nxt_c = conv_part(*items[0], d)
                    for k_, it in enumerate(items):
                        cur_c = nxt_c
                        if k_ + 1 < len(items):
                            nxt_c = conv_part(*items[k_ + 1], d)
                        gate_part(*it, d, *cur_c)
                    if d == 0:
                        exchange(hfin, hfin[:, :], 128, KC, F32, hx_both, hoth, hoth[:, :], hx_tmp)
                        drain()
                cx.dma("sp", lru_o, lru_o[:, :, o, :, :], lst, lst[:, :, :, :], stm("lruo"))

        def even_attn(l):
            e = l // 2
            with Phase():
                load_x, store_x, stats, norm_mod = mk_norm()
                wea = cx.sb("wea", [128, KC, NCA], BF16)
                wsrc = w_ev[e].rearrange("(kc p) n -> p kc n", p=128)
                cx.dma("pool", wea, wea[:, :, 0:1024], w_ev, wsrc[:, :, 0:1024], stm("wea"))
                cx.dma("pool", wea, wea[:, :, 1024:NCA], w_ev, wsrc[:, :, 1024:NCA], stm("wea"), partial=True)
                kT = cx.sb("kT", [64, 2, NT + 128], BF16)
                vaug = cx.sb("vaug", [128, NBLK + 1, 130], BF16)
                cx.op("dve", lambda g_: g_.memset(vaug[:, :, :], 1.0), writes=[vaug])
                kc_sb = cx.sb("kc_sb", [64, 2, 256], BF16)
                vc_sb = cx.sb("vc_sb", [128, 2, 130], BF16)
                kc_f = cx.sb("kc_f", [64, 2, 256], F32)
                vc_f = cx.sb("vc_f", [128, 2, 130], F32)
                msk_f = cx.sb("msk_f", [128, 3, 512], F32)
                msk = cx.sb("msk", [128, 3, 512], BF16)
                for k_ in range(2):
                    cx.dma("sp", kc_f, kc_f[:, k_, :], ckT, ckT[e, k_], stm("kcl"), partial=(k_ > 0))
                    cx.dma("sp", vc_f, vc_f[:, k_, :], cvaug, cvaug[e, k_], stm("vcl"), partial=(k_ > 0))
                for k_ in range(3):
                    cx.dma("sp", msk_f, msk_f[:, k_, :], maskT, maskT[k_], stm("mskl"), partial=(k_ > 0))
                cx.op("dve", lambda v: v.tensor_copy(out=kc_sb[:, :, :], in_=kc_f[:, :, :]), reads=[kc_f], writes=[kc_sb])
                cx.op("dve", lambda v: v.tensor_copy(out=vc_sb[:, :, :], in_=vc_f[:, :, :]), reads=[vc_f], writes=[vc_sb])
                cx.op("dve", lambda v: v.tensor_copy(out=msk[:, :, :], in_=msk_f[:, :, :]), reads=[msk_f], writes=[msk])
                snk = cx.sb("snk", [128, 8], F32)
                esink = cx.sb("esink", [128, 8], F32)
                cx.dma("sp", snk, snk[:, :], sinkb, sinkb[e], stm("snkl"))
                cx.op("act", lambda a: a.activation(out=esink[:, :], in_=snk[:, :], func=AF.Exp),
                      reads=[snk], writes=[esink])
                rc_ring = Ring([cx.sb(f"rc{i}", [64, 2, TILE], F32) for i in range(2)])
                t1r = Ring([cx.sb(f"t1r{i}", [64, TILE], F32) for i in range(2)])
                t2r = Ring([cx.sb(f"t2r{i}", [64, TILE], F32) for i in range(2)])
                qst_ring = Ring([cx.sb(f"qst{i}", [64, 8, TILE], BF16) for i in range(2)])
                kvst = Ring([cx.sb(f"kvst{i}", [128, 128], F32) for i in range(4)])
                qd = Tk(None, "qd")
                cn = {"rc": 0, "q": 0, "kv": 0, "qb": 0, "oa": 0}

                def proj_fm(h1, col0, pcol0, is_s, rc, dt_, dap):
                    pq = psum.next()
                    for kc in range(KC):
                        cx.op("pe", lambda t_, kc=kc: t_.matmul(
                            pq[0:64, :], lhsT=wea[:, kc, col0:col0 + 64], rhs=h1[:, kc, :],
                            start=(kc == 0), stop=(kc == KC - 1)), reads=[wea, h1], writes=[pq], inc=(kc == KC - 1))
                    if not is_s:
                        cx.op("act", lambda a: a.copy(out=dap, in_=pq[0:64, :]), reads=[pq], writes=[dt_])
                        return
                    pp = psum.next()
                    for kc in range(KC):
                        cx.op("pe", lambda t_, kc=kc: t_.matmul(
                            pp[0:64, :], lhsT=wea[:, kc, pcol0:pcol0 + 64], rhs=h1[:, kc, :],
                            start=(kc == 0), stop=(kc == KC - 1)), reads=[wea, h1], writes=[pp], inc=(kc == KC - 1))
                    t1, t2 = t1r.next(), t2r.next()
                    cx.op("dve", lambda v: v.tensor_tensor(out=t1[:, :], in0=pq[0:64, :], in1=rc[:, 0, :], op=ALU.mult),
                          reads=[pq, rc], writes=[t1])
                    cx.op("dve", lambda v: v.tensor_tensor(out=t2[:, :], in0=pp[0:64, :], in1=rc[:, 1, :], op=ALU.mult),
                          reads=[pp, rc], writes=[t2])
                    cx.op("dve", lambda g_: g_.tensor_tensor(out=dap, in0=t1[:, :], in1=t2[:, :], op=ALU.add),
                          reads=[t1, t2], writes=[dt_])

                nxt_h = norm_mod(load_x(0), mod[l]["A1"], mod[l]["B1"], tile_j(0))
                for t in range(NTILE):
                    is_s = t * TILE < TS
                    h1 = nxt_h
                    if t + 1 < NTILE:
                        nxt_h = norm_mod(load_x(t + 1), mod[l]["A1"], mod[l]["B1"], tile_j(t + 1))
                    rc = None
                    if is_s:
                        rc = rc_ring.next()
                        cx.dma("sp", rc, rc[:, :, :], rope_d, rope_d[:, :, t * TILE:(t + 1) * TILE],
                               stm(f"rc{cn['rc'] % 2}"))
                        cn["rc"] += 1
                    qst = qst_ring.next()
                    for h in range(8):
                        proj_fm(h1, C_Q + h * 64, C_PQ + h * 64, is_s, rc, qst, qst[:, h, :])
                    cx.dma("sp", qd, q_d[:, :, t * TILE:(t + 1) * TILE], qst, qst[:, :, :], stm(f"qst{cn['q'] % 2}"))
                    cn["q"] += 1
                    qd.w, qd.r = [], []
                    for kv in range(2):
                        proj_fm(h1, C_K + kv * 64, C_PK + kv * 64, is_s, rc, kT, kT[:, kv, t * TILE:(t + 1) * TILE])
                    for b in range(4):
                        blk = t * 4 + b
                        pv = psum.next()
                        for kc in range(KC):
                            cx.op("pe", lambda t_, kc=kc: t_.matmul(
                                pv[:, 0:128], lhsT=h1[:, kc, b * 128:(b + 1) * 128], rhs=wea[:, kc, C_V:C_V + 128],
                                start=(kc == 0), stop=(kc == KC - 1)), reads=[wea, h1], writes=[pv],
                                inc=(kc == KC - 1))
                        cx.op("act", lambda a: a.copy(
                            out=vaug[:, blk, :].rearrange("p (k d) -> p k d", k=2)[:, :, 0:64],
                            in_=pv[:, 0:128].rearrange("p (k d) -> p k d", k=2)), reads=[pv], writes=[vaug])
                        if not is_s:
                            sq_i, r0 = (blk - NBS) // 2, ((blk - NBS) % 2) * 128
                            vs = kvst.next()
                            cx.op("act", lambda v: v.copy(out=vs[:, :], in_=pv[:, 0:128]), reads=[pv], writes=[vs])
                            cx.dma("sp", nv_o, nv_o[sq_i, e, r0:r0 + 128, :], vs, vs[:, :], stm(f"kvo{cn['kv'] % 4}"))
                            cn["kv"] += 1
                            pk = psum.next()
                            for kc in range(KC):
                                cx.op("pe", lambda t_, kc=kc: t_.matmul(
                                    pk[:, 0:128], lhsT=h1[:, kc, b * 128:(b + 1) * 128],
                                    rhs=wea[:, kc, C_K:C_K + 128], start=(kc == 0), stop=(kc == KC - 1)),
                                    reads=[wea, h1], writes=[pk], inc=(kc == KC - 1))
                            ks = kvst.next()
                            cx.op("dve", lambda v: v.tensor_copy(out=ks[:, :], in_=pk[:, 0:128]), reads=[pk], writes=[ks])
                            cx.dma("sp", nk_o, nk_o[sq_i, e, r0:r0 + 128, :], ks, ks[:, :], stm(f"kvo{cn['kv'] % 4}"))
                            cn["kv"] += 1
                xk_both = cx.sb("xk_both", [64, 2, 256], BF16)
                xk_tmp = cx.sb("xk_tmp", [64, 256], BF16)
                xk_src = cx.sb("xk_src", [64, 2, 128], BF16)
                cx.op("dve", lambda v: v.tensor_copy(out=xk_src[:, :, :], in_=kT[:, :, TS - 128:TS]), reads=[kT],
                      writes=[xk_src])
                xk_oth = cx.sb("xk_oth", [64, 2, 128], BF16)
                exchange(xk_src, xk_src[:, :, :].rearrange("p a b -> p (a b)"), 64, 256, BF16, xk_both, xk_oth,
                         xk_oth[:, :, :].rearrange("p a b -> p (a b)"), xk_tmp)
                cx.op("dve", lambda v: v.tensor_copy(out=kT[:, :, NT:NT + 128], in_=xk_oth[:, :, :]), reads=[xk_oth],
                      writes=[kT])
                xv_both = cx.sb("xv_both", [128, 2, 130], BF16)
                xv_tmp = cx.sb("xv_tmp", [128, 130], BF16)
                xv_src = cx.sb("xv_src", [128, 130], BF16)
                xv_oth = cx.sb("xv_oth", [128, 130], BF16)
                cx.op("dve", lambda v: v.tensor_copy(out=xv_src[:, :], in_=vaug[:, NBS - 1, :]), reads=[vaug],
                      writes=[xv_src])
                exchange(xv_src, xv_src[:, :], 128, 130, BF16, xv_both, xv_oth, xv_oth[:, :], xv_tmp)
                cx.op("dve", lambda v: v.tensor_copy(out=vaug[:, NBLK, :], in_=xv_oth[:, :]), reads=[xv_oth],
                      writes=[vaug])
                drain()
                qb_ring = Ring([cx.sb(f"qb{i}", [64, 8, 128], BF16) for i in range(2)])
                pT_ring = Ring([cx.sb(f"pT{i}", [128, 512], BF16) for i in range(4)])
                otok_ring = Ring([cx.sb(f"otok{i}", [128, 512], BF16) for i in range(2)])
                oaT_ring = Ring([cx.sb(f"oaT{i}", [128, 4, 128], BF16) for i in range(2)])
                den = cx.sb("den", [128, 8], F32)
                rden = cx.sb("rden", [128, 8, 1], F32)
                zd = Tk(None, "zd3")
                for blk in range(NBLK):
                    qb = qb_ring.next()
                    cx.dma("pool", qb, qb[:, :, :], qd, q_d[:, :, blk * 128:(blk + 1) * 128], stm(f"qb{cn['qb'] % 2}"))
                    cn["qb"] += 1
                    if blk < NBS:
                        keys = [("c", 0, None), ("c", 1, None)]
                        if blk > 0:
                            keys.append(("w", blk - 1, 0))
                        keys.append(("w", blk, None))
                        if blk < NBS - 1:
                            keys.append(("w", blk + 1, 1))
                        elif os.environ.get("KDBG_NOHALO", "0") == "0":
                            keys.append(("h", NBLK, 2))
                    else:
                        b0 = NBS + ((blk - NBS) // 2) * 2
                        keys = [("w", b0, None), ("w", b0 + 1, None)]
                    def scores(kvh, kind, kb, mk):
                        ps_ = psum.next()
                        kt_t = kc_sb if kind == "c" else kT
                        if kind == "c":
                            kap = kc_sb[:, kvh, kb * 128:(kb + 1) * 128]
                        elif kind == "h":
                            kap = kT[:, kvh, NT:NT + 128]
                        else:
                            kap = kT[:, kvh, kb * 128:(kb + 1) * 128]
                        for g in range(4):
                            h = kvh * 4 + g
                            cx.op("pe", lambda t_, g=g, h=h: t_.matmul(
                                ps_[:, g * 128:(g + 1) * 128], lhsT=kap, rhs=qb[:, h, :],
                                start=(g == 0), stop=(g == 3 and mk is None), skip_group_check=True),
                                reads=[kt_t, qb], writes=[ps_], inc=(g == 3 and mk is None))
                        if mk is not None:
                            cx.op("pe", lambda t_: t_.matmul(ps_[:, :], lhsT=ident_b[:, :], rhs=msk[:, mk, :],
                                                             start=False, stop=True, skip_group_check=True),
                                  reads=[ident_b, msk], writes=[ps_])
                        pT = pT_ring.next()
                        cx.op("act", lambda a: a.activation(out=pT[:, :], in_=ps_[:, :], func=AF.Exp, scale=0.125),
                              reads=[ps_], writes=[pT])
                        return pT

                    work = [(kvh, ki, kk) for kvh in range(2) for ki, kk in enumerate(keys)]
                    pend = scores(work[0][0], *work[0][2])
                    for wi, (kvh, ki, (kind, kb, mk)) in enumerate(work):
                        pT = pend
                        if wi + 1 < len(work):
                            pend = scores(work[wi + 1][0], *work[wi + 1][2])
                        ob = accb[kvh]
                        v_t = vc_sb if kind == "c" else vaug
                        for g in range(4):
                            vap = (vc_sb[:, kb, kvh * 65:(kvh + 1) * 65] if kind == "c"
                                   else vaug[:, kb, kvh * 65:(kvh + 1) * 65])
                            last = (ki == len(keys) - 1)
                            cx.op("pe", lambda t_, g=g, vap=vap: t_.matmul(
                                ob[:, g * 65:(g + 1) * 65], lhsT=pT[:, g * 128:(g + 1) * 128], rhs=vap,
                                start=(ki == 0 and g == 0), stop=last, skip_group_check=True),
                                reads=[pT, v_t], writes=[ob], inc=(g == 3))
                    otok = otok_ring.next()
                    for kvh in range(2):
                        ob = accb[kvh]
                        o3 = ob[:, 0:260].rearrange("p (g d) -> p g d", g=4)
                        cx.op("dve", lambda v, kvh=kvh, o3=o3: v.tensor_tensor(
                            out=den[:, kvh * 4:(kvh + 1) * 4], in0=o3[:, :, 64], in1=esink[:, kvh * 4:(kvh + 1) * 4],
                            op=ALU.add), reads=[ob, esink], writes=[den])
                        cx.op("dve", lambda v, kvh=kvh: v.reciprocal(
                            out=rden[:, kvh * 4:(kvh + 1) * 4, :].rearrange("p a b -> p (a b)"),
                            in_=den[:, kvh * 4:(kvh + 1) * 4]), reads=[den], writes=[rden])
                        cx.op("dve", lambda v, kvh=kvh, o3=o3: v.tensor_tensor(
                            out=otok[:, kvh * 256:(kvh + 1) * 256].rearrange("p (g d) -> p g d", g=4),
                            in0=o3[:, :, 0:64], in1=rden[:, kvh * 4:(kvh + 1) * 4, :].to_broadcast([128, 4, 64]),
                            op=ALU.mult), reads=[ob, rden], writes=[otok])
                    ptr = psum.next()
                    ptb = ptr.ap[:, :].bitcast(BF16)
                    for cc in range(4):
                        cx.op("pe", lambda t_, cc=cc: t_.transpose(
                            ptb[:, cc * 128:(cc + 1) * 128], otok[:, cc * 128:(cc + 1) * 128], ident_b[:, :]),
                            reads=[otok, ident_b], writes=[ptr], inc=(cc == 3))
                    oaT = oaT_ring.next()
                    cx.op("act", lambda a: a.copy(out=oaT[:, :, :].rearrange("p a b -> p (a b)"), in_=ptb[:, 0:512]),
                          reads=[ptr], writes=[oaT])
                    cx.dma("sp", zd, z_d[:, 0:4, blk * 128:(blk + 1) * 128], oaT, oaT[:, :, :], stm(f"oast{cn['oa'] % 2}"))
                    cn["oa"] += 1
                    zd.w, zd.r, qd.r = [], [], []

        def even_gla(l):
            e = l // 2
            X = mybir.AxisListType.X
            with Phase():
                load_x, store_x, stats, norm_mod = mk_norm()
                wgl = cx.sb("wgl", [128, KC, NCG], BF16)
                wsrc = w_ev[e].rearrange("(kc p) n -> p kc n", p=128)
                cx.dma("pool", wgl, wgl[:, :, 0:800], w_ev, wsrc[:, :, NCA:NCA + 800], stm("wgl"))
                cx.dma("pool", wgl, wgl[:, :, 800:NCG], w_ev, wsrc[:, :, NCA + 800:NCE], stm("wgl"), partial=True)
                wal = cx.sb("wal", [64, 256], F32)
                gv = cx.sb("gv", [64, 8 + 512], F32)
                cx.dma("sp", wal, wal[:, :], walp, walp[e], stm("wal"))
                cx.dma("sp", gv, gv[:, :], glav, glav[e], stm("gvl"))
                nbal = cx.sb("nbal", [64, 8], F32)
                cx.op("dve", lambda v: v.tensor_scalar(out=nbal[:, :], in0=gv[:, 0:8], scalar1=-1.0, scalar2=None,
                                                       op0=ALU.mult), reads=[gv], writes=[nbal])
                gm = cx.sb("gm", [64, 2, 512], BF16)
                cx.dma("pool", gm, gm[:, :, :], gmask, gmask.ap.rearrange("d p n -> p d n"), stm("gml"))
                rm = cx.sb("rm", [64, 2, 4 * TILE], BF16)
                cx.op("dve", lambda v: v.memset(rm[:, :, :], 1.0), writes=[rm])
                cx.op("dve", lambda v: v.memset(rm[:, 0, 0:4 * TILE:64], 0.0), writes=[rm])
                cx.op("dve", lambda v: v.memset(rm[:, 1, 63:4 * TILE:64], 0.0), writes=[rm])
                lrT = cx.sb("lrT", [64, TILE], F32)
                e1 = cx.sb("e1", [64, TILE], F32)
                Lall = cx.sb("Lall", [64, 4, TILE], F32)
                Lc = cx.sb("Lc", [64, 4, TILE], F32)
                Eq = cx.sb("Eq", [64, 4, TILE], F32)
                Ek = cx.sb("Ek", [64, 4, TILE], F32)
                q_e = cx.sb("q_e", [64, 4, TILE], BF16)
                k_e = cx.sb("k_e", [64, 4, TILE], BF16)
                vt = cx.sb("vt", [64, 8, 512], BF16)
                sr = cx.sb("sr", [64, 8, 512], BF16)
                keT = cx.sb("keT", [64, 8, 256], BF16)
                atm_r = Ring([cx.sb(f"atm{i}", [64, 512], BF16) for i in range(2)])
                T_r = Ring([cx.sb(f"Tt{i}", [64, 4, 128], F32) for i in range(2)])
                U_r = Ring([cx.sb(f"Ut{i}", [64, 4, 128], F32) for i in range(2)])
                Ub_r = Ring([cx.sb(f"Ub{i}", [64, 4, 128], BF16) for i in range(2)])
                ofs_r = Ring([cx.sb(f"ofs{i}", [64, 512], F32) for i in range(2)])
                o_r = Ring([cx.sb(f"ot{i}", [64, 4, 128], F32) for i in range(2)])
                osq = cx.sb("osq", [64, 4, 128], F32)
                ss = cx.sb("ss", [64, 4], F32)
                rs = cx.sb("rs", [64, 4, 1], F32)
                gs = cx.sb("gs", [64, 512], F32)
                ob_r = Ring([cx.sb(f"obk{i}", [64, 512], BF16) for i in range(2)])
                obT_r = Ring([cx.sb(f"obT{i}", [128, 4, 64], BF16) for i in range(2)])
                ofd, zd = Tk(None, "ofd"), Tk(None, "zd5")
                cn = {"of": 0, "ol": 0, "z": 0}
                state = {}

                def tile_chunks(t):
                    if t * TILE < TS:
                        return [(ci, 0, t * 8 + ci, TS // 64) for ci in range(8)]
                    s0 = 1 + (t - TS // TILE) * 2
                    return [(ci, s0 + ci // 4, ci % 4, 4) for ci in range(8)]

                def sweep_tile(t, d, h1):
                    pl = psum.next()
                    for kc in range(KC):
                        cx.op("pe", lambda t_, kc=kc: t_.matmul(
                            pl[0:64, :], lhsT=wgl[:, kc, G_LR:G_LR + 64], rhs=h1[:, kc, :],
                            start=(kc == 0), stop=(kc == KC - 1)), reads=[wgl, h1], writes=[pl], inc=(kc == KC - 1))
                    cx.op("act", lambda a: a.copy(out=lrT[:, :], in_=pl[0:64, :]), reads=[pl], writes=[lrT])
                    for h in range(4):
                        yl = psum.next()
                        cx.op("pe", lambda t_, h=h: t_.matmul(
                            yl[0:64, :], lhsT=wal[32 * d:32 * d + 16, h * 64:(h + 1) * 64],
                            rhs=lrT[32 * d:32 * d + 16, :], start=True, stop=True), reads=[wal, lrT], writes=[yl])
                        cx.op("act", lambda a, h=h: a.activation(out=e1[:, :], in_=yl[0:64, :], func=AF.Exp, scale=-1.0,
                                                                 bias=nbal[:, d * 4 + h:d * 4 + h + 1]),
                              reads=[yl, nbal], writes=[e1])
                        cx.op("act", lambda a, h=h: a.activation(out=Lall[:, h, :], in_=e1[:, :], func=AF.Ln, bias=1.0,
                                                                 scale=1.0), reads=[e1], writes=[Lall])
                    Lf = Lall[:, :, :].rearrange("p a b -> p (a b)")
                    Lcf = Lc[:, :, :].rearrange("p a b -> p (a b)")
                    if d == 0:
                        cx.op("dve", lambda v: v.tensor_tensor_scan(out=Lcf, data0=rm[:, 0, :], data1=Lf, initial=0.0,
                                                                    op0=ALU.mult, op1=ALU.add),
                              reads=[rm, Lall], writes=[Lc])
                    else:
                        cx.op("dve", lambda v: v.tensor_tensor_scan(out=Lcf[:, ::-1], data0=rm[:, 1, :][:, ::-1],
                                                                    data1=Lf[:, ::-1], initial=0.0,
                                                                    op0=ALU.mult, op1=ALU.add),
                              reads=[rm, Lall], writes=[Lc])
                    cx.op("act", lambda a: a.activation(out=Eq[:, :, :].rearrange("p a b -> p (a b)"), in_=Lcf,
                                                        func=AF.Exp, scale=-1.0 / 16), reads=[Lc], writes=[Eq])
                    cx.op("act", lambda a: a.activation(out=Ek[:, :, :].rearrange("p a b -> p (a b)"), in_=Lcf,
                                                        func=AF.Exp, scale=1.0 / 16), reads=[Lc], writes=[Ek])
                    for h in range(4):
                        pq = psum.next()
                        for kc in range(KC):
                            cx.op("pe", lambda t_, kc=kc, h=h: t_.matmul(
                                pq[0:64, :], lhsT=wgl[:, kc, G_QB + h * 64:G_QB + (h + 1) * 64], rhs=h1[:, kc, :],
                                start=(kc == 0), stop=(kc == KC - 1)), reads=[wgl, h1], writes=[pq], inc=(kc == KC - 1))
                        cx.op("dve", lambda v, h=h: v.scalar_tensor_tensor(
                            out=q_e[:, h, :], in0=pq[0:64, :], scalar=0.125, in1=Eq[:, h, :], op0=ALU.mult,
                            op1=ALU.mult), reads=[pq, Eq], writes=[q_e])
                        pk = psum.next()
                        for kc in range(KC):
                            cx.op("pe", lambda t_, kc=kc, h=h: t_.matmul(
                                pk[0:64, :], lhsT=wgl[:, kc, G_KB + h * 64:G_KB + (h + 1) * 64], rhs=h1[:, kc, :],
                                start=(kc == 0), stop=(kc == KC - 1)), reads=[wgl, h1], writes=[pk], inc=(kc == KC - 1))
                        cx.op("dve", lambda v, h=h: v.tensor_tensor(
                            out=k_e[:, h, :], in0=pk[0:64, :], in1=Ek[:, h, :], op=ALU.mult),
                            reads=[pk, Ek], writes=[k_e])
                    for ci in range(8):
                        c0 = ci * 64
                        pv = psum.next()
                        for kc in range(KC):
                            cx.op("pe", lambda t_, kc=kc: t_.matmul(
                                pv[0:64, :], lhsT=h1[:, kc, c0:c0 + 64], rhs=wgl[:, kc, G_VB:G_VB + 512],
                                start=(kc == 0), stop=(kc == KC - 1)), reads=[wgl, h1], writes=[pv], inc=(kc == KC - 1))
                        cx.op("act", lambda a: a.copy(out=vt[:, ci, :], in_=pv[0:64, :]), reads=[pv], writes=[vt])
                        if d == 1:
                            pr = psum.next()
                            for kc in range(KC):
                                cx.op("pe", lambda t_, kc=kc: t_.matmul(
                                    pr[0:64, :], lhsT=h1[:, kc, c0:c0 + 64], rhs=wgl[:, kc, G_RB:G_RB + 512],
                                    start=(kc == 0), stop=(kc == KC - 1)), reads=[wgl, h1], writes=[pr],
                                    inc=(kc == KC - 1))
                            cx.op("act", lambda a: a.activation(out=sr[:, ci, :], in_=pr[0:64, :], func=AF.Silu),
                                  reads=[pr], writes=[sr])
                        ptk = psum.next()
                        ptkb = ptk.ap[:, :].bitcast(BF16)
                        for h in range(4):
                            cx.op("pe", lambda t_, h=h: t_.transpose(
                                ptkb[0:64, h * 64:(h + 1) * 64], k_e[:, h, c0:c0 + 64], ident_b[0:64, 0:64]),
                                reads=[k_e, ident_b], writes=[ptk], inc=(h == 3))
                        cx.op("act", lambda a: a.copy(out=keT[:, ci, :], in_=ptkb[0:64, 0:256]), reads=[ptk], writes=[keT])
                    order = tile_chunks(t)
                    if d == 1:
                        order = order[::-1]
                    atm = None
                    for (ci, sq_i, pos, nch) in order:
                        c0 = ci * 64
                        cp = ci % 2
                        first = (pos == 0) if d == 0 else (pos == nch - 1)
                        last = (pos == nch - 1) if d == 0 else (pos == 0)
                        need_at = (cp == 0) if d == 0 else (cp == 1)
                        if need_at:
                            pat = psum.next()
                            cb = ci - cp
                            for cq in range(2):
                                cc0 = (cb + cq) * 64
                                for h in range(4):
                                    cx.op("pe", lambda t_, h=h, cq=cq, cc0=cc0: t_.matmul(
                                        pat[0:64, cq * 256 + h * 64:cq * 256 + (h + 1) * 64],
                                        lhsT=k_e[:, h, cc0:cc0 + 64], rhs=q_e[:, h, cc0:cc0 + 64],
                                        start=(cq == 0 and h == 0), stop=(cq == 1 and h == 3), skip_group_check=True),
                                        reads=[k_e, q_e], writes=[pat], inc=(cq == 1 and h == 3))
                            atm = atm_r.next()
                            cx.op("dve", lambda v: v.tensor_tensor(out=atm[:, :], in0=pat[0:64, :], in1=gm[:, d, :],
                                                                   op=ALU.mult), reads=[pat, gm], writes=[atm])
                        if first:
                            U = U_r.next()
                            Ub = Ub_r.next()
                            if sq_i == 0 and d == 0:
                                cx.dma("sp", U, U[:, :, :], sgl, sgl[e, d].rearrange("h k v -> k h v"), stm("sgl"))
                            elif sq_i == 0:
                                cx.op("dve", lambda v: v.tensor_copy(out=U[:, :, :], in_=Uoth[:, :, :]), reads=[Uoth],
                                      writes=[U])
                            else:
                                cx.op("dve", lambda v: v.memset(U[:, :, :], 0.0), writes=[U])
                            cx.op("act", lambda a: a.copy(out=Ub[:, :, :], in_=U[:, :, :]), reads=[U], writes=[Ub])
                            state["U"], state["Ub"] = U, Ub
                        U, Ub = state["U"], state["Ub"]
                        pkv = psum.next()
                        for h in range(4):
                            cx.op("pe", lambda t_, h=h: t_.matmul(
                                pkv[0:64, h * 128:(h + 1) * 128], lhsT=keT[:, ci, h * 64:(h + 1) * 64],
                                rhs=vt[:, ci, h * 128:(h + 1) * 128], start=(h == 0), stop=(h == 3),
                                skip_group_check=True), reads=[keT, vt], writes=[pkv], inc=(h == 3))
                        po = psum.next()
                        for h in range(4):
                            cx.op("pe", lambda t_, h=h: t_.matmul(
                                po[0:64, h * 128:(h + 1) * 128], lhsT=atm[:, cp * 256 + h * 64:cp * 256 + (h + 1) * 64],
                                rhs=vt[:, ci, h * 128:(h + 1) * 128], start=(h == 0), stop=False,
                                skip_group_check=True), reads=[atm, vt], writes=[po], inc=False)
                            cx.op("pe", lambda t_, h=h: t_.matmul(
                                po[0:64, h * 128:(h + 1) * 128], lhsT=q_e[:, h, c0:c0 + 64], rhs=Ub[:, h, :],
                                start=False, stop=(h == 3), skip_group_check=True),
                                reads=[q_e, Ub], writes=[po], inc=(h == 3))
                        Tt = T_r.next()
                        cx.op("dve", lambda v: v.tensor_tensor(out=Tt[:, :, :].rearrange("p a b -> p (a b)"),
                                                               in0=pkv[0:64, :],
                                                               in1=U[:, :, :].rearrange("p a b -> p (a b)"), op=ALU.add),
                              reads=[pkv, U], writes=[Tt])
                        U2, Ub2 = U_r.next(), Ub_r.next()
                        dcol = c0 + 63 if d == 0 else c0
                        cx.op("dve", lambda v: v.tensor_tensor(
                            out=U2[:, :, :], in0=Tt[:, :, :], in1=Eq[:, :, dcol:dcol + 1].to_broadcast([64, 4, 128]),
                            op=ALU.mult), reads=[Tt, Eq], writes=[U2])
                        cx.op("act", lambda a: a.copy(out=Ub2[:, :, :], in_=U2[:, :, :]), reads=[U2], writes=[Ub2])
                        state["U"], state["Ub"] = U2, Ub2
                        if last and sq_i > 0:
                            cx.dma("sp", gla_o, gla_o[sq_i - 1, e, d].rearrange("h k v -> k h v"), U2, U2[:, :, :],
                                   stm("glao"))
                        if last and sq_i == 0 and d == 0:
                            cx.op("dve", lambda v: v.tensor_copy(out=Ufin[:, :, :], in_=U2[:, :, :]), reads=[U2],
                                  writes=[Ufin])
                        gch = (t * TILE + c0) // 64
                        if d == 0:
                            ofs = ofs_r.next()
                            cx.op("act", lambda a: a.copy(out=ofs[:, :], in_=po[0:64, :]), reads=[po], writes=[ofs])
                            cx.dma("sp", ofd, of_d[gch], ofs, ofs[:, :], stm(f"ofst{cn['of'] % 2}"))
                            cn["of"] += 1
                            ofd.w, ofd.r = [], []
                        else:
                            ofs = ofs_r.next()
                            cx.dma("pool", ofs, ofs[:, :], ofd, of_d[gch], stm(f"ofl{cn['ol'] % 2}"))
                            cn["ol"] += 1
                            ofd.r = []
                            o = o_r.next()
                            of_ = o[:, :, :].rearrange("p a b -> p (a b)")
                            cx.op("dve", lambda v: v.tensor_tensor(out=of_, in0=po[0:64, :], in1=ofs[:, :], op=ALU.add),
                                  reads=[po, ofs], writes=[o])
                            cx.op("dve", lambda v: v.tensor_tensor(out=osq[:, :, :], in0=o[:, :, :], in1=o[:, :, :],
                                                                   op=ALU.mult), reads=[o], writes=[osq])
                            cx.op("dve", lambda v: v.tensor_reduce(out=ss[:, :], in_=osq[:, :, :], axis=X, op=ALU.add),
                                  reads=[osq], writes=[ss])
                            rs2 = rs[:, :, :].rearrange("p a b -> p (a b)")
                            cx.op("act", lambda a: a.activation(out=rs2, in_=ss[:, :], func=AF.Ln, scale=1.0 / 128,
                                                                bias=epsb[0:64, 0:1]), reads=[ss, epsb], writes=[rs])
                            cx.op("act", lambda a: a.activation(out=rs2, in_=rs2, func=AF.Exp, scale=-0.5),
                                  reads=[rs], writes=[rs])
                            cx.op("dve", lambda v: v.tensor_tensor(
                                out=o[:, :, :], in0=o[:, :, :], in1=rs[:, :, :].to_broadcast([64, 4, 128]),
                                op=ALU.mult), reads=[o, rs], writes=[o])
                            cx.op("dve", lambda v: v.tensor_tensor(out=gs[:, :], in0=gv[:, 8:520], in1=sr[:, ci, :],
                                                                   op=ALU.mult), reads=[gv, sr], writes=[gs])
                            obk = ob_r.next()
                            cx.op("dve", lambda v: v.tensor_tensor(out=obk[:, :], in0=of_, in1=gs[:, :], op=ALU.mult),
                                  reads=[o, gs], writes=[obk])
                            ptr = psum.next()
                            ptb = ptr.ap[:, :].bitcast(BF16)
                            for cc in range(4):
                                cx.op("pe", lambda t_, cc=cc: t_.transpose(
                                    ptb[:, cc * 64:(cc + 1) * 64], obk[:, cc * 128:(cc + 1) * 128], ident_b[0:64, 0:64]),
                                    reads=[obk, ident_b], writes=[ptr], inc=(cc == 3))
                            obT = obT_r.next()
                            cx.op("act", lambda a: a.copy(out=obT[:, :, :].rearrange("p a b -> p (a b)"),
                                                          in_=ptb[:, 0:256]), reads=[ptr], writes=[obT])
                            tk0 = t * TILE + c0
                            cx.dma("sp", zd, z_d[:, 4:8, tk0:tk0 + 64], obT, obT[:, :, :], stm(f"gzst{cn['z'] % 2}"))
                            cn["z"] += 1
                            zd.w, zd.r = [], []

                Ufin = cx.sb("Ufin", [64, 4, 128], F32)
                Uoth = cx.sb("Uoth", [64, 4, 128], F32)
                xu_both = cx.sb("xu_both", [64, 2, 512], F32)
                xu_tmp = cx.sb("xu_tmp", [64, 512], F32)
                def hnorm(t):
                    return norm_mod(load_x(t), mod[l]["A1"], mod[l]["B1"], tile_j(t))

                order0 = list(range(NTILE))
                order1 = list(range(TS // TILE - 1, -1, -1)) + list(range(TS // TILE, NTILE))
                nxt_h = hnorm(order0[0])
                for i_, t in enumerate(order0):
                    h1 = nxt_h
                    nxt_h = hnorm(order0[i_ + 1]) if i_ + 1 < NTILE else hnorm(order1[0])
                    sweep_tile(t, 0, h1)
                exchange(Ufin, Ufin[:, :, :].rearrange("p a b -> p (a b)"), 64, 512, F32, xu_both, Uoth,
                         Uoth[:, :, :].rearrange("p a b -> p (a b)"), xu_tmp)
                drain()
                for i_, t in enumerate(order1):
                    h1 = nxt_h
                    if i_ + 1 < NTILE:
                        nxt_h = hnorm(order1[i_ + 1])
                    sweep_tile(t, 1, h1)

        def even_zero_gla(l):
            with Phase():
                zz = cx.sb("zz", [128, 4, TILE], BF16)
                cx.op("pool", lambda g_: g_.memset(zz[:, :, :], 0.0), writes=[zz])
                zd = Tk(None, "zd4")
                for t in range(NTILE):
                    cx.dma("sp", zd, z_d[:, 4:8, t * TILE:(t + 1) * TILE], zz, zz[:, :, :], stm("zzst"))
                    zd.w, zd.r = [], []

        def ffn_phase(l, has_mix):
            with Phase():
                load_x, store_x, stats, norm_mod = mk_norm()
                act_t = cx.sb("act_t", [128, FC, TILE], BF16)
                sg_ring = Ring([cx.sb(f"sg{i}", [128, TILE], BF16) for i in range(2)])
                win_ring = Ring([cx.sb(f"win{i}", [128, KC, 512], BF16) for i in range(3)])
                wo_ring = Ring([cx.sb(f"wo{i}", [128, FC, 128], BF16) for i in range(3)])
                cnt = {"win": 0, "wo": 0, "z": 0}
                md = mod[l]
                if has_mix:
                    wmo = cx.sb("wmo", [128, KC, D], BF16)
                    wsrc_t = w_out_odd if l % 2 == 1 else w_out_even
                    cx.dma("pool", wmo, wmo[:, :, :], wsrc_t, wsrc_t[l // 2].rearrange("(kc p) n -> p kc n", p=128),
                           stm("wmo"))
                    z_ring = Ring([cx.sb(f"zr{i}", [128, KC, TILE], BF16) for i in range(2)])
                    zd = Tk(None, "zd2")
                si, so = s_in[l % 2], s_out[l % 2]

                def stage_a(t):
                    j = tile_j(t)
                    xt = load_x(t)
                    if has_mix:
                        zr = z_ring.next()
                        cx.dma("sp", zr, zr[:, :, :], zd, z_d[:, :, t * TILE:(t + 1) * TILE], stm(f"zl{cnt['z'] % 2}"))
                        cnt["z"] += 1
                        for oc in range(KC):
                            po = psum.next()
                            for kc in range(KC):
                                cx.op("pe", lambda t_, kc=kc, oc=oc, po=po: t_.matmul(
                                    po[:, :], lhsT=wmo[:, kc, oc * 128:(oc + 1) * 128], rhs=zr[:, kc, :],
                                    start=(kc == 0), stop=(kc == KC - 1)), reads=[wmo, zr], writes=[po],
                                    inc=(kc == KC - 1))
                            cx.op("dve", lambda v, oc=oc, po=po: v.scalar_tensor_tensor(
                                out=xt[:, oc, :], in0=po[:, :], scalar=md["G1"][:, oc, j:j + 1], in1=xt[:, oc, :],
                                op0=ALU.mult, op1=ALU.add), reads=[po, md["G1"], xt], writes=[xt])
                    h2 = norm_mod(xt, md["A2"], md["B2"], j)
                    return xt, h2

                nxt = stage_a(0)
                for t in range(NTILE):
                    j = tile_j(t)
                    xt, h2 = nxt
                    for q in range(11):
                        wt = win_ring.next()
                        cx.dma("pool", wt, wt[:, :, :], si[q], si[q][:, :, :], stm(f"win{cnt['win'] % 3}"))
                        cnt["win"] += 1
                        for jj in range(2):
                            fc = q * 2 + jj
                            pg = psum.next()
                            for kc in range(KC):
                                cx.op("pe", lambda t_, kc=kc, jj=jj, pg=pg, wt=wt: t_.matmul(
                                    pg[:, :], lhsT=wt[:, kc, jj * 128:(jj + 1) * 128], rhs=h2[:, kc, :],
                                    start=(kc == 0), stop=(kc == KC - 1)),
                                    reads=[wt, h2], writes=[pg], inc=(kc == KC - 1))
                            pu = psum.next()
                            for kc in range(KC):
                                cx.op("pe", lambda t_, kc=kc, jj=jj, pu=pu, wt=wt: t_.matmul(
                                    pu[:, :], lhsT=wt[:, kc, 256 + jj * 128:256 + (jj + 1) * 128], rhs=h2[:, kc, :],
                                    start=(kc == 0), stop=(kc == KC - 1)),
                                    reads=[wt, h2], writes=[pu], inc=(kc == KC - 1))
                            sg = sg_ring.next()
                            cx.op("act", lambda a, pg=pg, sg=sg: a.activation(out=sg[:, :], in_=pg[:, :], func=AF.Silu),
                                  reads=[pg], writes=[sg])
                            cx.op("dve", lambda v, fc=fc, pu=pu, sg=sg: v.tensor_tensor(
                                out=act_t[:, fc, :], in0=pu[:, :], in1=sg[:, :], op=ALU.mult),
                                reads=[pu, sg], writes=[act_t])
                    if t + 1 < NTILE:
                        nxt = stage_a(t + 1)
                    for oc in range(KC):
                        wo = wo_ring.next()
                        cx.dma("pool", wo, wo[:, :, :], so[oc], so[oc][:, :, :], stm(f"wo{cnt['wo'] % 3}"))
                        cnt["wo"] += 1
                        po = psum.next()
                        for fc in range(FC):
                            cx.op("pe", lambda t_, fc=fc, po=po, wo=wo: t_.matmul(
                                po[:, :], lhsT=wo[:, fc, :], rhs=act_t[:, fc, :], start=(fc == 0), stop=(fc == FC - 1)),
                                reads=[wo, act_t], writes=[po], inc=(fc == FC - 1))
                        cx.op("dve", lambda v, oc=oc, po=po: v.scalar_tensor_tensor(
                            out=xt[:, oc, :], in0=po[:, :], scalar=md["G2"][:, oc, j:j + 1], in1=xt[:, oc, :],
                            op0=ALU.mult, op1=ALU.add), reads=[po, md["G2"], xt], writes=[xt])
                    store_x(t, xt)

        def final_phase():
            with Phase():
                load_x, store_x, stats, norm_mod = mk_norm()
                yfm_r = Ring([cx.sb(f"yfm{i}", [128, KC, TILE], F32) for i in range(2)])
                for t in range(NTILE):
                    xt = load_x(t)
                    rstd = stats(xt)
                    yfm = yfm_r.next()
                    for c in range(KC):
                        cx.op("dve", lambda v, c=c: v.scalar_tensor_tensor(
                            out=yfm[:, c, :], in0=xt[:, c, :], scalar=nfin_sb[:, c:c + 1], in1=rstd[:, :],
                            op0=ALU.mult, op1=ALU.mult), reads=[xt, nfin_sb, rstd], writes=[yfm])
                    cx.dma("sp", y_out, y_out[:, :, t * TILE:(t + 1) * TILE], yfm, yfm[:, :, :], stm(f"yst{t % 2}"))

        for l in range(DBG_DEPTH):
            prep_ffn_weights(l)
            do_mix = (DBG_MIX == "all") or (DBG_MIX == "odd" and l % 2 == 1) or (DBG_MIX == "even" and l % 2 == 0)
            if do_mix and l % 2 == 1:
                odd_phase1(l)
                odd_phase2(l)
            if do_mix and l % 2 == 0:
                even_attn(l)
                if DBG_GLA:
                    even_gla(l)
                else:
                    even_zero_gla(l)
            ffn_phase(l, do_mix)
        final_phase()
        for st in list(sp_st.values()):
            if st["cnt"] > 0:
                nc.sync.wait_ge(st["sem"], st["cnt"])
        cx.es = es
    return nc


_NC = None


def _fm(v):
    v = np.asarray(v, np.float32)
    lead = v.shape[:-1]
    r = v.reshape(*lead, KC, 128)
    return np.ascontiguousarray(np.moveaxis(r, -1, 0))


def kernel(x_prompt, x_sample, c, cache_k, cache_v, state_gla, state_lru, c_ctx,
           w_ada, b_ada, norm_mix, norm_ffn, w_in_even, attn_sink, w_alpha, b_alpha, gla_gain,
           w_out_even, w_in_odd, conv_w, conv_b, w_gate_a, b_gate_a, w_gate_x, b_gate_x,
           lru_lambda, w_out_odd, w_ffn_in, w_ffn_out, norm_final):
    global _NC
    if _NC is None:
        _NC = build_program()
    nc = _NC
    f = lambda a: np.ascontiguousarray(np.asarray(a, np.float32))
    x_prompt, x_sample = f(x_prompt), f(x_sample)
    w_ada, b_ada = f(w_ada), f(b_ada)
    nmix_fm = np.moveaxis(f(norm_mix).reshape(DEPTH, KC, 128), -1, 1)
    nmix2 = np.ascontiguousarray(np.repeat(nmix_fm[..., None], 2, axis=-1))
    nffn_fm = np.moveaxis(f(norm_ffn).reshape(DEPTH, KC, 128), -1, 1)
    nffn2 = np.ascontiguousarray(np.repeat(nffn_fm[..., None], 2, axis=-1))
    nfin = np.ascontiguousarray(f(norm_final).reshape(KC, 128).T)
    shared = {"nmix2": nmix2, "nffn2": nffn2, "nfin": nfin,
              "w_ffn_in": f(w_ffn_in), "w_ffn_out": f(w_ffn_out), "ident_f": np.eye(128, dtype=np.float32),
              "w_in_odd": f(w_in_odd), "w_out_odd": f(w_out_odd), "w_out_even": f(w_out_even)}
    perm = np.concatenate([np.arange(16, 32), np.arange(0, 16), np.arange(48, 64), np.arange(32, 48)])
    q_idx = np.arange(512)
    pq_idx = (np.arange(8)[:, None] * 64 + perm[None, :]).reshape(-1)
    k_idx = 512 + np.arange(128)
    pk_idx = 512 + np.concatenate([perm, 64 + perm])
    v_idx = 640 + np.arange(128)
    qb_idx = 768 + np.arange(256)
    kb_idx = 1024 + np.arange(256)
    vb_idx = 1280 + np.arange(512)
    rb_idx = 1792 + np.arange(512)
    w_in_even = f(w_in_even)
    w_ev_r = []
    for r in range(2):
        lf, lb = (0, 16) if r == 0 else (16, 0)
        lr_idx = 2304 + np.concatenate([lf + np.arange(16), lf + np.arange(16), lb + np.arange(16), lb + np.arange(16)])
        ev_idx = np.concatenate([q_idx, pq_idx, k_idx, pk_idx, v_idx, qb_idx, kb_idx, lr_idx, vb_idx, rb_idx])
        assert ev_idx.size == NCE
        w_ev_r.append(np.ascontiguousarray(w_in_even[:, :, ev_idx]))
    inv = (10000.0 ** (-np.arange(16, dtype=np.float32) / 16)).astype(np.float32)
    sgn = np.concatenate([-np.ones(16), np.ones(16), -np.ones(16), np.ones(16)]).astype(np.float32)
    rope_r = []
    for r in range(2):
        tok = np.arange(TS) if r == 0 else (2 * TS - 1 - np.arange(TS))
        ar = (tok // 64).astype(np.float32)[:, None] * inv
        ac = (tok % 64).astype(np.float32)[:, None] * inv
        ang = np.concatenate([ar, ar, ac, ac], axis=-1)
        cs = np.stack([np.cos(ang), np.sin(ang) * sgn], axis=0).astype(np.float32)
        rope_r.append(np.ascontiguousarray(cs.transpose(2, 0, 1)))
    ia = np.arange(128)
    mp = np.where(ia[:, None] >= ia[None, :], 0.0, NEG).astype(np.float32)
    mn = np.where(ia[:, None] <= ia[None, :], 0.0, NEG).astype(np.float32)
    mh = np.where(ia[:, None] + ia[None, :] >= 127, 0.0, NEG).astype(np.float32)
    shared["maskT"] = np.ascontiguousarray(np.stack([np.tile(mp, (1, 4)), np.tile(mn, (1, 4)), np.tile(mh, (1, 4))], 0))
    shared["sinkb"] = np.ascontiguousarray(np.broadcast_to(f(attn_sink)[:, None, :], (2, 128, 8)))
    w_alpha, b_alpha, gla_gain, state_gla = f(w_alpha), f(b_alpha), f(gla_gain), f(state_gla)
    ic = np.arange(64)
    gf = (ic[None, :] >= ic[:, None]).astype(np.float32)
    gb = (ic[None, :] <= ic[:, None]).astype(np.float32)
    shared["gmask"] = np.ascontiguousarray(np.stack([np.tile(gf, (1, 8)), np.tile(gb, (1, 8))], 0))
    cache_k, cache_v = f(cache_k), f(cache_v)
    conv_w, conv_b, b_gate_a, b_gate_x, lru_lambda, state_lru = (f(a) for a in (
        conv_w, conv_b, b_gate_a, b_gate_x, lru_lambda, state_lru))
    w_gate_a, w_gate_x = f(w_gate_a), f(w_gate_x)
    zero_tap = np.zeros(D, np.float32)
    rank_in = []
    for r in range(2):
        dl = [0, 1] if r == 0 else [1, 0]
        walp = np.zeros((2, 64, 256), np.float32)
        walp[:, 0:16] = w_alpha[:, dl[0]]
        walp[:, 32:48] = w_alpha[:, dl[1]]
        glav = np.zeros((2, 64, 520), np.float32)
        glav[:, :, 0:8] = b_alpha[:, dl].reshape(2, 2, 4, 64).transpose(0, 3, 1, 2).reshape(2, 64, 8)
        glav[:, :, 8:] = gla_gain[:, None, :]
        sel = np.zeros((128, 2), np.float32)
        sel[:, 1 - r] = 1.0
        rank_in.append({"w_ev": w_ev_r[r], "rope_d": rope_r[r], "walp": walp, "glav": glav, "selm": sel,
                        "w_gate_a": np.ascontiguousarray(w_gate_a[:, dl]),
                        "w_gate_x": np.ascontiguousarray(w_gate_x[:, dl])})
    in_maps = []
    for core in range(8):
        b, r = core // 2, core % 2
        dl = [0, 1] if r == 0 else [1, 0]
        if r == 0:
            xs_ = x_sample[b, 0:TS]
            xp_ = x_prompt[4 * b:4 * b + 2].reshape(NPS * PL, D)
        else:
            xs_ = x_sample[b, TS:2 * TS][::-1]
            xp_ = x_prompt[4 * b + 2:4 * b + 4][:, ::-1].reshape(NPS * PL, D)
        m = dict(shared)
        m.update(rank_in[r])
        cs_ = slice(r * 3072, (r + 1) * 3072)
        m["w_ada"] = np.ascontiguousarray(w_ada[:, :, cs_])
        m["b_ada_s"] = np.ascontiguousarray(b_ada[:, cs_].reshape(DEPTH * 24, 128).T[:, :, None])
        cv = np.stack([f(c)[b], f(c_ctx)], axis=-1)
        m["cvt"] = np.ascontiguousarray(np.moveaxis(cv.reshape(KC, 128, 2), 1, 0))
        lv = []
        for o in range(2):
            taps = [conv_w[o, 0], conv_w[o, 1], conv_w[o, 2], conv_w[o, 3], zero_tap]
            if r == 1:
                taps = taps[::-1]
            vecs = taps + [conv_b[o]]
            for d in dl:
                vecs += [b_gate_a[o, d], b_gate_x[o, d], lru_lambda[o, d], state_lru[b, o, d]]
            lv.append(np.stack(vecs, 0).reshape(14, KC, 128).transpose(2, 1, 0))
        m["lruv"] = np.ascontiguousarray(np.stack(lv, 0))
        ck = cache_k[b]
        m["ckT"] = np.ascontiguousarray(ck.transpose(0, 2, 3, 1))
        m["sgl"] = np.ascontiguousarray(state_gla[b][:, dl])
        cva = np.ones((2, 256, 2, 65), np.float32)
        cva[:, :, :, 0:64] = cache_v[b]
        m["cvaug"] = np.ascontiguousarray(cva.reshape(2, 2, 128, 130))
        xtok = np.concatenate([xs_, xp_], axis=0)
        m["xin"] = np.ascontiguousarray(xtok.reshape(NT, KC, 128).transpose(2, 1, 0))
        in_maps.append(m)
    res = run_bass_kernel_spmd(nc, in_maps, core_ids=list(range(8)))
    y_s = np.zeros((4, 2 * TS, D), np.float32)
    y_p = np.zeros((16, PL, D), np.float32)
    n_lru = np.zeros((16, 2, 2, D), np.float32)
    n_g = np.zeros((16, 2, 2, 4, 64, 128), np.float32)
    n_k = np.zeros((16, 2, PL, 2, 64), np.float32)
    n_v = np.zeros((16, 2, PL, 2, 64), np.float32)
    for core in range(8):
        b, r = core // 2, core % 2
        dl = [0, 1] if r == 0 else [1, 0]
        o_ = res.results[core]
        y = o_["y"].transpose(2, 1, 0).reshape(NT, D)
        p0 = 4 * b + 2 * r
        yp = y[TS:].reshape(NPS, PL, D)
        nk = o_["nk_o"].reshape(NPS, 2, PL, 2, 64)
        nv = o_["nv_o"].reshape(NPS, 2, PL, 2, 64)
        lo = o_["lru_o"].transpose(1, 2, 3, 4, 0).reshape(NPS, 2, 2, D)
        if r == 0:
            y_s[b, 0:TS] = y[:TS]
        else:
            y_s[b, TS:2 * TS] = y[:TS][::-1]
            yp, nk, nv = yp[:, ::-1], nk[:, :, ::-1], nv[:, :, ::-1]
        y_p[p0:p0 + 2] = yp
        n_k[p0:p0 + 2] = nk
        n_v[p0:p0 + 2] = nv
        n_g[p0:p0 + 2] = o_["gla_o"][:, :, dl]
        n_lru[p0:p0 + 2] = lo[:, :, dl]
    return (y_p, y_s, n_k, n_v, n_g, n_lru)
```
